# Optimizing a Trainium2 kernel written in Bass

```python
import jax, jax.numpy as jnp
from jax import lax
import numpy as np

D_MODEL = 1024
BATCH = 8
SEQ = 2048
DEPTH = 1

N_ATTN_HEADS = 8
HEAD_DIM = 64
D_ATTN = N_ATTN_HEADS * HEAD_DIM
D_POOL = D_MODEL - D_ATTN
POOL_WINDOWS = (2, 4, 8, 16)
N_POOL_GROUPS = len(POOL_WINDOWS)
POOL_GROUP_DIM = D_POOL // N_POOL_GROUPS
D_MIX = D_ATTN + D_POOL
D_IN_PROJ = 3 * D_ATTN + D_POOL
MOBA_BLOCK = 256
MOBA_TOPK = 3
Q_CHUNK = 128
ROPE_THETA = 10000.0
D_FF = ((8 * D_MODEL // 3 + 127) // 128) * 128
CONV_WIDTH = 3
RMS_EPS = 1e-6
NEG_INF = -1e30

kernel_name = "hybrid_moba_pool_convffn_layer"


def rmsnorm(x, g):
    xf = x.astype(jnp.float32)
    y = xf * lax.rsqrt(jnp.mean(xf * xf, axis=-1, keepdims=True) + RMS_EPS)
    return (y * g.astype(jnp.float32)).astype(x.dtype)


def rope(x):
    S, Dh = x.shape[1], x.shape[-1]
    inv_freq = ROPE_THETA ** (-jnp.arange(0, Dh, 2, dtype=jnp.float32) / Dh)
    ang = jnp.arange(S, dtype=jnp.float32)[:, None] * inv_freq[None, :]
    cos, sin = jnp.cos(ang)[:, None, :], jnp.sin(ang)[:, None, :]
    xf = x.astype(jnp.float32)
    x1, x2 = xf[..., : Dh // 2], xf[..., Dh // 2:]
    return jnp.concatenate([x1 * cos - x2 * sin, x2 * cos + x1 * sin], axis=-1).astype(x.dtype)


def moba_attention(q, k, v):
    B, H, S, Dh = q.shape
    nb = -(-S // MOBA_BLOCK)
    pad = nb * MOBA_BLOCK - S
    kb = jnp.pad(k, ((0, 0), (0, 0), (0, pad), (0, 0))).reshape(B, H, nb, MOBA_BLOCK, Dh)
    vb = jnp.pad(v, ((0, 0), (0, 0), (0, pad), (0, 0))).reshape(B, H, nb, MOBA_BLOCK, Dh)
    scale = HEAD_DIM ** -0.5
    n_qc = S // Q_CHUNK
    n_sel = min(MOBA_TOPK, nb - 1)

    def to_chunks(a):
        rest = a.shape[3:]
        a = a.reshape((B, H, n_qc, Q_CHUNK) + rest)
        a = jnp.moveaxis(a, 2, 1)
        return a.reshape((B * n_qc, H, Q_CHUNK) + rest)

    qc = to_chunks(q)
    step_idx = jnp.arange(B * n_qc)
    bidx, cidx = step_idx // n_qc, step_idx % n_qc
    h_idx = jnp.arange(H)[:, None, None]

    def own_block_scores(q_c, kb_b, ci):
        ob = (ci * Q_CHUNK) // MOBA_BLOCK
        qpos = ci * Q_CHUNK + jnp.arange(Q_CHUNK)
        kpos = ob * MOBA_BLOCK + jnp.arange(MOBA_BLOCK)
        k_own = kb_b[:, ob]
        s = jnp.einsum('hqd,hkd->hqk', q_c, k_own).astype(jnp.float32) * scale
        return jnp.where(kpos[None, None, :] <= qpos[None, :, None], s, NEG_INF), ob

    if n_sel > 0:
        k_mean = jnp.mean(kb.astype(jnp.float32), axis=3)
        gate = jnp.einsum('bhsd,bhnd->bhsn', q.astype(jnp.float32), k_mean)
        q_blk = jnp.arange(S) // MOBA_BLOCK
        past = jnp.arange(nb)[None, :] < q_blk[:, None]
        gate = jnp.where(past, gate, -jnp.inf)
        _, sel = lax.top_k(gate, n_sel)
        sel_ok = sel < q_blk[:, None]
        selc, okc = to_chunks(sel), to_chunks(sel_ok)

        def step(args):
            bi, ci, q_c, sel_c, ok_c = args
            kb_b, vb_b = kb[bi], vb[bi]
            s_own, ob = own_block_scores(q_c, kb_b, ci)
            k_sel = kb_b[h_idx, sel_c]
            v_sel = vb_b[h_idx, sel_c]
            s_sel = jnp.einsum('hqd,hqnkd->hqnk', q_c, k_sel).astype(jnp.float32) * scale
            s_sel = jnp.where(ok_c[..., None], s_sel, NEG_INF).reshape(H, Q_CHUNK, n_sel * MOBA_BLOCK)
            p = jax.nn.softmax(jnp.concatenate([s_sel, s_own], axis=-1), axis=-1).astype(v.dtype)
            p_sel = p[..., : n_sel * MOBA_BLOCK].reshape(H, Q_CHUNK, n_sel, MOBA_BLOCK)
            p_own = p[..., n_sel * MOBA_BLOCK:]
            return (jnp.einsum('hqnk,hqnkd->hqd', p_sel, v_sel)
                    + jnp.einsum('hqk,hkd->hqd', p_own, vb_b[:, ob]))

        out = lax.map(step, (bidx, cidx, qc, selc, okc))
    else:
        def step(args):
            bi, ci, q_c = args
            kb_b, vb_b = kb[bi], vb[bi]
            s_own, ob = own_block_scores(q_c, kb_b, ci)
            p = jax.nn.softmax(s_own, axis=-1).astype(v.dtype)
            return jnp.einsum('hqk,hkd->hqd', p, vb_b[:, ob])

        out = lax.map(step, (bidx, cidx, qc))
    out = out.reshape(B, n_qc, H, Q_CHUNK, Dh).transpose(0, 1, 3, 2, 4)
    return out.reshape(B, S, H * Dh)


def pool_mixer(u, w_pool, b_pool, pool_scale):
    B, S, _ = u.shape
    ug = u.reshape(B, S, N_POOL_GROUPS, POOL_GROUP_DIM)
    c = jnp.pad(jnp.cumsum(ug.astype(jnp.float32), axis=1), ((0, 0), (1, 0), (0, 0), (0, 0)))
    t = jnp.arange(S)
    means = []
    for g, w in enumerate(POOL_WINDOWS):
        lo = jnp.maximum(t + 1 - w, 0)
        cnt = (t + 1 - lo).astype(jnp.float32)
        means.append((c[:, t + 1, g] - c[:, lo, g]) / cnt[None, :, None])
    diff = (jnp.stack(means, axis=2) - ug.astype(jnp.float32)).astype(u.dtype)
    y = jnp.einsum('bsgi,gio->bsgo', diff, w_pool).reshape(B, S, D_POOL) + b_pool
    return y * pool_scale


def conv_ffn(h, w_up, conv_w, conv_b, w_down):
    a = h @ w_up
    a = lax.conv_general_dilated(a, conv_w, window_strides=(1,),
                                 padding=((CONV_WIDTH - 1, 0),),
                                 dimension_numbers=('NWC', 'WIO', 'NWC'),
                                 feature_group_count=2 * D_FF) + conv_b
    gate, val = a[..., :D_FF], a[..., D_FF:]
    return (jax.nn.silu(gate) * val) @ w_down


def setup_inputs(seed: int = 0) -> dict:
    key = jax.random.key(seed)
    ks = jax.random.split(key, 16)
    f32 = jnp.float32

    def nrm(k, shape, scale):
        return jax.random.normal(k, shape, f32) * scale

    return {
        "x": nrm(ks[0], (BATCH, SEQ, D_MODEL), 1.0),
        "norm_mix_pre": 1.0 + nrm(ks[1], (DEPTH, D_MODEL), 0.05),
        "w_in": nrm(ks[2], (DEPTH, D_MODEL, D_IN_PROJ), D_MODEL ** -0.5),
        "w_pool": nrm(ks[3], (DEPTH, N_POOL_GROUPS, POOL_GROUP_DIM, POOL_GROUP_DIM), POOL_GROUP_DIM ** -0.5),
        "b_pool": nrm(ks[4], (DEPTH, D_POOL), 0.01),
        "pool_scale": 1.0 + nrm(ks[5], (DEPTH, D_POOL), 0.1),
        "w_out": nrm(ks[6], (DEPTH, D_MIX, D_MODEL), D_MIX ** -0.5),
        "norm_mix_post": 1.0 + nrm(ks[7], (DEPTH, D_MODEL), 0.05),
        "norm_ffn_pre": 1.0 + nrm(ks[8], (DEPTH, D_MODEL), 0.05),
        "w_up": nrm(ks[9], (DEPTH, D_MODEL, 2 * D_FF), D_MODEL ** -0.5),
        "conv_w": nrm(ks[10], (DEPTH, CONV_WIDTH, 1, 2 * D_FF), CONV_WIDTH ** -0.5),
        "conv_b": nrm(ks[11], (DEPTH, 2 * D_FF), 0.01),
        "w_down": nrm(ks[12], (DEPTH, D_FF, D_MODEL), D_FF ** -0.5),
        "norm_ffn_post": 1.0 + nrm(ks[13], (DEPTH, D_MODEL), 0.05),
    }


def reference(x, norm_mix_pre, w_in, w_pool, b_pool, pool_scale, w_out, norm_mix_post,
              norm_ffn_pre, w_up, conv_w, conv_b, w_down, norm_ffn_post):
    B, S, _ = x.shape
    for l in range(DEPTH):
        h = rmsnorm(x, norm_mix_pre[l])
        proj = h @ w_in[l]
        q = proj[..., :D_ATTN].reshape(B, S, N_ATTN_HEADS, HEAD_DIM)
        k = proj[..., D_ATTN:2 * D_ATTN].reshape(B, S, N_ATTN_HEADS, HEAD_DIM)
        v = proj[..., 2 * D_ATTN:3 * D_ATTN].reshape(B, S, N_ATTN_HEADS, HEAD_DIM)
        u = proj[..., 3 * D_ATTN:]
        q = rope(q).transpose(0, 2, 1, 3)
        k = rope(k).transpose(0, 2, 1, 3)
        v = v.transpose(0, 2, 1, 3)
        y_attn = moba_attention(q, k, v)
        y_pool = pool_mixer(u, w_pool[l], b_pool[l], pool_scale[l])
        mix = jnp.concatenate([y_attn, y_pool], axis=-1) @ w_out[l]
        x = x + rmsnorm(mix, norm_mix_post[l])
        h = rmsnorm(x, norm_ffn_pre[l])
        f = conv_ffn(h, w_up[l], conv_w[l], conv_b[l], w_down[l])
        x = x + rmsnorm(f, norm_ffn_post[l])
    return x
```

```python
import numpy as np
from contextlib import ExitStack
import concourse.bass as bass
import concourse.mybir as mybir
from concourse.bass_utils import run_bass_kernel_spmd
from concourse.alu_op_type import AluOpType as ALU

F32 = mybir.dt.float32
BF16 = mybir.dt.bfloat16
I32 = mybir.dt.int32
AF = mybir.ActivationFunctionType
AX = mybir.AxisListType

S = 2048
D = 1024
NT = 16
NG = 4
H = 8
DH = 64
DFF = 2816
NPAIR = 22
NEG = -30000.0
EPS = 1e-6
SCALE = DH ** -0.5
COMPUTE = ("pe", "act", "dve", "pool")
TUNE = dict(r0="dve", r2="dve", r3="dve", ktevac="act", add1="dve", add3="dve", corr="dve", gmult="pool")


class Buf:
    def __init__(self):
        self.w = []
        self.r = []
        self.pw = []

    def rd(self):
        return list(self.w)

    def wr_new(self):
        deps = self.r + self.w
        self.pw = deps
        self.w = []
        self.r = []
        return list(deps)

    def wr_more(self):
        return list(self.pw)


class Sched:
    COST = dict(pe=lambda n: 0.008 + n / 2370.0, act=lambda n: 0.22 + n / 1400.0, dve=lambda n: 0.29 + n / 960.0,
                pool=lambda n: 0.36 + n / 560.0, sp=lambda n: 0.05)

    def __init__(self, nc, st):
        self.nc = nc
        self.st = st
        self.streams = {e: [] for e in COMPUTE + ("sp",)}
        self.sems = {e: st.enter_context(nc.semaphore("s_" + e)) for e in COMPUTE}
        self.dsem = {}
        self.seg = 0
        self.order = 0

    def new_segment(self):
        self.seg += 1

    def _collect(self, eng, rd, wr, wp, deps):
        ds = list(deps)
        for b in rd:
            ds += b.rd()
        for b in wr:
            ds += b.wr_new()
        for b in wp:
            ds += b.wr_more()
            if eng == "pe":
                ds += [h for h in b.w if h[0] == "pe"]
        return [d for d in ds if d is not None]

    def op(self, eng, fn, rd=(), wr=(), wp=(), deps=(), n=None):
        raw = self._collect(eng, rd, wr, wp, deps)
        idx = len(self.streams[eng])
        if n is None:
            n = 512
        self.streams[eng].append(dict(fn=fn, raw=raw, kind="op", sig=False, seq=0, eng=eng, idx=idx, seg=self.seg,
                                      cost=self.COST[eng](n), prog=self.order))
        self.order += 1
        h = (eng, idx)
        for b in rd:
            b.r.append(h)
        for b in list(wr) + list(wp):
            b.w.append(h)
        return h

    def dma(self, queue, fn, key, rd=(), wr=(), wp=(), deps=(), nbytes=524288):
        raw = self._collect(queue, rd, wr, wp, deps)
        if key not in self.dsem:
            self.dsem[key] = [self.st.enter_context(self.nc.semaphore("d_" + key)), 0]
        idx = len(self.streams[queue])
        ent = dict(fn=fn, raw=raw, kind="dma", key=key, sig=False, seq=0, eng=queue, idx=idx, seg=self.seg,
                   cost=(0.65 if queue == "pool" else 0.08), lat=2.0 + nbytes / 120000.0, prog=self.order)
        self.order += 1
        self.streams[queue].append(ent)
        h = ["dma", key, None, ent]
        for b in rd:
            b.r.append(h)
        for b in list(wr) + list(wp):
            b.w.append(h)
        return h

    @staticmethod
    def _ent_of(sched, d):
        return d[3] if d[0] == "dma" else sched.streams[d[0]][d[1]]

    def _schedule(self):
        import heapq
        nseg = self.seg + 1
        self.est = []
        new_streams = {e: [] for e in self.streams}
        t_base = 0.0
        for sg in range(nseg):
            ops = [o for e in self.streams for o in self.streams[e] if o["seg"] == sg]
            if not ops:
                continue
            ids = {id(o) for o in ops}
            users = {id(o): [] for o in ops}
            remaining = {}
            ready = {}
            for o in ops:
                cnt = 0
                for d in o["raw"]:
                    p = self._ent_of(self, d)
                    if id(p) in ids:
                        users[id(p)].append(o)
                        cnt += 1
                remaining[id(o)] = cnt
                ready[id(o)] = t_base
            heaps = {e: [] for e in self.streams}
            for o in ops:
                if remaining[id(o)] == 0:
                    heapq.heappush(heaps[o["eng"]], (ready[id(o)], o["prog"], id(o), o))
            free_at = {e: t_base for e in self.streams}
            done = 0
            tmax = t_base
            while done < len(ops):
                best = None
                for e, hp in heaps.items():
                    if hp:
                        st_ = max(free_at[e], hp[0][0])
                        if best is None or st_ < best[0] or (st_ == best[0] and hp[0][1] < best[2]):
                            best = (st_, e, hp[0][1])
                assert best is not None, "scheduler deadlock (dependency cycle)"
                st_, e, _ = best
                hp = heaps[e]
                cands = []
                while hp and hp[0][0] <= st_ + 1e-9:
                    cands.append(heapq.heappop(hp))
                cands.sort(key=lambda c: c[1])
                chosen = cands[0]
                for c in cands[1:]:
                    heapq.heappush(hp, c)
                o = chosen[3]
                fin_eng = st_ + o["cost"]
                free_at[e] = fin_eng
                fin = fin_eng + (o["lat"] if o["kind"] == "dma" else 0.0)
                o["t_fin"] = fin
                tmax = max(tmax, fin)
                new_streams[e].append(o)
                done += 1
                for u in users[id(o)]:
                    lat = 0.12 if (u["eng"] == e and o["kind"] != "dma") else 0.28
                    ready[id(u)] = max(ready[id(u)], fin + lat)
                    remaining[id(u)] -= 1
                    if remaining[id(u)] == 0:
                        heapq.heappush(heaps[u["eng"]], (ready[id(u)], u["prog"], id(u), u))
            self.est.append(round(tmax, 1))
            t_base = tmax
        for e in self.streams:
            assert len(new_streams[e]) == len(self.streams[e])
            self.streams[e] = new_streams[e]
            for pos, o in enumerate(new_streams[e]):
                o["pos"] = pos
        last_of_seg = {}
        for e in COMPUTE:
            for o in self.streams[e]:
                last_of_seg[(e, o["seg"])] = o["pos"]
        for e, stream in self.streams.items():
            prev_seg = None
            for o in stream:
                best = {}
                deps = []
                for d in o["raw"]:
                    if d[0] == "dma":
                        deps.append(d)
                        continue
                    p = self.streams_lookup(d)
                    if d[0] == "pe" and e == "pe":
                        continue
                    if d[0] not in best or best[d[0]] < p["pos"]:
                        best[d[0]] = p["pos"]
                if o["seg"] != prev_seg and o["seg"] > 0:
                    for e2 in COMPUTE:
                        sgs = [s_ for (ee, s_) in last_of_seg if ee == e2 and s_ < o["seg"]]
                        if sgs:
                            pos2 = last_of_seg[(e2, max(sgs))]
                            if not (e2 == "pe" and e == "pe"):
                                if e2 not in best or best[e2] < pos2:
                                    best[e2] = pos2
                prev_seg = o["seg"]
                for k, v in best.items():
                    deps.append((k, v))
                o["deps"] = deps

    def streams_lookup(self, d):
        return self._byid[(d[0], d[1])]

    def emit(self, block):
        self._byid = {}
        for e, stream in self.streams.items():
            for o in stream:
                self._byid[(e, o["idx"])] = o
        self._schedule()
        for eng, stream in self.streams.items():
            for o in stream:
                for d in o["deps"]:
                    if d[0] != "dma":
                        self.streams[d[0]][d[1]]["sig"] = True
        for eng, stream in self.streams.items():
            n = 0
            for o in stream:
                if o["kind"] == "op" and o["sig"]:
                    n += 1
                    o["seq"] = n
        keyq = {}
        for eng, stream in self.streams.items():
            for o in stream:
                if o["kind"] == "dma":
                    assert keyq.setdefault(o["key"], eng) == eng, "dma key used from two queues"
                    self.dsem[o["key"]][1] += 16 * o["ndma"]
                    o["target"] = self.dsem[o["key"]][1]

        def make(eng):
            def body(e):
                waited = {}
                for o in self.streams[eng]:
                    need = {}
                    for d in o["deps"]:
                        if d[0] == "dma":
                            ent = d[3]
                            sem, val = self.dsem[d[1]][0], ent["target"]
                            k = "d_" + d[1]
                        else:
                            src = self.streams[d[0]][d[1]]
                            sem, val = self.sems[d[0]], src["seq"]
                            k = d[0]
                        if k not in need or need[k][1] < val:
                            need[k] = (sem, val)
                    for k, (sem, val) in need.items():
                        if waited.get(k, 0) < val:
                            e.wait_ge(sem, val)
                            waited[k] = val
                    ins = o["fn"](e)
                    if o["kind"] == "dma":
                        sem = self.dsem[o["key"]][0]
                        for i in ins:
                            i.then_inc(sem, 16)
                    elif o["sig"]:
                        ins.then_inc(self.sems[eng], 1)
            return body

        block.sync(make("sp"))
        block.gpsimd(make("pool"))
        block.scalar(make("act"))
        block.vector(make("dve"))
        block.tensor(make("pe"))


def build(dbg=False, p1_only=False):
    nc = bass.Bass("TRN2", target_bir_lowering=False)

    def din(name, shape):
        return nc.dram_tensor(name, list(shape), F32, kind="ExternalInput").ap()

    x_d = din("x", [S, D])
    w_in_d = din("w_in", [D, 2 * D])
    w_out_d = din("w_out", [D, D])
    w_up_d = din("w_up_r", [NPAIR, 128, 8, 256])
    w_down_d = din("w_down", [DFF, D])
    w_pool_d = din("w_pool", [4, 128, 128])
    gains_d = din("gains", [4, 128, D])
    colp_d = din("colp", [128, 8])
    cwb_d = din("cwb", [128, 44, 4])
    rope_d = din("rope", [128, NT, 64])
    mk_d = din("mk", [128, 3, 8, 8])
    blk_d = din("blkind", [8, S])
    poolA_d = din("poolA", [128, 3, 4, 128])
    idtri_d = din("idtri", [128, 2, 128])
    out_d = nc.dram_tensor("out", [S, D], F32, kind="ExternalOutput").ap()

    with ExitStack() as st:
        def sb(name, shape, dt):
            return st.enter_context(nc.sbuf_tensor(name, list(shape), dt))

        yT = sb("yT", [128, 8, S], BF16)
        RA = sb("RA", [128, 8192], F32)
        RB = sb("RB", [128, 8192], F32)
        RC = sb("RC", [128, 8192], F32)
        RD = sb("RD", [128, 2, D], F32)
        RE = sb("RE", [128, 6144], F32)
        RS = sb("RS", [128, 11264], F32)
        CST = sb("CST", [128, 2, 128], BF16)
        SM = sb("SM", [128, 256], F32)

        w_in_sb = RA[:, :].bitcast(BF16).rearrange("p (c n) -> p c n", c=8)
        KT = RB[:, :].bitcast(BF16).rearrange("p (h n) -> p h n", h=8)
        VA = RC[:, :].bitcast(BF16).rearrange("p (t h n) -> p t h n", t=NT, h=8)
        hT = RE[:, 0:2048].bitcast(BF16).rearrange("p (c n) -> p c n", c=8)
        QTs = [RE[:, 2048 * (1 + i):2048 * (2 + i)].bitcast(BF16).rearrange("p (h n) -> p h n", h=8)
               for i in range(2)]
        w_out_sb = RA[:, 0:4096].bitcast(BF16).rearrange("p (c n) -> p c n", c=8)
        wdA = RA[:, 4096:8192].bitcast(BF16).rearrange("p (c n) -> p c n", c=8)
        wdB = RB[:, 0:7168].bitcast(BF16).rearrange("p (c n) -> p c n", c=14)
        gbuf = RC[:, 0:5632].bitcast(BF16).rearrange("p (c n) -> p c n", c=NPAIR)
        NWU = 3
        wu = [RC[:, 5632 + i * 1024: 5632 + (i + 1) * 1024].bitcast(BF16).rearrange("p (c n) -> p c n", c=8)
              for i in range(2)] + \
             [RE[:, 4096:5120].bitcast(BF16).rearrange("p (c n) -> p c n", c=8)]
        hT2 = RE[:, 0:2048].bitcast(BF16).rearrange("p (c n) -> p c n", c=8)
        xn2 = [RE[:, 2048 + i * 512: 2048 + (i + 1) * 512].bitcast(BF16) for i in range(2)]
        n1t = RE[:, 3072:4096]

        o = [0]

        def carve(nwords):
            a = o[0]
            o[0] += nwords
            assert o[0] <= 11264, o[0]
            return RS[:, a:a + nwords]

        g_pre = carve(1024)
        ropet = carve(NT * 64).rearrange("p (t n) -> p t n", t=NT)
        mk = carve(192).rearrange("p (a q j) -> p a q j", a=3, q=8)
        poolA = carve(768).bitcast(BF16).rearrange("p (a g n) -> p a g n", a=3, g=4)
        wpool = carve(256).bitcast(BF16).rearrange("p (g n) -> p g n", g=4)
        colp = carve(8)
        xn = [carve(512).bitcast(BF16) for _ in range(2)]
        rt1s = [carve(512).rearrange("p (h n) -> p h n", h=8) for _ in range(2)]
        rt2s = [carve(512).rearrange("p (h n) -> p h n", h=8) for _ in range(2)]
        qr = carve(256).bitcast(BF16).rearrange("p (h n) -> p h n", h=8)
        kr = carve(256).bitcast(BF16).rearrange("p (h n) -> p h n", h=8)
        uring = carve(5 * 256).bitcast(BF16).rearrange("p (t n) -> p t n", t=5)
        diffs = carve(256).bitcast(BF16)
        kmT = carve(32).bitcast(BF16).rearrange("p (h j) -> p h j", h=8)
        gm = carve(64).rearrange("p (h j) -> p h j", h=8)
        kmTf = carve(64).rearrange("p (h j) -> p h j", h=8)
        top8 = carve(64).rearrange("p (h j) -> p h j", h=8)
        selA = carve(64).rearrange("p (h j) -> p h j", h=8)
        MB = carve(32).bitcast(BF16).rearrange("p (h j) -> p h j", h=8)
        junk = carve(512).bitcast(BF16)
        off_live = o[0]
        PT = [carve(256).bitcast(BF16) for _ in range(3)]
        rinv = [carve(512) for _ in range(2)]
        o[0] = 0
        x1 = carve(5 * 1024).rearrange("p (t n) -> p t n", t=5)
        g_post = carve(1024)
        g_fpre = carve(1024)
        g_fpost = carve(1024)
        cwb = carve(176).rearrange("p (c k) -> p c k", c=44)
        halo = carve(176).rearrange("p (a c k) -> p a c k", a=2, c=44)
        corr = carve(88).rearrange("p (c k) -> p c k", c=44)
        ctmp = carve(44)
        assert o[0] <= off_live, (o[0], off_live)
        tA = [carve(512) for _ in range(4)]
        junk2 = RE[:, 5632:6144].bitcast(BF16)

        ident = CST[:, 0, :]
        tri = CST[:, 1, :]
        ssq = SM[:, 0:16]
        rstd = SM[:, 16:32]
        nt_a = SM[:, 32:48]
        nt_b = SM[:, 48:64]
        nt_c = SM[:, 64:80]
        sq1 = SM[:, 80:96]
        sq2 = SM[:, 96:112]
        sq3 = SM[:, 112:128]
        rs1 = SM[:, 128:144]
        rs2 = SM[:, 144:160]
        rs3 = SM[:, 160:176]

        pbig_t = st.enter_context(nc.psum_tensor("pbig", [128, 1024], F32))
        pT_t = st.enter_context(nc.psum_tensor("pT", [128, 512], F32))
        up_t = st.enter_context(nc.psum_tensor("pup", [128, 2048], F32))
        p7_t = st.enter_context(nc.psum_tensor("p7", [128, 512], F32))
        banks = [pbig_t[:, 0:512], pbig_t[:, 512:1024], pT_t[:, :]] + \
                [up_t[:, k * 512:(k + 1) * 512] for k in range(4)] + [p7_t[:, :]]
        acc = [pbig_t[:, :], up_t[:, 0:1024]]
        block = st.enter_context(nc.Block())
        sc = Sched(nc, st)

        def dma(queue, key, pairs, **kw):
            def fn(e, pairs=pairs):
                return [e.dma_start(out=o_, in_=i_) for (o_, i_) in pairs]
            h = sc.dma(queue, fn, key, **kw)
            h[3]["ndma"] = len(pairs)
            return h

        B = {}

        def buf(name):
            if name not in B:
                B[name] = Buf()
            return B[name]

        bank = [buf("bank%d" % i) for i in range(8)]
        accb = [[bank[0], bank[1]], [bank[3], bank[4]]]

        dma("sp", "c1", [(g_pre, gains_d[0]), (ropet, rope_d), (mk, mk_d), (colp, colp_d)],
            wr=[buf("g_pre"), buf("ropet"), buf("mk"), buf("colp")])
        dma("pool", "c2", [(CST[:, :, :], idtri_d), (poolA, poolA_d), (wpool, w_pool_d.rearrange("g i o -> i g o"))]
            + [(KT[64:72, h, :], blk_d) for h in range(H)],
            wr=[buf("cst"), buf("poolA"), buf("wpool"), buf("KTind")])
        w_in_v = w_in_d.rearrange("(c p) n -> p c n", p=128)
        dma("pool", "win", [(w_in_sb[:, c, :], w_in_v[:, c, :]) for c in range(8)], wr=[buf("w_in")])

        xslot = [buf("xs%d" % i) for i in range(2)]

        def load_x(tt, deps=()):
            s = tt % 2
            return dma("sp", "x%d" % s, [(RD[:, s, :], x_d[tt * 128:(tt + 1) * 128, :])], wr=[xslot[s]], deps=deps)

        sc.op("pool", lambda e: e.memset(VA[:, :, :, 64:128], 1.0), wr=[buf("VAones")], n=512)
        sc.op("pool", lambda e: e.memset(kmT, 0.0), wr=[buf("kmT")], n=64)

        def rsqrt_col(ssq_col, ssq_buf, y, name, newton_eng="dve", extra_rd=()):
            v, hv, t = nt_a[:, 0:1], nt_b[:, 0:1], nt_c[:, 0:1]
            col = int(name.split("_")[-1])
            v, hv, t = nt_a[:, col:col + 1], nt_b[:, col:col + 1], nt_c[:, col:col + 1]
            bv, bh, bt, by = buf(name + "_v"), buf(name + "_h"), buf(name + "_t"), buf(name)
            sc.op("dve", lambda e: e.tensor_scalar(out=v, in0=ssq_col, scalar1=1.0 / D, scalar2=EPS,
                                                   op0=ALU.mult, op1=ALU.add),
                  rd=[ssq_buf] + list(extra_rd), wr=[bv], n=1)
            sc.op("dve", lambda e: e.tensor_scalar(out=y.bitcast(I32), in0=v.bitcast(I32), scalar1=-0.5,
                                                   scalar2=1597463007.0, op0=ALU.mult, op1=ALU.add),
                  rd=[bv], wr=[by], n=1)
            ne = newton_eng
            sc.op(ne, lambda e: e.tensor_scalar(out=hv, in0=v, scalar1=-0.5, scalar2=None, op0=ALU.mult),
                  rd=[bv], wr=[bh], n=1)
            for _ in range(3):
                if ne == "dve":
                    sc.op("dve", lambda e: e.scalar_tensor_tensor(out=t, in0=hv, scalar=y, in1=y,
                                                                 op0=ALU.mult, op1=ALU.mult),
                          rd=[bh, by], wr=[bt], n=1)
                else:
                    sc.op(ne, lambda e: e.tensor_tensor(out=t, in0=hv, in1=y, op=ALU.mult), rd=[bh, by], wr=[bt], n=1)
                    sc.op(ne, lambda e: e.tensor_tensor(out=t, in0=t, in1=y, op=ALU.mult), rd=[bt, by], wr=[bt], n=1)
                sc.op(ne, lambda e: e.tensor_scalar(out=y, in0=t, scalar1=1.5, scalar2=y, op0=ALU.add, op1=ALU.mult),
                      rd=[bt, by], wr=[by], n=1)
            return by

        load_x(0)
        load_x(1)
        bT = bank[2]
        bT2 = bank[3]
        psT = banks[2].bitcast(BF16)
        psT2 = banks[3].bitcast(BF16)
        hT_b = [buf("hT%d" % i) for i in range(4)]
        QT_b = [buf("QT0"), buf("QT1")]
        KT_b = [buf("KT%d" % g) for g in range(NG)]
        VA_b = [buf("VA%d" % t) for t in range(NT)]
        ur_b = [buf("ur%d" % i) for i in range(5)]
        yT_b = [buf("yT%d" % g) for g in range(NG)]

        def prep_gen(G):
            QT = QTs[G % 2]
            QTb = QT_b[G % 2]

            def S1(i):
                tt = 4 * G + i
                s_ = tt % 2
                sc.op("act", lambda e: e.activation(out=junk, in_=RD[:, s_, :], func=AF.Square,
                                                    accum_out=ssq[:, tt:tt + 1]),
                      rd=[xslot[s_]], wr=[buf("junk"), buf("ssq%d" % tt)], n=1024)

            def S2(i):
                tt = 4 * G + i
                s_ = tt % 2
                rb = rsqrt_col(ssq[:, tt:tt + 1], buf("ssq%d" % tt), rstd[:, tt:tt + 1], "r0_%d" % tt, TUNE["r0"])
                sc.op("dve", lambda e: e.scalar_tensor_tensor(out=xn[tt % 2], in0=RD[:, s_, :],
                                                             scalar=rstd[:, tt:tt + 1], in1=g_pre,
                                                             op0=ALU.mult, op1=ALU.mult),
                      rd=[xslot[s_], rb, buf("g_pre")], wr=[buf("xn%d" % (tt % 2))], n=1024)
                if tt + 2 < NT:
                    load_x(tt + 2)

            def S3(i):
                tt = 4 * G + i
                for c in range(8):
                    sc.op("pe", lambda e, c=c: e.transpose(out=psT[:, c * 128:(c + 1) * 128],
                                                          in_=xn[tt % 2][:, c * 128:(c + 1) * 128], identity=ident),
                          rd=[buf("xn%d" % (tt % 2)), buf("cst")], wr=[bT] if c == 0 else (), wp=[bT] if c else (), n=128)

            def S4(i):
                sc.op("act", lambda e: e.copy(out=hT[:, :, i * 128:(i + 1) * 128],
                                             in_=psT.rearrange("p (c n) -> p c n", c=8)),
                      rd=[bT], wr=[hT_b[i]], n=1024)

            for step in ([(S1, 0), (S1, 1)], [(S2, 0)], [(S3, 0), (S2, 1)], [(S4, 0), (S1, 2)],
                         [(S3, 1), (S2, 2)], [(S4, 1), (S1, 3)], [(S3, 2), (S2, 3)], [(S4, 2)], [(S3, 3)], [(S4, 3)]):
                for fn, i in step:
                    fn(i)
                yield

            def proj(i, j, pb):
                for c in range(8):
                    sc.op("pe", lambda e, c=c: e.matmul(
                        banks[pb], lhsT=hT[:, c, i * 128:(i + 1) * 128],
                        rhs=w_in_sb[:, c, j * 512:(j + 1) * 512], start=(c == 0), stop=(c == 7)),
                        rd=[hT_b[i], buf("w_in")], wr=[bank[pb]] if c == 0 else (), wp=[bank[pb]] if c else ())

            def rope_dve(i, j):
                tt = 4 * G + i
                pb = j
                rt1, rt2 = rt1s[j], rt2s[j]
                psv = banks[pb].rearrange("p (h n) -> p h n", h=8)
                cosb = ropet[:, tt, 0:32].unsqueeze(1).to_broadcast([128, 8, 32])
                sinb = ropet[:, tt, 32:64].unsqueeze(1).to_broadcast([128, 8, 32])
                b1_, b2_ = buf("rt1_%d" % j), buf("rt2_%d" % j)
                cos4 = ropet[:, tt, 0:32].unsqueeze(1).unsqueeze(1).to_broadcast([128, 8, 2, 32])
                sc.op("dve", lambda e: e.tensor_tensor(out=rt1.rearrange("p h (a n) -> p h a n", a=2),
                                                       in0=psv.rearrange("p h (a n) -> p h a n", a=2), in1=cos4, op=ALU.mult),
                      rd=[bank[pb], buf("ropet")], wr=[b1_])
                sc.op("dve", lambda e: e.tensor_tensor(out=rt2[:, :, 0:32], in0=psv[:, :, 32:64], in1=sinb, op=ALU.mult),
                      rd=[bank[pb], buf("ropet")], wr=[b2_])
                sc.op("dve", lambda e: e.tensor_tensor(out=rt2[:, :, 32:64], in0=psv[:, :, 0:32], in1=sinb, op=ALU.mult),
                      rd=[bank[pb], buf("ropet")], wp=[b2_])

            def rope_pool(i, j):
                rt1, rt2 = rt1s[j], rt2s[j]
                b1_, b2_ = buf("rt1_%d" % j), buf("rt2_%d" % j)
                dst = qr if j == 0 else kr
                dstb = buf("qr") if j == 0 else buf("kr")
                sc.op("pool", lambda e: e.tensor_tensor(out=dst[:, :, 0:32], in0=rt1[:, :, 0:32], in1=rt2[:, :, 0:32],
                                                        op=ALU.subtract), rd=[b1_, b2_], wr=[dstb])
                sc.op("pool", lambda e: e.tensor_tensor(out=dst[:, :, 32:64], in0=rt1[:, :, 32:64], in1=rt2[:, :, 32:64],
                                                        op=ALU.add), rd=[b1_, b2_], wp=[dstb])

            def b1(i):
                proj(i, 0, 0)
                proj(i, 1, 1)

            def b2(i):
                rope_dve(i, 0)
                rope_dve(i, 1)

            def b3(i):
                rope_pool(i, 0)
                rope_pool(i, 1)
                proj(i, 2, 0)
                proj(i, 3, 1)

            def b4(i):
                tt = 4 * G + i
                psv = banks[0].rearrange("p (h n) -> p h n", h=8)
                sc.op("act", lambda e: e.copy(out=VA[:, tt, :, 0:64], in_=psv),
                      rd=[bank[0]], wr=[VA_b[tt]], deps=buf("VAones").rd())
                sc.op("act", lambda e: e.copy(out=uring[:, tt % 5, :], in_=banks[1]),
                      rd=[bank[1]], wr=[ur_b[tt % 5]])
                for h in range(H):
                    sc.op("pe", lambda e, h=h: e.transpose(out=psT[0:64, h * 128:(h + 1) * 128], in_=qr[:, h, :],
                                                          identity=ident),
                          rd=[buf("qr"), buf("cst")], wr=[bT] if h == 0 else (), wp=[bT] if h else (), n=128)
                for h in range(H):
                    sc.op("pe", lambda e, h=h: e.transpose(out=psT2[0:64, h * 128:(h + 1) * 128], in_=kr[:, h, :],
                                                          identity=ident),
                          rd=[buf("kr"), buf("cst")], wr=[bT2] if h == 0 else (), wp=[bT2] if h else (), n=128)

            def b5(i):
                tt = 4 * G + i
                sc.op("act", lambda e: e.copy(
                    out=QT[0:64, :, i * 128:(i + 1) * 128], in_=psT[0:64, :].rearrange("p (h n) -> p h n", h=8)),
                    rd=[bT], wr=[QTb] if i == 0 else (), wp=[QTb] if i else (), n=1024)
                sc.op(TUNE["ktevac"], lambda e: (e.tensor_copy if TUNE["ktevac"] == "dve" else e.copy)(
                    out=KT[0:64, :, tt * 128:(tt + 1) * 128], in_=psT2[0:64, :].rearrange("p (h n) -> p h n", h=8)),
                    rd=[bT2], wr=[KT_b[G]] if i == 0 else (), wp=[KT_b[G]] if i else (), n=1024)
                if tt % 2 == 1:
                    jb = tt // 2
                    sc.op("dve", lambda e: e.tensor_reduce(
                        out=kmTf[0:64, :, jb:jb + 1], in_=KT[0:64, :, jb * 256:(jb + 1) * 256],
                        axis=AX.X, op=ALU.add), rd=[KT_b[G]], wr=[buf("kmTf")], n=64)
                    sc.op("dve", lambda e: e.tensor_copy(out=kmT[0:64, :, jb:jb + 1], in_=kmTf[0:64, :, jb:jb + 1]),
                          rd=[buf("kmTf")], wp=[buf("kmT")], n=64)

            def b6(i):
                for h in range(H):
                    sc.op("pe", lambda e, h=h: e.matmul(
                        banks[3][:, h * 8:(h + 1) * 8], lhsT=QT[0:64, h, i * 128:(i + 1) * 128], rhs=kmT[0:64, h, :],
                        start=True, stop=True),
                        rd=[QTb, buf("kmT")], wr=[bT2] if h == 0 else (), wp=[bT2] if h else (), n=8)

            def b7(i):
                qb = (4 * G + i) // 2
                pastneg = mk[:, 0, qb, :].unsqueeze(1).to_broadcast([128, 8, 8])
                pastP = mk[:, 1, qb, :].unsqueeze(1).to_broadcast([128, 8, 8])
                Cc = mk[:, 2, qb, :].unsqueeze(1).to_broadcast([128, 8, 8])
                sc.op("dve", lambda e: e.tensor_tensor(
                    out=gm, in0=banks[3][:, 0:64].rearrange("p (h j) -> p h j", h=8), in1=pastneg, op=ALU.add),
                    rd=[bT2, buf("mk")], wr=[buf("gm")], n=64)
                for h in range(H):
                    sc.op("dve", lambda e, h=h: e.max(out=top8[:, h, :], in_=gm[:, h, :]),
                          rd=[buf("gm")], wr=[buf("top8")] if h == 0 else (), wp=[buf("top8")] if h else (), n=8)
                sc.op("dve", lambda e: e.tensor_tensor(out=selA, in0=gm, in1=top8[:, :, 2:3].to_broadcast([128, 8, 8]),
                                                       op=ALU.is_ge),
                      rd=[buf("gm"), buf("top8")], wr=[buf("selA")], n=64)
                sc.op("dve", lambda e: e.tensor_tensor(out=gm, in0=selA, in1=pastP, op=ALU.mult),
                      rd=[buf("selA"), buf("mk")], wr=[buf("gm")], n=64)
                sc.op("dve", lambda e: e.tensor_tensor(out=MB, in0=gm, in1=Cc, op=ALU.add),
                      rd=[buf("gm"), buf("mk")], wr=[buf("MB")], n=64)

            def b8(i):
                for h in range(H):
                    sc.op("pe", lambda e, h=h: e.transpose(out=psT[64:72, h * 128:(h + 1) * 128], in_=MB[:, h, :],
                                                          identity=ident),
                          rd=[buf("MB"), buf("cst")], wr=[bT] if h == 0 else (), wp=[bT] if h else (), n=128)

            def b9(i):
                sc.op("act", lambda e: e.copy(out=QT[64:72, :, i * 128:(i + 1) * 128],
                                             in_=psT[64:72, :].rearrange("p (h n) -> p h n", h=8)),
                      rd=[bT], wp=[QTb], n=1024)

            stages = [b1, b2, b3, b4, b5, b6, b7, b8, b9]
            SKEW = 5
            nsteps = len(stages) + 3 * SKEW
            for t in range(nsteps):
                for i in range(4):
                    k = t - SKEW * i
                    if 0 <= k < len(stages):
                        stages[k](i)
                yield
            if G == NG - 1:
                w_out_v = w_out_d.rearrange("(c p) n -> p c n", p=128)
                w_down_v = w_down_d.rearrange("(c p) n -> p c n", p=128)
                dma("pool", "wout", [(w_out_sb[:, c, :], w_out_v[:, c, :]) for c in range(8)], wr=[buf("w_in")])
                dma("pool", "wdA", [(wdA[:, c, :], w_down_v[:, c, :]) for c in range(8)], wp=[buf("w_in")])

        def attn_gen(G):
            QT = QTs[G % 2]
            QTb = QT_b[G % 2]
            nkt = 4 * G + 4
            its = [(h, kt) for h in range(H) for kt in range(nkt)]
            N = len(its)

            def emit_qk(n):
                h, kt = its[n]
                nd = kt - 4 * G
                c0 = max(0, nd) * 128
                sslot = 5 + (n % 2)
                ps = banks[sslot]
                sc.op("pe", lambda e: e.matmul(
                    ps[:, c0:512], lhsT=KT[0:72, h, kt * 128:(kt + 1) * 128], rhs=QT[0:72, h, c0:512],
                    start=True, stop=(nd < 0)),
                    rd=[KT_b[kt // 4], QTb, buf("KTind")], wr=[bank[sslot]])
                if nd >= 0:
                    sc.op("pe", lambda e: e.matmul(ps[:, c0:c0 + 128], lhsT=ident, rhs=tri, start=False, stop=True),
                          rd=[buf("cst")], wp=[bank[sslot]], n=128)

            pending = []
            if G == NG - 1:
                for r_ in range(12):
                    sc.op("pe", lambda e: e.matmul(banks[4], lhsT=ident, rhs=VA[:, 0, 0:4, :].rearrange("p a b -> p (a b)"), start=True, stop=True),
                          rd=[QTb, buf("cst"), VA_b[0], buf("VAones")], wr=[bank[4]] if r_ == 0 else (), wp=[bank[4]] if r_ else ())
            emit_qk(0)
            if N > 1:
                emit_qk(1)
            for n in range(N):
                h, kt = its[n]
                nd = kt - 4 * G
                c0 = max(0, nd) * 128
                sslot = 5 + (n % 2)
                ps = banks[sslot]
                pslot = n % 3
                pob = 7 if h % 2 == 0 else 4
                po = banks[pob]
                sc.op("act", lambda e, c0=c0, ps=ps, pslot=pslot: e.activation(
                    out=PT[pslot][:, c0:512], in_=ps[:, c0:512], func=AF.Exp, scale=SCALE),
                    rd=[bank[sslot]], wr=[buf("PT%d" % pslot)])
                sc.op("pe", lambda e, h=h, kt=kt, c0=c0, pslot=pslot, po=po: e.matmul(
                    po[:, c0:512], lhsT=VA[:, kt, h, :], rhs=PT[pslot][:, c0:512],
                    start=(kt == 0), stop=(kt == nkt - 1)),
                    rd=[buf("PT%d" % pslot), VA_b[kt], buf("VAones")],
                    wr=[bank[pob]] if kt == 0 else (), wp=[bank[pob]] if kt else ())
                if n + 2 < N:
                    emit_qk(n + 2)
                if kt == nkt - 1:
                    def norm(h=h, po=po, pob=pob):
                        ri = rinv[h % 2]
                        rib = buf("rinv%d" % (h % 2))
                        sc.op("dve", lambda e: e.reciprocal(out=ri[0:64, :], in_=po[64:128, :]),
                              rd=[bank[pob]], wr=[rib])
                        p0 = (h % 2) * 64
                        sc.op("dve", lambda e: e.tensor_tensor(
                            out=yT[p0:p0 + 64, h // 2, G * 512:(G + 1) * 512], in0=po[0:64, :], in1=ri[0:64, :],
                            op=ALU.mult),
                            rd=[bank[pob], rib], wr=[yT_b[G]] if h == 0 else (), wp=[yT_b[G]] if h else ())
                    pending.append((n + 2, norm))
                while pending and (pending[0][0] <= n or n == N - 1):
                    pending.pop(0)[1]()
                yield
        def pool_gen(G):
            for g in range(4):
                for i in range(4):
                    tt = 4 * G + i
                    a_idx = 2 if tt == 0 else 0
                    sc.op("pe", lambda e, g=g, i=i, tt=tt, a_idx=a_idx: e.matmul(
                        banks[0][:, i * 128:(i + 1) * 128], lhsT=uring[:, tt % 5, g * 128:(g + 1) * 128],
                        rhs=poolA[:, a_idx, g, :], start=True, stop=(tt == 0)),
                        rd=[ur_b[tt % 5], buf("poolA")], wr=[bank[0]] if i == 0 else (), wp=[bank[0]] if i else (), n=128)
                    if tt > 0:
                        sc.op("pe", lambda e, g=g, i=i, tt=tt: e.matmul(
                            banks[0][:, i * 128:(i + 1) * 128], lhsT=uring[:, (tt - 1) % 5, g * 128:(g + 1) * 128],
                            rhs=poolA[:, 1, g, :], start=False, stop=True),
                            rd=[ur_b[(tt - 1) % 5], buf("poolA")], wp=[bank[0]], n=128)
                sc.op("act", lambda e: e.copy(out=diffs, in_=banks[0]), rd=[bank[0]], wr=[buf("diffs")])
                sc.op("pe", lambda e, g=g: e.matmul(banks[1], lhsT=wpool[:, g, :], rhs=diffs, start=True, stop=True),
                      rd=[buf("diffs"), buf("wpool")], wr=[bank[1]])
                sc.op("dve", lambda e, g=g: e.tensor_scalar(
                    out=yT[:, 4 + g, G * 512:(G + 1) * 512], in0=banks[1], scalar1=colp[:, g:g + 1],
                    scalar2=colp[:, 4 + g:5 + g], op0=ALU.add, op1=ALU.mult),
                    rd=[bank[1], buf("colp")], wp=[yT_b[G]])
                yield

        def drain(gen):
            n = 0
            for _ in gen:
                n += 1
            return n

        def count(genf, G):
            return None

        outs = []
        BAR = [None]
        if dbg:
            ydbg = nc.dram_tensor("dbg_yT", [1024, S], F32, kind="ExternalOutput").ap()
            for c in range(8):
                for half in range(4):
                    sc.op("dve", lambda e, c=c, half=half: e.tensor_copy(out=RD[:, 0, 0:512],
                                                                        in_=yT[:, c, half * 512:(half + 1) * 512]),
                          rd=yT_b, wr=[buf("dbgt")])
                    outs.append(dma("sp", "dbg", [(ydbg[c * 128:(c + 1) * 128, half * 512:(half + 1) * 512],
                                                   RD[:, 0, 0:512])], rd=[buf("dbgt")]))
        if True:
            w_down_v = w_down_d.rearrange("(c p) n -> p c n", p=128)
            wu_b = [buf("wu%d" % i) for i in range(NWU)]

            def load_wu(n):
                pr = n % NPAIR
                return dma("pool", "wu%d" % (n % NWU), [(wu[n % NWU], w_up_d[pr])], wr=[wu_b[n % NWU]],
                           deps=BAR[0] if n < NWU else ())
            x1_b = [buf("x1_%d" % i) for i in range(5)]
            hT2_b = buf("hT2")
            g_b = [buf("gbuf%d" % p_) for p_ in range(NPAIR)]
            tA_b = [buf("tA%d" % i) for i in range(4)]
            halo_b = [buf("halo0"), buf("halo1")]
            n1a = n1t
            n1b = RE[:, 5120:6144]
            acc3 = [pbig_t[:, :], up_t[:, 0:1024], up_t[:, 1024:2048]]
            acc3b = [[bank[0], bank[1]], [bank[3], bank[4]], [bank[5], bank[6]]]

            def mk_sec1(G, with_d, early=False):
                def aidx(i):
                    if early:
                        return 0
                    return 1 if with_d else (1, 2)[i % 2]

                def Amm(i):
                    tt = 4 * G + i
                    pacc, pb_ = acc3[aidx(i)], acc3b[aidx(i)]
                    load_x(tt, deps=BAR[0] if (tt < 2 or early) else ())
                    for dh in range(2):
                        for c in range(8):
                            sc.op("pe", lambda e, c=c, dh=dh: e.matmul(
                                pacc[:, dh * 512:(dh + 1) * 512], lhsT=yT[:, c, tt * 128:(tt + 1) * 128],
                                rhs=w_out_sb[:, c, dh * 512:(dh + 1) * 512], start=(c == 0), stop=(c == 7)),
                                rd=[yT_b[G], buf("w_in")], wr=[pb_[dh]] if c == 0 else (), wp=[pb_[dh]] if c else (),
                                deps=BAR[0] if ((tt == 0 or early) and c == 0) else ())

                def Apost(i, part=None):
                    tt = 4 * G + i
                    s_ = tt % 2
                    pacc, pb_ = acc3[aidx(i)], acc3b[aidx(i)]
                    xb = buf("xn2_%d" % (tt % 2))
                    if part in (None, 0):
                        sc.op("act", lambda e: e.activation(out=xn2[tt % 2], in_=pacc, func=AF.Square,
                                                            accum_out=sq1[:, tt:tt + 1]),
                              rd=pb_, wr=[xb, buf("sq1_%d" % tt)], deps=BAR[0] if (tt == 0 or early) else (), n=1024)
                    if part in (None, 1):
                        rsqrt_col(sq1[:, tt:tt + 1], buf("sq1_%d" % tt), rs1[:, tt:tt + 1], "r1_%d" % tt, "pool")
                    if part in (None, 2):
                        rb1 = buf("r1_%d" % tt)
                        sc.op("dve", lambda e: e.scalar_tensor_tensor(
                            out=n1a, in0=pacc, scalar=rs1[:, tt:tt + 1], in1=g_post, op0=ALU.mult, op1=ALU.mult),
                            rd=pb_ + [rb1, buf("p2c")], wr=[buf("n1a")], n=1024)
                        sc.op(TUNE["add1"], lambda e: e.tensor_tensor(out=x1[:, tt % 5, :], in0=n1a, in1=RD[:, s_, :], op=ALU.add),
                              rd=[buf("n1a"), xslot[s_]], wr=[x1_b[tt % 5]], deps=BAR[0] if (tt == 0 or early) else (), n=1024)

                def Ba(i, part=None):
                    tt = 4 * G + i
                    xb = buf("xn2_%d" % (tt % 2))
                    if part in (None, 0):
                        sc.op("act", lambda e: e.activation(out=xn2[tt % 2], in_=x1[:, tt % 5, :], func=AF.Square,
                                                            accum_out=sq2[:, tt:tt + 1]),
                              rd=[x1_b[tt % 5]], wr=[xb, buf("sq2_%d" % tt)], n=1024)
                    if part in (None, 1):
                        rsqrt_col(sq2[:, tt:tt + 1], buf("sq2_%d" % tt), rs2[:, tt:tt + 1], "r2_%d" % tt, TUNE["r2"],
                                  [buf("r1_%d" % tt)])
                    if part in (None, 2):
                        rb2 = buf("r2_%d" % tt)
                        sc.op("dve", lambda e: e.scalar_tensor_tensor(out=xn2[tt % 2], in0=x1[:, tt % 5, :],
                                                                     scalar=rs2[:, tt:tt + 1], in1=g_fpre,
                                                                     op0=ALU.mult, op1=ALU.mult),
                              rd=[x1_b[tt % 5], rb2, buf("p2c")], wr=[xb], n=1024)

                def Bb(i, part=None):
                    tt = 4 * G + i
                    xb = buf("xn2_%d" % (tt % 2))
                    if part in (None, 0):
                        for c in range(8):
                            sc.op("pe", lambda e, c=c: e.transpose(out=psT[:, c * 128:(c + 1) * 128],
                                                                  in_=xn2[tt % 2][:, c * 128:(c + 1) * 128], identity=ident),
                                  rd=[xb, buf("cst")], wr=[bT] if c == 0 else (), wp=[bT] if c else (), n=128)
                    if part in (None, 1):
                        sc.op("act", lambda e: e.copy(out=hT2[:, :, i * 128:(i + 1) * 128],
                                                     in_=psT.rearrange("p (c n) -> p c n", c=8)),
                              rd=[bT], wr=[hT2_b] if i == 0 else (), wp=[hT2_b] if i else (), n=1024)
                return dict(Amm=Amm, Apost=Apost, Ba=Ba, Bb=Bb)

            def mk_down(G):
                def didx(i):
                    return (0, 2)[i % 2]

                def Dmm(i):
                    pacc, pb_ = acc3[didx(i)], acc3b[didx(i)]
                    for dh in range(2):
                        for c in range(NPAIR):
                            wsrc = wdA[:, c, dh * 512:(dh + 1) * 512] if c < 8 else wdB[:, c - 8, dh * 512:(dh + 1) * 512]
                            sc.op("pe", lambda e, c=c, dh=dh, wsrc=wsrc: e.matmul(
                                pacc[:, dh * 512:(dh + 1) * 512], lhsT=gbuf[:, c, i * 128:(i + 1) * 128], rhs=wsrc,
                                start=(c == 0), stop=(c == NPAIR - 1)),
                                rd=[g_b[c], buf("w_in"), buf("wdB")], wr=[pb_[dh]] if c == 0 else (), wp=[pb_[dh]] if c else ())

                def Dpost(i):
                    tt = 4 * G + i
                    pacc, pb_ = acc3[didx(i)], acc3b[didx(i)]
                    sc.op("act", lambda e: e.activation(out=n1b, in_=pacc, func=AF.Square,
                                                        accum_out=sq3[:, tt:tt + 1]),
                          rd=pb_, wr=[buf("n1b"), buf("sq3_%d" % tt)], n=1024)
                    rb3 = rsqrt_col(sq3[:, tt:tt + 1], buf("sq3_%d" % tt), rs3[:, tt:tt + 1], "r3_%d" % tt, TUNE["r3"],
                                    [buf("r2_%d" % tt)])
                    sc.op("dve", lambda e: e.scalar_tensor_tensor(
                        out=n1b, in0=pacc, scalar=rs3[:, tt:tt + 1], in1=g_fpost, op0=ALU.mult, op1=ALU.mult),
                        rd=pb_ + [rb3, buf("p2c")], wr=[buf("n1b")], n=1024)
                    sc.op(TUNE["add3"], lambda e: e.tensor_tensor(out=x1[:, tt % 5, :], in0=n1b, in1=x1[:, tt % 5, :], op=ALU.add),
                          rd=[buf("n1b")], wr=[x1_b[tt % 5]], n=1024)
                    outs.append(dma("sp", "out%d" % (tt % 5), [(out_d[tt * 128:(tt + 1) * 128, :], x1[:, tt % 5, :])],
                                    rd=[x1_b[tt % 5]]))
                return dict(Dmm=Dmm, Dpost=Dpost)

            ORDER = ["Amm0", "Dmm0", "Apost0", "Amm1", "Dmm1", "Dpost0", "Ba0", "Apost1", "Amm2", "Bb0", "Dmm2", "Dpost1",
                     "Ba1", "Apost2", "Amm3", "Bb1", "Dmm3", "Dpost2", "Ba2", "Apost3", "Bb2", "Ba3", "Dpost3", "Bb3"]

            def run_boundary(Gd, Ga):
                fns = {}
                if Gd is not None:
                    fns.update(mk_down(Gd))
                if Ga is not None:
                    fns.update(mk_sec1(Ga, Gd is not None))
                for name in ORDER:
                    key, i = name[:-1], int(name[-1])
                    if key in fns:
                        fns[key](i)

            def up_section(G):
                if G > 0:
                    hp = halo[:, (G - 1) % 2, :, :]
                    hb = halo_b[(G - 1) % 2]
                    sc.op("pool", lambda e: e.tensor_tensor(out=corr[:, :, 0], in0=hp[:, :, 1], in1=cwb[:, :, 1],
                                                            op=ALU.mult), rd=[hb, buf("p2c")], wr=[buf("corr")], n=44)
                    sc.op("pool", lambda e: e.tensor_tensor(out=ctmp, in0=hp[:, :, 0], in1=cwb[:, :, 0],
                                                            op=ALU.mult), rd=[hb, buf("p2c")], wr=[buf("ctmp")], n=44)
                    sc.op("pool", lambda e: e.tensor_tensor(out=corr[:, :, 0], in0=corr[:, :, 0], in1=ctmp, op=ALU.add),
                          rd=[buf("ctmp"), buf("corr")], wr=[buf("corr")], n=44)
                    sc.op("pool", lambda e: e.tensor_tensor(out=corr[:, :, 1], in0=hp[:, :, 1], in1=cwb[:, :, 0],
                                                            op=ALU.mult), rd=[hb, buf("p2c")], wr=[buf("corr")], n=44)
                    for k_ in range(2):
                        sc.op("pool", lambda e, k_=k_: e.tensor_tensor(out=corr[:, :, k_], in0=corr[:, :, k_],
                                                                      in1=cwb[:, :, 3], op=ALU.add),
                              rd=[buf("p2c")], wr=[buf("corr")], n=44)
                hcur = halo[:, G % 2, :, :]
                hcb = halo_b[G % 2]
                first_halo = [True]
                prev_fin = [None]
                for p in range(NPAIR):
                    n = G * NPAIR + p
                    ws = n % NWU
                    for half in range(2):
                        q_ = 2 * n + half
                        bk = 3 + (q_ % 4)
                        ts = q_ % 4
                        ch = p + NPAIR * half
                        ps = banks[bk]
                        for c in range(8):
                            sc.op("pe", lambda e, c=c, half=half, ps=ps, ws=ws: e.matmul(
                                ps, lhsT=wu[ws][:, c, half * 128:(half + 1) * 128], rhs=hT2[:, c, :],
                                start=(c == 0), stop=(c == 7)),
                                rd=[wu_b[ws], hT2_b], wr=[bank[bk]] if c == 0 else (), wp=[bank[bk]] if c else (),
                                deps=BAR[0] if (n == 0 and c == 0) else ())
                        if G == 0:
                            sc.op("act", lambda e, ps=ps, ch=ch, ts=ts: e.activation(
                                out=tA[ts], in_=ps, func=AF.Identity, scale=cwb[:, ch, 2:3], bias=cwb[:, ch, 3:4]),
                                rd=[bank[bk], buf("p2c")], wr=[tA_b[ts]], deps=BAR[0] if q_ < 4 else ())
                        else:
                            sc.op("act", lambda e, ps=ps, ch=ch, ts=ts: e.activation(
                                out=tA[ts][:, 2:512], in_=ps[:, 2:512], func=AF.Identity, scale=cwb[:, ch, 2:3],
                                bias=cwb[:, ch, 3:4]),
                                rd=[bank[bk], buf("p2c")], wr=[tA_b[ts]])
                            for k_ in range(2):
                                sc.op("act", lambda e, ps=ps, ch=ch, ts=ts, k_=k_: e.activation(
                                    out=tA[ts][:, k_:k_ + 1], in_=ps[:, k_:k_ + 1], func=AF.Identity,
                                    scale=cwb[:, ch, 2:3], bias=corr[:, ch, k_:k_ + 1]),
                                    rd=[bank[bk], buf("p2c"), buf("corr")], wp=[tA_b[ts]], n=1)
                        sc.op("act", lambda e, ps=ps, ch=ch: e.copy(out=hcur[:, ch, :], in_=ps[:, 510:512]),
                              rd=[bank[bk]], wr=[hcb] if first_halo[0] else (), wp=[hcb] if not first_halo[0] else (), n=2)
                        first_halo[0] = False
                        sc.op("dve", lambda e, ps=ps, ch=ch, ts=ts: e.scalar_tensor_tensor(
                            out=tA[ts][:, 1:512], in0=ps[:, 0:511], scalar=cwb[:, ch, 1:2], in1=tA[ts][:, 1:512],
                            op0=ALU.mult, op1=ALU.add),
                            rd=[bank[bk], buf("p2c")], wr=[tA_b[ts]])
                        sc.op("dve", lambda e, ps=ps, ch=ch, ts=ts: e.scalar_tensor_tensor(
                            out=tA[ts][:, 2:512], in0=ps[:, 0:510], scalar=cwb[:, ch, 0:1], in1=tA[ts][:, 2:512],
                            op0=ALU.mult, op1=ALU.add),
                            rd=[bank[bk], buf("p2c")], wr=[tA_b[ts]])
                    if n + NWU < NG * NPAIR:
                        load_wu(n + NWU)
                    if WDB["n"] < 14 and n >= 2:
                        c_ = WDB["n"]
                        dma("pool", "wdB", [(wdB[:, c_, :], w_down_v[:, 8 + c_, :])],
                            wr=[buf("wdB")] if c_ == 0 else (), wp=[buf("wdB")] if c_ else (), deps=BAR[0])
                        WDB["n"] += 1

                    def fin(n=n, p=p):
                        tg, tv = (2 * n) % 4, (2 * n + 1) % 4
                        sc.op("act", lambda e: e.activation(out=tA[tg], in_=tA[tg], func=AF.Silu),
                              rd=[tA_b[tg]], wr=[tA_b[tg]])
                        sc.op(TUNE["gmult"], lambda e: e.tensor_tensor(out=gbuf[:, p, :], in0=tA[tg], in1=tA[tv], op=ALU.mult),
                              rd=[tA_b[tg], tA_b[tv]], wr=[g_b[p]],
                              deps=BAR[0] if n == 0 else ())
                    if prev_fin[0] is not None:
                        prev_fin[0]()
                    prev_fin[0] = fin
                prev_fin[0]()

        drain(prep_gen(0))
        for G in range(NG):
            drain(pool_gen(G))
            if G + 1 < NG:
                drain(attn_gen(G))
                drain(prep_gen(G + 1))
            else:
                sc.new_segment()
                BAR[0] = []
                dma("sp", "c3", [(g_post, gains_d[1]), (g_fpre, gains_d[2]), (g_fpost, gains_d[3]), (cwb, cwb_d)],
                    wr=[buf("p2c")])
                fns0 = mk_sec1(0, False, early=True)
                drain(attn_gen(G))
                for name in ("Amm0", "Apost0", "Amm1", "Ba0", "Apost1", "Bb0", "Amm2", "Ba1", "Apost2", "Bb1", "Amm3",
                             "Ba2", "Apost3", "Bb2", "Ba3", "Bb3"):
                    fns0[name[:-1]](int(name[-1]))

        sc.new_segment()
        BAR[0] = []
        for n in range(NWU):
            load_wu(n)
        sc.op("pool", lambda e: e.memset(corr, 0.0), wr=[buf("corr")], deps=BAR[0], n=512)
        WDB = {"n": 0}
        if True:
            for G in range(NG):
                up_section(G)
                run_boundary(G, G + 1 if G + 1 < NG else None)

        sc.op("sp", lambda e: e.nop(), deps=outs)
        sc.emit(block)
    return nc


def host_prep(inputs):
    f = np.float32
    w_up = np.asarray(inputs["w_up"][0], f)
    w_up_r = np.empty((NPAIR, 128, 8, 256), f)
    wu4 = w_up.reshape(8, 128, 2 * DFF)
    for pr in range(NPAIR):
        w_up_r[pr, :, :, 0:128] = wu4[:, :, pr * 128:(pr + 1) * 128].transpose(1, 0, 2)
        w_up_r[pr, :, :, 128:256] = wu4[:, :, DFF + pr * 128:DFF + (pr + 1) * 128].transpose(1, 0, 2)
    gains = np.stack([np.broadcast_to(np.asarray(inputs[k][0], f), (128, D)) for k in
                      ("norm_mix_pre", "norm_mix_post", "norm_ffn_pre", "norm_ffn_post")]).astype(f)
    colp = np.concatenate([np.asarray(inputs["b_pool"][0], f).reshape(4, 128).T,
                           np.asarray(inputs["pool_scale"][0], f).reshape(4, 128).T], axis=1)
    cw = np.asarray(inputs["conv_w"][0], f)[:, 0, :]
    cb = np.asarray(inputs["conv_b"][0], f)
    cwb = np.concatenate([cw, cb[None]], axis=0).reshape(4, 44, 128).transpose(2, 1, 0)
    inv_freq = (10000.0 ** (-np.arange(0, DH, 2, dtype=f) / DH)).astype(f)
    ang = (np.arange(S, dtype=f)[:, None] * inv_freq[None, :]).astype(f)
    rope = np.concatenate([np.cos(ang), np.sin(ang)], axis=1).astype(f)
    rope = rope.reshape(NT, 128, 64).transpose(1, 0, 2)
    mk = np.zeros((3, 8, 8), f)
    for qb in range(8):
        for j in range(8):
            mk[0, qb, j] = 0.0 if j < qb else -1e30
            mk[1, qb, j] = -NEG if j < qb else 0.0
            mk[2, qb, j] = 0.0 if j == qb else NEG
    mk = np.broadcast_to(mk, (128, 3, 8, 8))
    blk = np.zeros((8, S), f)
    for j in range(8):
        blk[j, j * 256:(j + 1) * 256] = 1.0
    poolA = np.zeros((3, 4, 128, 128), f)
    tp = np.arange(128)[:, None]
    t = np.arange(128)[None, :]
    for g, w in enumerate((2, 4, 8, 16)):
        band = ((t - tp) >= 0) & ((t - tp) < w)
        poolA[0, g] = band / w - (t == tp)
        poolA[1, g] = ((t + 128 - tp) < w) / w
        cnt = np.minimum(t + 1, w)
        poolA[2, g] = band / cnt - (t == tp)
    poolA = poolA.transpose(2, 0, 1, 3)
    idtri = np.zeros((128, 2, 128), f)
    idtri[:, 0, :] = np.eye(128)
    idtri[:, 1, :] = np.where(tp > t, NEG, 0.0)
    c = lambda a: np.ascontiguousarray(a, dtype=f)
    shared = dict(
        w_in=c(inputs["w_in"][0]), w_out=c(inputs["w_out"][0]), w_up_r=c(w_up_r), w_down=c(inputs["w_down"][0]),
        w_pool=c(inputs["w_pool"][0]), gains=c(gains), colp=c(colp), cwb=c(cwb), rope=c(rope), mk=c(mk),
        blkind=c(blk), poolA=c(poolA), idtri=c(idtri))
    return shared


_NC_CACHE = {}


def kernel(**inputs):
    shared = host_prep(inputs)
    x = np.asarray(inputs["x"], np.float32)
    if "nc" not in _NC_CACHE:
        _NC_CACHE["nc"] = build()
    nc = _NC_CACHE["nc"]
    in_maps = [dict(shared, x=np.ascontiguousarray(x[b])) for b in range(8)]
    res = run_bass_kernel_spmd(nc, in_maps, core_ids=list(range(8)))
    return np.stack([r["out"] for r in res.results], axis=0).astype(np.float32)
```

```python
import numpy as np
from contextlib import ExitStack
import concourse.bass as bass
import concourse.mybir as mybir
from concourse.bass_utils import run_bass_kernel_spmd
from concourse.alu_op_type import AluOpType as ALU

F32 = mybir.dt.float32
BF16 = mybir.dt.bfloat16
I32 = mybir.dt.int32
AF = mybir.ActivationFunctionType
AX = mybir.AxisListType

S = 2048
D = 1024
NT = 16
NG = 4
H = 8
DH = 64
DFF = 2816
NPAIR = 22
NEG = -30000.0
EPS = 1e-6
SCALE = DH ** -0.5
COMPUTE = ("pe", "act", "dve", "pool")
TUNE = dict(r0="dve", r2="dve", r3="dve", ktevac="act", add1="dve", add3="dve", corr="dve", gmult="pool")


class Buf:
    def __init__(self):
        self.w = []
        self.r = []
        self.pw = []

    def rd(self):
        return list(self.w)

    def wr_new(self):
        deps = self.r + self.w
        self.pw = deps
        self.w = []
        self.r = []
        return list(deps)

    def wr_more(self):
        return list(self.pw)


class Sched:
    COST = dict(pe=lambda n: 0.008 + n / 2370.0, act=lambda n: 0.22 + n / 1400.0, dve=lambda n: 0.29 + n / 960.0,
                pool=lambda n: 0.36 + n / 560.0, sp=lambda n: 0.05)

    def __init__(self, nc, st):
        self.nc = nc
        self.st = st
        self.streams = {e: [] for e in COMPUTE + ("sp",)}
        self.sems = {e: st.enter_context(nc.semaphore("s_" + e)) for e in COMPUTE}
        self.dsem = {}
        self.seg = 0
        self.order = 0

    def new_segment(self):
        self.seg += 1

    def _collect(self, eng, rd, wr, wp, deps):
        ds = list(deps)
        for b in rd:
            ds += b.rd()
        for b in wr:
            ds += b.wr_new()
        for b in wp:
            ds += b.wr_more()
            if eng == "pe":
                ds += [h for h in b.w if h[0] == "pe"]
        return [d for d in ds if d is not None]

    def op(self, eng, fn, rd=(), wr=(), wp=(), deps=(), n=None):
        raw = self._collect(eng, rd, wr, wp, deps)
        idx = len(self.streams[eng])
        if n is None:
            n = 512
        self.streams[eng].append(dict(fn=fn, raw=raw, kind="op", sig=False, seq=0, eng=eng, idx=idx, seg=self.seg,
                                      cost=self.COST[eng](n), prog=self.order))
        self.order += 1
        h = (eng, idx)
        for b in rd:
            b.r.append(h)
        for b in list(wr) + list(wp):
            b.w.append(h)
        return h

    def dma(self, queue, fn, key, rd=(), wr=(), wp=(), deps=(), nbytes=524288):
        raw = self._collect(queue, rd, wr, wp, deps)
        if key not in self.dsem:
            self.dsem[key] = [self.st.enter_context(self.nc.semaphore("d_" + key)), 0]
        idx = len(self.streams[queue])
        ent = dict(fn=fn, raw=raw, kind="dma", key=key, sig=False, seq=0, eng=queue, idx=idx, seg=self.seg,
                   cost=(0.65 if queue == "pool" else 0.08), lat=2.0 + nbytes / 120000.0, prog=self.order)
        self.order += 1
        self.streams[queue].append(ent)
        h = ["dma", key, None, ent]
        for b in rd:
            b.r.append(h)
        for b in list(wr) + list(wp):
            b.w.append(h)
        return h

    @staticmethod
    def _ent_of(sched, d):
        return d[3] if d[0] == "dma" else sched.streams[d[0]][d[1]]

    def _schedule(self):
        import heapq
        nseg = self.seg + 1
        self.est = []
        new_streams = {e: [] for e in self.streams}
        t_base = 0.0
        for sg in range(nseg):
            ops = [o for e in self.streams for o in self.streams[e] if o["seg"] == sg]
            if not ops:
                continue
            ids = {id(o) for o in ops}
            users = {id(o): [] for o in ops}
            remaining = {}
            ready = {}
            for o in ops:
                cnt = 0
                for d in o["raw"]:
                    p = self._ent_of(self, d)
                    if id(p) in ids:
                        users[id(p)].append(o)
                        cnt += 1
                remaining[id(o)] = cnt
                ready[id(o)] = t_base
            heaps = {e: [] for e in self.streams}
            for o in ops:
                if remaining[id(o)] == 0:
                    heapq.heappush(heaps[o["eng"]], (ready[id(o)], o["prog"], id(o), o))
            free_at = {e: t_base for e in self.streams}
            done = 0
            tmax = t_base
            while done < len(ops):
                best = None
                for e, hp in heaps.items():
                    if hp:
                        st_ = max(free_at[e], hp[0][0])
                        if best is None or st_ < best[0] or (st_ == best[0] and hp[0][1] < best[2]):
                            best = (st_, e, hp[0][1])
                assert best is not None, "scheduler deadlock (dependency cycle)"
                st_, e, _ = best
                hp = heaps[e]
                cands = []
                while hp and hp[0][0] <= st_ + 1e-9:
                    cands.append(heapq.heappop(hp))
                cands.sort(key=lambda c: c[1])
                chosen = cands[0]
                for c in cands[1:]:
                    heapq.heappush(hp, c)
                o = chosen[3]
                fin_eng = st_ + o["cost"]
                free_at[e] = fin_eng
                fin = fin_eng + (o["lat"] if o["kind"] == "dma" else 0.0)
                o["t_fin"] = fin
                tmax = max(tmax, fin)
                new_streams[e].append(o)
                done += 1
                for u in users[id(o)]:
                    lat = 0.12 if (u["eng"] == e and o["kind"] != "dma") else 0.28
                    ready[id(u)] = max(ready[id(u)], fin + lat)
                    remaining[id(u)] -= 1
                    if remaining[id(u)] == 0:
                        heapq.heappush(heaps[u["eng"]], (ready[id(u)], u["prog"], id(u), u))
            self.est.append(round(tmax, 1))
            t_base = tmax
        for e in self.streams:
            assert len(new_streams[e]) == len(self.streams[e])
            self.streams[e] = new_streams[e]
            for pos, o in enumerate(new_streams[e]):
                o["pos"] = pos
        last_of_seg = {}
        for e in COMPUTE:
            for o in self.streams[e]:
                last_of_seg[(e, o["seg"])] = o["pos"]
        for e, stream in self.streams.items():
            prev_seg = None
            for o in stream:
                best = {}
                deps = []
                for d in o["raw"]:
                    if d[0] == "dma":
                        deps.append(d)
                        continue
                    p = self.streams_lookup(d)
                    if d[0] == "pe" and e == "pe":
                        continue
                    if d[0] not in best or best[d[0]] < p["pos"]:
                        best[d[0]] = p["pos"]
                if o["seg"] != prev_seg and o["seg"] > 0:
                    for e2 in COMPUTE:
                        sgs = [s_ for (ee, s_) in last_of_seg if ee == e2 and s_ < o["seg"]]
                        if sgs:
                            pos2 = last_of_seg[(e2, max(sgs))]
                            if not (e2 == "pe" and e == "pe"):
                                if e2 not in best or best[e2] < pos2:
                                    best[e2] = pos2
                prev_seg = o["seg"]
                for k, v in best.items():
                    deps.append((k, v))
                o["deps"] = deps

    def streams_lookup(self, d):
        return self._byid[(d[0], d[1])]

    def emit(self, block):
        self._byid = {}
        for e, stream in self.streams.items():
            for o in stream:
                self._byid[(e, o["idx"])] = o
        self._schedule()
        for eng, stream in self.streams.items():
            for o in stream:
                for d in o["deps"]:
                    if d[0] != "dma":
                        self.streams[d[0]][d[1]]["sig"] = True
        for eng, stream in self.streams.items():
            n = 0
            for o in stream:
                if o["kind"] == "op" and o["sig"]:
                    n += 1
                    o["seq"] = n
        keyq = {}
        for eng, stream in self.streams.items():
            for o in stream:
                if o["kind"] == "dma":
                    assert keyq.setdefault(o["key"], eng) == eng, "dma key used from two queues"
                    self.dsem[o["key"]][1] += 16 * o["ndma"]
                    o["target"] = self.dsem[o["key"]][1]

        def make(eng):
            def body(e):
                waited = {}
                for o in self.streams[eng]:
                    need = {}
                    for d in o["deps"]:
                        if d[0] == "dma":
                            ent = d[3]
                            sem, val = self.dsem[d[1]][0], ent["target"]
                            k = "d_" + d[1]
                        else:
                            src = self.streams[d[0]][d[1]]
                            sem, val = self.sems[d[0]], src["seq"]
                            k = d[0]
                        if k not in need or need[k][1] < val:
                            need[k] = (sem, val)
                    for k, (sem, val) in need.items():
                        if waited.get(k, 0) < val:
                            e.wait_ge(sem, val)
                            waited[k] = val
                    ins = o["fn"](e)
                    if o["kind"] == "dma":
                        sem = self.dsem[o["key"]][0]
                        for i in ins:
                            i.then_inc(sem, 16)
                    elif o["sig"]:
                        ins.then_inc(self.sems[eng], 1)
            return body

        block.sync(make("sp"))
        block.gpsimd(make("pool"))
        block.scalar(make("act"))
        block.vector(make("dve"))
        block.tensor(make("pe"))


def build(dbg=False, p1_only=False):
    nc = bass.Bass("TRN2", target_bir_lowering=False)

    def din(name, shape):
        return nc.dram_tensor(name, list(shape), F32, kind="ExternalInput").ap()

    x_d = din("x", [S, D])
    w_in_d = din("w_in", [D, 2 * D])
    w_out_d = din("w_out", [D, D])
    w_up_d = din("w_up_r", [NPAIR, 128, 8, 256])
    w_down_d = din("w_down", [DFF, D])
    w_pool_d = din("w_pool", [4, 128, 128])
    gains_d = din("gains", [4, 128, D])
    colp_d = din("colp", [128, 8])
    cwb_d = din("cwb", [128, 44, 4])
    rope_d = din("rope", [128, NT, 64])
    mk_d = din("mk", [128, 3, 8, 8])
    blk_d = din("blkind", [8, S])
    poolA_d = din("poolA", [128, 3, 4, 128])
    idtri_d = din("idtri", [128, 2, 128])
    out_d = nc.dram_tensor("out", [S, D], F32, kind="ExternalOutput").ap()

    with ExitStack() as st:
        def sb(name, shape, dt):
            return st.enter_context(nc.sbuf_tensor(name, list(shape), dt))

        yT = sb("yT", [128, 8, S], BF16)
        RA = sb("RA", [128, 8192], F32)
        RB = sb("RB", [128, 8192], F32)
        RC = sb("RC", [128, 8192], F32)
        RD = sb("RD", [128, 2, D], F32)
        RE = sb("RE", [128, 6144], F32)
        RS = sb("RS", [128, 11264], F32)
        CST = sb("CST", [128, 2, 128], BF16)
        SM = sb("SM", [128, 256], F32)

        w_in_sb = RA[:, :].bitcast(BF16).rearrange("p (c n) -> p c n", c=8)
        KT = RB[:, :].bitcast(BF16).rearrange("p (h n) -> p h n", h=8)
        VA = RC[:, :].bitcast(BF16).rearrange("p (t h n) -> p t h n", t=NT, h=8)
        hT = RE[:, 0:2048].bitcast(BF16).rearrange("p (c n) -> p c n", c=8)
        QTs = [RE[:, 2048 * (1 + i):2048 * (2 + i)].bitcast(BF16).rearrange("p (h n) -> p h n", h=8)
               for i in range(2)]
        w_out_sb = RA[:, 0:4096].bitcast(BF16).rearrange("p (c n) -> p c n", c=8)
        wdA = RA[:, 4096:8192].bitcast(BF16).rearrange("p (c n) -> p c n", c=8)
        wdB = RB[:, 0:7168].bitcast(BF16).rearrange("p (c n) -> p c n", c=14)
        gbuf = RC[:, 0:5632].bitcast(BF16).rearrange("p (c n) -> p c n", c=NPAIR)
        NWU = 3
        wu = [RC[:, 5632 + i * 1024: 5632 + (i + 1) * 1024].bitcast(BF16).rearrange("p (c n) -> p c n", c=8)
              for i in range(2)] + \
             [RE[:, 4096:5120].bitcast(BF16).rearrange("p (c n) -> p c n", c=8)]
        hT2 = RE[:, 0:2048].bitcast(BF16).rearrange("p (c n) -> p c n", c=8)
        xn2 = [RE[:, 2048 + i * 512: 2048 + (i + 1) * 512].bitcast(BF16) for i in range(2)]
        n1t = RE[:, 3072:4096]

        o = [0]

        def carve(nwords):
            a = o[0]
            o[0] += nwords
            assert o[0] <= 11264, o[0]
            return RS[:, a:a + nwords]

        g_pre = carve(1024)
        ropet = carve(NT * 64).rearrange("p (t n) -> p t n", t=NT)
        mk = carve(192).rearrange("p (a q j) -> p a q j", a=3, q=8)
        poolA = carve(768).bitcast(BF16).rearrange("p (a g n) -> p a g n", a=3, g=4)
        wpool = carve(256).bitcast(BF16).rearrange("p (g n) -> p g n", g=4)
        colp = carve(8)
        xn = [carve(512).bitcast(BF16) for _ in range(2)]
        rt1s = [carve(512).rearrange("p (h n) -> p h n", h=8) for _ in range(2)]
        rt2s = [carve(512).rearrange("p (h n) -> p h n", h=8) for _ in range(2)]
        qr = carve(256).bitcast(BF16).rearrange("p (h n) -> p h n", h=8)
        kr = carve(256).bitcast(BF16).rearrange("p (h n) -> p h n", h=8)
        uring = carve(5 * 256).bitcast(BF16).rearrange("p (t n) -> p t n", t=5)
        diffs = carve(256).bitcast(BF16)
        kmT = carve(32).bitcast(BF16).rearrange("p (h j) -> p h j", h=8)
        gm = carve(64).rearrange("p (h j) -> p h j", h=8)
        kmTf = carve(64).rearrange("p (h j) -> p h j", h=8)
        top8 = carve(64).rearrange("p (h j) -> p h j", h=8)
        selA = carve(64).rearrange("p (h j) -> p h j", h=8)
        MB = carve(32).bitcast(BF16).rearrange("p (h j) -> p h j", h=8)
        junk = carve(512).bitcast(BF16)
        off_live = o[0]
        PT = [carve(256).bitcast(BF16) for _ in range(3)]
        rinv = [carve(512) for _ in range(2)]
        o[0] = 0
        x1 = carve(5 * 1024).rearrange("p (t n) -> p t n", t=5)
        g_post = carve(1024)
        g_fpre = carve(1024)
        g_fpost = carve(1024)
        cwb = carve(176).rearrange("p (c k) -> p c k", c=44)
        halo = carve(176).rearrange("p (a c k) -> p a c k", a=2, c=44)
        corr = carve(88).rearrange("p (c k) -> p c k", c=44)
        ctmp = carve(44)
        assert o[0] <= off_live, (o[0], off_live)
        tA = [carve(512) for _ in range(4)]
        junk2 = RE[:, 5632:6144].bitcast(BF16)

        ident = CST[:, 0, :]
        tri = CST[:, 1, :]
        ssq = SM[:, 0:16]
        rstd = SM[:, 16:32]
        nt_a = SM[:, 32:48]
        nt_b = SM[:, 48:64]
        nt_c = SM[:, 64:80]
        sq1 = SM[:, 80:96]
        sq2 = SM[:, 96:112]
        sq3 = SM[:, 112:128]
        rs1 = SM[:, 128:144]
        rs2 = SM[:, 144:160]
        rs3 = SM[:, 160:176]

        pbig_t = st.enter_context(nc.psum_tensor("pbig", [128, 1024], F32))
        pT_t = st.enter_context(nc.psum_tensor("pT", [128, 512], F32))
        up_t = st.enter_context(nc.psum_tensor("pup", [128, 2048], F32))
        p7_t = st.enter_context(nc.psum_tensor("p7", [128, 512], F32))
        banks = [pbig_t[:, 0:512], pbig_t[:, 512:1024], pT_t[:, :]] + \
                [up_t[:, k * 512:(k + 1) * 512] for k in range(4)] + [p7_t[:, :]]
        acc = [pbig_t[:, :], up_t[:, 0:1024]]
        block = st.enter_context(nc.Block())
        sc = Sched(nc, st)

        def dma(queue, key, pairs, **kw):
            def fn(e, pairs=pairs):
                return [e.dma_start(out=o_, in_=i_) for (o_, i_) in pairs]
            h = sc.dma(queue, fn, key, **kw)
            h[3]["ndma"] = len(pairs)
            return h

        B = {}

        def buf(name):
            if name not in B:
                B[name] = Buf()
            return B[name]

        bank = [buf("bank%d" % i) for i in range(8)]
        accb = [[bank[0], bank[1]], [bank[3], bank[4]]]

        dma("sp", "c1", [(g_pre, gains_d[0]), (ropet, rope_d), (mk, mk_d), (colp, colp_d)],
            wr=[buf("g_pre"), buf("ropet"), buf("mk"), buf("colp")])
        dma("pool", "c2", [(CST[:, :, :], idtri_d), (poolA, poolA_d), (wpool, w_pool_d.rearrange("g i o -> i g o"))]
            + [(KT[64:72, h, :], blk_d) for h in range(H)],
            wr=[buf("cst"), buf("poolA"), buf("wpool"), buf("KTind")])
        w_in_v = w_in_d.rearrange("(c p) n -> p c n", p=128)
        dma("pool", "win", [(w_in_sb[:, c, :], w_in_v[:, c, :]) for c in range(8)], wr=[buf("w_in")])

        xslot = [buf("xs%d" % i) for i in range(2)]

        def load_x(tt, deps=()):
            s = tt % 2
            return dma("sp", "x%d" % s, [(RD[:, s, :], x_d[tt * 128:(tt + 1) * 128, :])], wr=[xslot[s]], deps=deps)

        sc.op("pool", lambda e: e.memset(VA[:, :, :, 64:128], 1.0), wr=[buf("VAones")], n=512)
        sc.op("pool", lambda e: e.memset(kmT, 0.0), wr=[buf("kmT")], n=64)

        def rsqrt_col(ssq_col, ssq_buf, y, name, newton_eng="dve", extra_rd=()):
            v, hv, t = nt_a[:, 0:1], nt_b[:, 0:1], nt_c[:, 0:1]
            col = int(name.split("_")[-1])
            v, hv, t = nt_a[:, col:col + 1], nt_b[:, col:col + 1], nt_c[:, col:col + 1]
            bv, bh, bt, by = buf(name + "_v"), buf(name + "_h"), buf(name + "_t"), buf(name)
            sc.op("dve", lambda e: e.tensor_scalar(out=v, in0=ssq_col, scalar1=1.0 / D, scalar2=EPS,
                                                   op0=ALU.mult, op1=ALU.add),
                  rd=[ssq_buf] + list(extra_rd), wr=[bv], n=1)
            sc.op("dve", lambda e: e.tensor_scalar(out=y.bitcast(I32), in0=v.bitcast(I32), scalar1=-0.5,
                                                   scalar2=1597463007.0, op0=ALU.mult, op1=ALU.add),
                  rd=[bv], wr=[by], n=1)
            ne = newton_eng
            sc.op(ne, lambda e: e.tensor_scalar(out=hv, in0=v, scalar1=-0.5, scalar2=None, op0=ALU.mult),
                  rd=[bv], wr=[bh], n=1)
            for _ in range(3):
                if ne == "dve":
                    sc.op("dve", lambda e: e.scalar_tensor_tensor(out=t, in0=hv, scalar=y, in1=y,
                                                                 op0=ALU.mult, op1=ALU.mult),
                          rd=[bh, by], wr=[bt], n=1)
                else:
                    sc.op(ne, lambda e: e.tensor_tensor(out=t, in0=hv, in1=y, op=ALU.mult), rd=[bh, by], wr=[bt], n=1)
                    sc.op(ne, lambda e: e.tensor_tensor(out=t, in0=t, in1=y, op=ALU.mult), rd=[bt, by], wr=[bt], n=1)
                sc.op(ne, lambda e: e.tensor_scalar(out=y, in0=t, scalar1=1.5, scalar2=y, op0=ALU.add, op1=ALU.mult),
                      rd=[bt, by], wr=[by], n=1)
            return by

        load_x(0)
        load_x(1)
        bT = bank[2]
        bT2 = bank[3]
        psT = banks[2].bitcast(BF16)
        psT2 = banks[3].bitcast(BF16)
        hT_b = [buf("hT%d" % i) for i in range(4)]
        QT_b = [buf("QT0"), buf("QT1")]
        KT_b = [buf("KT%d" % t_) for t_ in range(NT)]
        VA_b = [buf("VA%d" % t) for t in range(NT)]
        ur_b = [buf("ur%d" % i) for i in range(5)]
        yT_b = [buf("yT%d" % g) for g in range(NG)]

        def prep_gen(G):
            QT = QTs[G % 2]
            QTb = QT_b[G % 2]

            def S1(i):
                tt = 4 * G + i
                s_ = tt % 2
                sc.op("act", lambda e: e.activation(out=junk, in_=RD[:, s_, :], func=AF.Square,
                                                    accum_out=ssq[:, tt:tt + 1]),
                      rd=[xslot[s_]], wr=[buf("junk"), buf("ssq%d" % tt)], n=1024)

            def S2(i):
                tt = 4 * G + i
                s_ = tt % 2
                rb = rsqrt_col(ssq[:, tt:tt + 1], buf("ssq%d" % tt), rstd[:, tt:tt + 1], "r0_%d" % tt, TUNE["r0"])
                sc.op("dve", lambda e: e.scalar_tensor_tensor(out=xn[tt % 2], in0=RD[:, s_, :],
                                                             scalar=rstd[:, tt:tt + 1], in1=g_pre,
                                                             op0=ALU.mult, op1=ALU.mult),
                      rd=[xslot[s_], rb, buf("g_pre")], wr=[buf("xn%d" % (tt % 2))], n=1024)
                if tt + 2 < NT:
                    load_x(tt + 2)

            def S3(i):
                tt = 4 * G + i
                for c in range(8):
                    sc.op("pe", lambda e, c=c: e.transpose(out=psT[:, c * 128:(c + 1) * 128],
                                                          in_=xn[tt % 2][:, c * 128:(c + 1) * 128], identity=ident),
                          rd=[buf("xn%d" % (tt % 2)), buf("cst")], wr=[bT] if c == 0 else (), wp=[bT] if c else (), n=128)

            def S4(i):
                sc.op("act", lambda e: e.copy(out=hT[:, :, i * 128:(i + 1) * 128],
                                             in_=psT.rearrange("p (c n) -> p c n", c=8)),
                      rd=[bT], wr=[hT_b[i]], n=1024)

            for step in ([(S1, 0), (S1, 1)], [(S2, 0)], [(S3, 0), (S2, 1)], [(S4, 0), (S1, 2)],
                         [(S3, 1), (S2, 2)], [(S4, 1), (S1, 3)], [(S3, 2), (S2, 3)], [(S4, 2)], [(S3, 3)], [(S4, 3)]):
                for fn, i in step:
                    fn(i)
                yield

            def proj(i, j, pb):
                for c in range(8):
                    sc.op("pe", lambda e, c=c: e.matmul(
                        banks[pb], lhsT=hT[:, c, i * 128:(i + 1) * 128],
                        rhs=w_in_sb[:, c, j * 512:(j + 1) * 512], start=(c == 0), stop=(c == 7)),
                        rd=[hT_b[i], buf("w_in")], wr=[bank[pb]] if c == 0 else (), wp=[bank[pb]] if c else ())

            def rope_dve(i, j):
                tt = 4 * G + i
                pb = j
                rt1, rt2 = rt1s[j], rt2s[j]
                psv = banks[pb].rearrange("p (h n) -> p h n", h=8)
                cosb = ropet[:, tt, 0:32].unsqueeze(1).to_broadcast([128, 8, 32])
                sinb = ropet[:, tt, 32:64].unsqueeze(1).to_broadcast([128, 8, 32])
                b1_, b2_ = buf("rt1_%d" % j), buf("rt2_%d" % j)
                sc.op("dve", lambda e: e.tensor_tensor(out=rt1[:, :, 0:32], in0=psv[:, :, 0:32], in1=cosb, op=ALU.mult),
                      rd=[bank[pb], buf("ropet")], wr=[b1_])
                sc.op("dve", lambda e: e.tensor_tensor(out=rt1[:, :, 32:64], in0=psv[:, :, 32:64], in1=cosb, op=ALU.mult),
                      rd=[bank[pb], buf("ropet")], wp=[b1_])
                sc.op("dve", lambda e: e.tensor_tensor(out=rt2[:, :, 0:32], in0=psv[:, :, 32:64], in1=sinb, op=ALU.mult),
                      rd=[bank[pb], buf("ropet")], wr=[b2_])
                sc.op("dve", lambda e: e.tensor_tensor(out=rt2[:, :, 32:64], in0=psv[:, :, 0:32], in1=sinb, op=ALU.mult),
                      rd=[bank[pb], buf("ropet")], wp=[b2_])

            def rope_pool(i, j):
                rt1, rt2 = rt1s[j], rt2s[j]
                b1_, b2_ = buf("rt1_%d" % j), buf("rt2_%d" % j)
                dst = qr if j == 0 else kr
                dstb = buf("qr") if j == 0 else buf("kr")
                sc.op("pool", lambda e: e.tensor_tensor(out=dst[:, :, 0:32], in0=rt1[:, :, 0:32], in1=rt2[:, :, 0:32],
                                                        op=ALU.subtract), rd=[b1_, b2_], wr=[dstb])
                sc.op("pool", lambda e: e.tensor_tensor(out=dst[:, :, 32:64], in0=rt1[:, :, 32:64], in1=rt2[:, :, 32:64],
                                                        op=ALU.add), rd=[b1_, b2_], wp=[dstb])

            def b1(i):
                proj(i, 0, 0)
                proj(i, 1, 1)

            def b2(i):
                rope_dve(i, 0)
                rope_dve(i, 1)

            def b3(i):
                rope_pool(i, 0)
                rope_pool(i, 1)
                proj(i, 2, 0)
                proj(i, 3, 1)

            def b4(i):
                tt = 4 * G + i
                psv = banks[0].rearrange("p (h n) -> p h n", h=8)
                sc.op("act", lambda e: e.copy(out=VA[:, tt, :, 0:64], in_=psv),
                      rd=[bank[0]], wr=[VA_b[tt]], deps=buf("VAones").rd())
                sc.op("act", lambda e: e.copy(out=uring[:, tt % 5, :], in_=banks[1]),
                      rd=[bank[1]], wr=[ur_b[tt % 5]])
                for h in range(H):
                    sc.op("pe", lambda e, h=h: e.transpose(out=psT[0:64, h * 128:(h + 1) * 128], in_=qr[:, h, :],
                                                          identity=ident),
                          rd=[buf("qr"), buf("cst")], wr=[bT] if h == 0 else (), wp=[bT] if h else (), n=128)
                for h in range(H):
                    sc.op("pe", lambda e, h=h: e.transpose(out=psT2[0:64, h * 128:(h + 1) * 128], in_=kr[:, h, :],
                                                          identity=ident),
                          rd=[buf("kr"), buf("cst")], wr=[bT2] if h == 0 else (), wp=[bT2] if h else (), n=128)

            def b5(i):
                tt = 4 * G + i
                sc.op("act", lambda e: e.copy(
                    out=QT[0:64, :, i * 128:(i + 1) * 128], in_=psT[0:64, :].rearrange("p (h n) -> p h n", h=8)),
                    rd=[bT], wr=[QTb] if i == 0 else (), wp=[QTb] if i else (), n=1024)
                sc.op(TUNE["ktevac"], lambda e: (e.tensor_copy if TUNE["ktevac"] == "dve" else e.copy)(
                    out=KT[0:64, :, tt * 128:(tt + 1) * 128], in_=psT2[0:64, :].rearrange("p (h n) -> p h n", h=8)),
                    rd=[bT2], wr=[KT_b[tt]], n=1024)
                if tt % 2 == 1:
                    jb = tt // 2
                    sc.op("dve", lambda e: e.tensor_reduce(
                        out=kmTf[0:64, :, jb:jb + 1], in_=KT[0:64, :, jb * 256:(jb + 1) * 256],
                        axis=AX.X, op=ALU.add), rd=[KT_b[tt - 1], KT_b[tt]], wr=[buf("kmTf")], n=64)
                    sc.op("dve", lambda e: e.tensor_copy(out=kmT[0:64, :, jb:jb + 1], in_=kmTf[0:64, :, jb:jb + 1]),
                          rd=[buf("kmTf")], wp=[buf("kmT")], n=64)

            def b6(i):
                for h in range(H):
                    sc.op("pe", lambda e, h=h: e.matmul(
                        banks[3][:, h * 8:(h + 1) * 8], lhsT=QT[0:64, h, i * 128:(i + 1) * 128], rhs=kmT[0:64, h, :],
                        start=True, stop=True),
                        rd=[QTb, buf("kmT")], wr=[bT2] if h == 0 else (), wp=[bT2] if h else (), n=8)

            def b7(i):
                qb = (4 * G + i) // 2
                pastneg = mk[:, 0, qb, :].unsqueeze(1).to_broadcast([128, 8, 8])
                pastP = mk[:, 1, qb, :].unsqueeze(1).to_broadcast([128, 8, 8])
                Cc = mk[:, 2, qb, :].unsqueeze(1).to_broadcast([128, 8, 8])
                sc.op("dve", lambda e: e.tensor_tensor(
                    out=gm, in0=banks[3][:, 0:64].rearrange("p (h j) -> p h j", h=8), in1=pastneg, op=ALU.add),
                    rd=[bT2, buf("mk")], wr=[buf("gm")], n=64)
                for h in range(H):
                    sc.op("dve", lambda e, h=h: e.max(out=top8[:, h, :], in_=gm[:, h, :]),
                          rd=[buf("gm")], wr=[buf("top8")] if h == 0 else (), wp=[buf("top8")] if h else (), n=8)
                sc.op("dve", lambda e: e.tensor_tensor(out=selA, in0=gm, in1=top8[:, :, 2:3].to_broadcast([128, 8, 8]),
                                                       op=ALU.is_ge),
                      rd=[buf("gm"), buf("top8")], wr=[buf("selA")], n=64)
                sc.op("dve", lambda e: e.tensor_tensor(out=gm, in0=selA, in1=pastP, op=ALU.mult),
                      rd=[buf("selA"), buf("mk")], wr=[buf("gm")], n=64)
                sc.op("dve", lambda e: e.tensor_tensor(out=MB, in0=gm, in1=Cc, op=ALU.add),
                      rd=[buf("gm"), buf("mk")], wr=[buf("MB")], n=64)

            def b8(i):
                for h in range(H):
                    sc.op("pe", lambda e, h=h: e.transpose(out=psT[64:72, h * 128:(h + 1) * 128], in_=MB[:, h, :],
                                                          identity=ident),
                          rd=[buf("MB"), buf("cst")], wr=[bT] if h == 0 else (), wp=[bT] if h else (), n=128)

            def b9(i):
                sc.op("act", lambda e: e.copy(out=QT[64:72, :, i * 128:(i + 1) * 128],
                                             in_=psT[64:72, :].rearrange("p (h n) -> p h n", h=8)),
                      rd=[bT], wp=[QTb], n=1024)

            stages = [b1, b2, b3, b4, b5, b6, b7, b8, b9]
            SKEW = 5
            nsteps = len(stages) + 3 * SKEW
            for t in range(nsteps):
                for i in range(4):
                    k = t - SKEW * i
                    if 0 <= k < len(stages):
                        stages[k](i)
                yield
            if G == NG - 1:
                w_out_v = w_out_d.rearrange("(c p) n -> p c n", p=128)
                w_down_v = w_down_d.rearrange("(c p) n -> p c n", p=128)
                dma("pool", "wout", [(w_out_sb[:, c, :], w_out_v[:, c, :]) for c in range(8)], wr=[buf("w_in")])
                dma("pool", "wdA", [(wdA[:, c, :], w_down_v[:, c, :]) for c in range(8)], wp=[buf("w_in")])

        def attn_gen(G):
            QT = QTs[G % 2]
            QTb = QT_b[G % 2]
            nkt = 4 * G + 4
            its = [(h, kt) for h in range(H) for kt in range(nkt)]
            N = len(its)

            def emit_qk(n):
                h, kt = its[n]
                nd = kt - 4 * G
                c0 = max(0, nd) * 128
                sslot = 5 + (n % 2)
                ps = banks[sslot]
                sc.op("pe", lambda e: e.matmul(
                    ps[:, c0:512], lhsT=KT[0:72, h, kt * 128:(kt + 1) * 128], rhs=QT[0:72, h, c0:512],
                    start=True, stop=(nd < 0)),
                    rd=[KT_b[kt], QTb, buf("KTind")], wr=[bank[sslot]])
                if nd >= 0:
                    sc.op("pe", lambda e: e.matmul(ps[:, c0:c0 + 128], lhsT=ident, rhs=tri, start=False, stop=True),
                          rd=[buf("cst")], wp=[bank[sslot]], n=128)

            pending = []
            if G == NG - 1:
                for r_ in range(12):
                    sc.op("pe", lambda e: e.matmul(banks[4], lhsT=ident, rhs=VA[:, 0, 0:4, :].rearrange("p a b -> p (a b)"), start=True, stop=True),
                          rd=[QTb, buf("cst"), VA_b[0], buf("VAones")], wr=[bank[4]] if r_ == 0 else (), wp=[bank[4]] if r_ else ())
            emit_qk(0)
            if N > 1:
                emit_qk(1)
            for n in range(N):
                h, kt = its[n]
                nd = kt - 4 * G
                c0 = max(0, nd) * 128
                sslot = 5 + (n % 2)
                ps = banks[sslot]
                pslot = n % 3
                pob = 7 if h % 2 == 0 else 4
                po = banks[pob]
                sc.op("act", lambda e, c0=c0, ps=ps, pslot=pslot: e.activation(
                    out=PT[pslot][:, c0:512], in_=ps[:, c0:512], func=AF.Exp, scale=SCALE),
                    rd=[bank[sslot]], wr=[buf("PT%d" % pslot)])
                sc.op("pe", lambda e, h=h, kt=kt, c0=c0, pslot=pslot, po=po: e.matmul(
                    po[:, c0:512], lhsT=VA[:, kt, h, :], rhs=PT[pslot][:, c0:512],
                    start=(kt == 0), stop=(kt == nkt - 1)),
                    rd=[buf("PT%d" % pslot), VA_b[kt], buf("VAones")],
                    wr=[bank[pob]] if kt == 0 else (), wp=[bank[pob]] if kt else ())
                if n + 2 < N:
                    emit_qk(n + 2)
                if kt == nkt - 1:
                    def norm(h=h, po=po, pob=pob):
                        ri = rinv[h % 2]
                        rib = buf("rinv%d" % (h % 2))
                        sc.op("dve", lambda e: e.reciprocal(out=ri[0:64, :], in_=po[64:128, :]),
                              rd=[bank[pob]], wr=[rib])
                        p0 = (h % 2) * 64
                        sc.op("dve", lambda e: e.tensor_tensor(
                            out=yT[p0:p0 + 64, h // 2, G * 512:(G + 1) * 512], in0=po[0:64, :], in1=ri[0:64, :],
                            op=ALU.mult),
                            rd=[bank[pob], rib], wr=[yT_b[G]] if h == 0 else (), wp=[yT_b[G]] if h else ())
                    pending.append((n + 2, norm))
                while pending and (pending[0][0] <= n or n == N - 1):
                    pending.pop(0)[1]()
                yield
        def pool_gen(G):
            for g in range(4):
                for i in range(4):
                    tt = 4 * G + i
                    a_idx = 2 if tt == 0 else 0
                    sc.op("pe", lambda e, g=g, i=i, tt=tt, a_idx=a_idx: e.matmul(
                        banks[0][:, i * 128:(i + 1) * 128], lhsT=uring[:, tt % 5, g * 128:(g + 1) * 128],
                        rhs=poolA[:, a_idx, g, :], start=True, stop=(tt == 0)),
                        rd=[ur_b[tt % 5], buf("poolA")], wr=[bank[0]] if i == 0 else (), wp=[bank[0]] if i else (), n=128)
                    if tt > 0:
                        sc.op("pe", lambda e, g=g, i=i, tt=tt: e.matmul(
                            banks[0][:, i * 128:(i + 1) * 128], lhsT=uring[:, (tt - 1) % 5, g * 128:(g + 1) * 128],
                            rhs=poolA[:, 1, g, :], start=False, stop=True),
                            rd=[ur_b[(tt - 1) % 5], buf("poolA")], wp=[bank[0]], n=128)
                sc.op("act", lambda e: e.copy(out=diffs, in_=banks[0]), rd=[bank[0]], wr=[buf("diffs")])
                sc.op("pe", lambda e, g=g: e.matmul(banks[1], lhsT=wpool[:, g, :], rhs=diffs, start=True, stop=True),
                      rd=[buf("diffs"), buf("wpool")], wr=[bank[1]])
                sc.op("dve", lambda e, g=g: e.tensor_scalar(
                    out=yT[:, 4 + g, G * 512:(G + 1) * 512], in0=banks[1], scalar1=colp[:, g:g + 1],
                    scalar2=colp[:, 4 + g:5 + g], op0=ALU.add, op1=ALU.mult),
                    rd=[bank[1], buf("colp")], wp=[yT_b[G]])
                yield

        def drain(gen):
            n = 0
            for _ in gen:
                n += 1
            return n

        def count(genf, G):
            return None

        outs = []
        BAR = [None]
        if dbg:
            ydbg = nc.dram_tensor("dbg_yT", [1024, S], F32, kind="ExternalOutput").ap()
            for c in range(8):
                for half in range(4):
                    sc.op("dve", lambda e, c=c, half=half: e.tensor_copy(out=RD[:, 0, 0:512],
                                                                        in_=yT[:, c, half * 512:(half + 1) * 512]),
                          rd=yT_b, wr=[buf("dbgt")])
                    outs.append(dma("sp", "dbg", [(ydbg[c * 128:(c + 1) * 128, half * 512:(half + 1) * 512],
                                                   RD[:, 0, 0:512])], rd=[buf("dbgt")]))
        if True:
            w_down_v = w_down_d.rearrange("(c p) n -> p c n", p=128)
            wu_b = [buf("wu%d" % i) for i in range(NWU)]

            def load_wu(n):
                pr = n % NPAIR
                return dma("pool", "wu%d" % (n % NWU), [(wu[n % NWU], w_up_d[pr])], wr=[wu_b[n % NWU]],
                           deps=BAR[0] if n < NWU else ())
            x1_b = [buf("x1_%d" % i) for i in range(5)]
            hT2_b = buf("hT2")
            g_b = [buf("gbuf%d" % p_) for p_ in range(NPAIR)]
            tA_b = [buf("tA%d" % i) for i in range(4)]
            halo_b = [buf("halo0"), buf("halo1")]
            n1a = n1t
            n1b = RE[:, 5120:6144]
            acc3 = [pbig_t[:, :], up_t[:, 0:1024], up_t[:, 1024:2048]]
            acc3b = [[bank[0], bank[1]], [bank[3], bank[4]], [bank[5], bank[6]]]

            def mk_sec1(G, with_d, early=False):
                def aidx(i):
                    if early:
                        return 0
                    return 1 if with_d else (1, 2)[i % 2]

                def Amm(i):
                    tt = 4 * G + i
                    pacc, pb_ = acc3[aidx(i)], acc3b[aidx(i)]
                    load_x(tt, deps=BAR[0] if (tt < 2 or early) else ())
                    for dh in range(2):
                        for c in range(8):
                            sc.op("pe", lambda e, c=c, dh=dh: e.matmul(
                                pacc[:, dh * 512:(dh + 1) * 512], lhsT=yT[:, c, tt * 128:(tt + 1) * 128],
                                rhs=w_out_sb[:, c, dh * 512:(dh + 1) * 512], start=(c == 0), stop=(c == 7)),
                                rd=[yT_b[G], buf("w_in")], wr=[pb_[dh]] if c == 0 else (), wp=[pb_[dh]] if c else (),
                                deps=BAR[0] if ((tt == 0 or early) and c == 0) else ())

                def Apost(i, part=None):
                    tt = 4 * G + i
                    s_ = tt % 2
                    pacc, pb_ = acc3[aidx(i)], acc3b[aidx(i)]
                    xb = buf("xn2_%d" % (tt % 2))
                    if part in (None, 0):
                        sc.op("act", lambda e: e.activation(out=xn2[tt % 2], in_=pacc, func=AF.Square,
                                                            accum_out=sq1[:, tt:tt + 1]),
                              rd=pb_, wr=[xb, buf("sq1_%d" % tt)], deps=BAR[0] if (tt == 0 or early) else (), n=1024)
                    if part in (None, 1):
                        rsqrt_col(sq1[:, tt:tt + 1], buf("sq1_%d" % tt), rs1[:, tt:tt + 1], "r1_%d" % tt, "pool")
                    if part in (None, 2):
                        rb1 = buf("r1_%d" % tt)
                        sc.op("dve", lambda e: e.scalar_tensor_tensor(
                            out=n1a, in0=pacc, scalar=rs1[:, tt:tt + 1], in1=g_post, op0=ALU.mult, op1=ALU.mult),
                            rd=pb_ + [rb1, buf("p2c")], wr=[buf("n1a")], n=1024)
                        sc.op(TUNE["add1"], lambda e: e.tensor_tensor(out=x1[:, tt % 5, :], in0=n1a, in1=RD[:, s_, :], op=ALU.add),
                              rd=[buf("n1a"), xslot[s_]], wr=[x1_b[tt % 5]], deps=BAR[0] if (tt == 0 or early) else (), n=1024)

                def Ba(i, part=None):
                    tt = 4 * G + i
                    xb = buf("xn2_%d" % (tt % 2))
                    if part in (None, 0):
                        sc.op("act", lambda e: e.activation(out=xn2[tt % 2], in_=x1[:, tt % 5, :], func=AF.Square,
                                                            accum_out=sq2[:, tt:tt + 1]),
                              rd=[x1_b[tt % 5]], wr=[xb, buf("sq2_%d" % tt)], n=1024)
                    if part in (None, 1):
                        rsqrt_col(sq2[:, tt:tt + 1], buf("sq2_%d" % tt), rs2[:, tt:tt + 1], "r2_%d" % tt, TUNE["r2"],
                                  [buf("r1_%d" % tt)])
                    if part in (None, 2):
                        rb2 = buf("r2_%d" % tt)
                        sc.op("dve", lambda e: e.scalar_tensor_tensor(out=xn2[tt % 2], in0=x1[:, tt % 5, :],
                                                                     scalar=rs2[:, tt:tt + 1], in1=g_fpre,
                                                                     op0=ALU.mult, op1=ALU.mult),
                              rd=[x1_b[tt % 5], rb2, buf("p2c")], wr=[xb], n=1024)

                def Bb(i, part=None):
                    tt = 4 * G + i
                    xb = buf("xn2_%d" % (tt % 2))
                    if part in (None, 0):
                        for c in range(8):
                            sc.op("pe", lambda e, c=c: e.transpose(out=psT[:, c * 128:(c + 1) * 128],
                                                                  in_=xn2[tt % 2][:, c * 128:(c + 1) * 128], identity=ident),
                                  rd=[xb, buf("cst")], wr=[bT] if c == 0 else (), wp=[bT] if c else (), n=128)
                    if part in (None, 1):
                        sc.op("act", lambda e: e.copy(out=hT2[:, :, i * 128:(i + 1) * 128],
                                                     in_=psT.rearrange("p (c n) -> p c n", c=8)),
                              rd=[bT], wr=[hT2_b] if i == 0 else (), wp=[hT2_b] if i else (), n=1024)
                return dict(Amm=Amm, Apost=Apost, Ba=Ba, Bb=Bb)

            def mk_down(G):
                def didx(i):
                    return (0, 2)[i % 2]

                def Dmm(i):
                    pacc, pb_ = acc3[didx(i)], acc3b[didx(i)]
                    for dh in range(2):
                        for c in range(NPAIR):
                            wsrc = wdA[:, c, dh * 512:(dh + 1) * 512] if c < 8 else wdB[:, c - 8, dh * 512:(dh + 1) * 512]
                            sc.op("pe", lambda e, c=c, dh=dh, wsrc=wsrc: e.matmul(
                                pacc[:, dh * 512:(dh + 1) * 512], lhsT=gbuf[:, c, i * 128:(i + 1) * 128], rhs=wsrc,
                                start=(c == 0), stop=(c == NPAIR - 1)),
                                rd=[g_b[c], buf("w_in"), buf("wdB")], wr=[pb_[dh]] if c == 0 else (), wp=[pb_[dh]] if c else ())

                def Dpost(i):
                    tt = 4 * G + i
                    pacc, pb_ = acc3[didx(i)], acc3b[didx(i)]
                    sc.op("act", lambda e: e.activation(out=n1b, in_=pacc, func=AF.Square,
                                                        accum_out=sq3[:, tt:tt + 1]),
                          rd=pb_, wr=[buf("n1b"), buf("sq3_%d" % tt)], n=1024)
                    rb3 = rsqrt_col(sq3[:, tt:tt + 1], buf("sq3_%d" % tt), rs3[:, tt:tt + 1], "r3_%d" % tt, TUNE["r3"],
                                    [buf("r2_%d" % tt)])
                    sc.op("dve", lambda e: e.scalar_tensor_tensor(
                        out=n1b, in0=pacc, scalar=rs3[:, tt:tt + 1], in1=g_fpost, op0=ALU.mult, op1=ALU.mult),
                        rd=pb_ + [rb3, buf("p2c")], wr=[buf("n1b")], n=1024)
                    sc.op(TUNE["add3"], lambda e: e.tensor_tensor(out=x1[:, tt % 5, :], in0=n1b, in1=x1[:, tt % 5, :], op=ALU.add),
                          rd=[buf("n1b")], wr=[x1_b[tt % 5]], n=1024)
                    outs.append(dma("sp", "out%d" % (tt % 5), [(out_d[tt * 128:(tt + 1) * 128, :], x1[:, tt % 5, :])],
                                    rd=[x1_b[tt % 5]]))
                return dict(Dmm=Dmm, Dpost=Dpost)

            ORDER = ["Amm0", "Dmm0", "Apost0", "Amm1", "Dmm1", "Dpost0", "Ba0", "Apost1", "Amm2", "Bb0", "Dmm2", "Dpost1",
                     "Ba1", "Apost2", "Amm3", "Bb1", "Dmm3", "Dpost2", "Ba2", "Apost3", "Bb2", "Ba3", "Dpost3", "Bb3"]

            def run_boundary(Gd, Ga):
                fns = {}
                if Gd is not None:
                    fns.update(mk_down(Gd))
                if Ga is not None:
                    fns.update(mk_sec1(Ga, Gd is not None))
                for name in ORDER:
                    key, i = name[:-1], int(name[-1])
                    if key in fns:
                        fns[key](i)

            def up_section(G):
                if G > 0:
                    hp = halo[:, (G - 1) % 2, :, :]
                    hb = halo_b[(G - 1) % 2]
                    sc.op("pool", lambda e: e.tensor_tensor(out=corr[:, :, 0], in0=hp[:, :, 1], in1=cwb[:, :, 1],
                                                            op=ALU.mult), rd=[hb, buf("p2c")], wr=[buf("corr")], n=44)
                    sc.op("pool", lambda e: e.tensor_tensor(out=ctmp, in0=hp[:, :, 0], in1=cwb[:, :, 0],
                                                            op=ALU.mult), rd=[hb, buf("p2c")], wr=[buf("ctmp")], n=44)
                    sc.op("pool", lambda e: e.tensor_tensor(out=corr[:, :, 0], in0=corr[:, :, 0], in1=ctmp, op=ALU.add),
                          rd=[buf("ctmp"), buf("corr")], wr=[buf("corr")], n=44)
                    sc.op("pool", lambda e: e.tensor_tensor(out=corr[:, :, 1], in0=hp[:, :, 1], in1=cwb[:, :, 0],
                                                            op=ALU.mult), rd=[hb, buf("p2c")], wr=[buf("corr")], n=44)
                    for k_ in range(2):
                        sc.op("pool", lambda e, k_=k_: e.tensor_tensor(out=corr[:, :, k_], in0=corr[:, :, k_],
                                                                      in1=cwb[:, :, 3], op=ALU.add),
                              rd=[buf("p2c")], wr=[buf("corr")], n=44)
                hcur = halo[:, G % 2, :, :]
                hcb = halo_b[G % 2]
                first_halo = [True]
                prev_fin = [None]
                for p in range(NPAIR):
                    n = G * NPAIR + p
                    ws = n % NWU
                    for half in range(2):
                        q_ = 2 * n + half
                        bk = 3 + (q_ % 4)
                        ts = q_ % 4
                        ch = p + NPAIR * half
                        ps = banks[bk]
                        for c in range(8):
                            sc.op("pe", lambda e, c=c, half=half, ps=ps, ws=ws: e.matmul(
                                ps, lhsT=wu[ws][:, c, half * 128:(half + 1) * 128], rhs=hT2[:, c, :],
                                start=(c == 0), stop=(c == 7)),
                                rd=[wu_b[ws], hT2_b], wr=[bank[bk]] if c == 0 else (), wp=[bank[bk]] if c else (),
                                deps=BAR[0] if (n == 0 and c == 0) else ())
                        if G == 0:
                            sc.op("act", lambda e, ps=ps, ch=ch, ts=ts: e.activation(
                                out=tA[ts], in_=ps, func=AF.Identity, scale=cwb[:, ch, 2:3], bias=cwb[:, ch, 3:4]),
                                rd=[bank[bk], buf("p2c")], wr=[tA_b[ts]], deps=BAR[0] if q_ < 4 else ())
                        else:
                            sc.op("act", lambda e, ps=ps, ch=ch, ts=ts: e.activation(
                                out=tA[ts][:, 2:512], in_=ps[:, 2:512], func=AF.Identity, scale=cwb[:, ch, 2:3],
                                bias=cwb[:, ch, 3:4]),
                                rd=[bank[bk], buf("p2c")], wr=[tA_b[ts]])
                            for k_ in range(2):
                                sc.op("act", lambda e, ps=ps, ch=ch, ts=ts, k_=k_: e.activation(
                                    out=tA[ts][:, k_:k_ + 1], in_=ps[:, k_:k_ + 1], func=AF.Identity,
                                    scale=cwb[:, ch, 2:3], bias=corr[:, ch, k_:k_ + 1]),
                                    rd=[bank[bk], buf("p2c"), buf("corr")], wp=[tA_b[ts]], n=1)
                        sc.op("act", lambda e, ps=ps, ch=ch: e.copy(out=hcur[:, ch, :], in_=ps[:, 510:512]),
                              rd=[bank[bk]], wr=[hcb] if first_halo[0] else (), wp=[hcb] if not first_halo[0] else (), n=2)
                        first_halo[0] = False
                        sc.op("dve", lambda e, ps=ps, ch=ch, ts=ts: e.scalar_tensor_tensor(
                            out=tA[ts][:, 1:512], in0=ps[:, 0:511], scalar=cwb[:, ch, 1:2], in1=tA[ts][:, 1:512],
                            op0=ALU.mult, op1=ALU.add),
                            rd=[bank[bk], buf("p2c")], wr=[tA_b[ts]])
                        sc.op("dve", lambda e, ps=ps, ch=ch, ts=ts: e.scalar_tensor_tensor(
                            out=tA[ts][:, 2:512], in0=ps[:, 0:510], scalar=cwb[:, ch, 0:1], in1=tA[ts][:, 2:512],
                            op0=ALU.mult, op1=ALU.add),
                            rd=[bank[bk], buf("p2c")], wr=[tA_b[ts]])
                    if n + NWU < NG * NPAIR:
                        load_wu(n + NWU)
                    if WDB["n"] < 14 and n >= 2:
                        c_ = WDB["n"]
                        dma("pool", "wdB", [(wdB[:, c_, :], w_down_v[:, 8 + c_, :])],
                            wr=[buf("wdB")] if c_ == 0 else (), wp=[buf("wdB")] if c_ else (), deps=BAR[0])
                        WDB["n"] += 1

                    def fin(n=n, p=p):
                        tg, tv = (2 * n) % 4, (2 * n + 1) % 4
                        sc.op("act", lambda e: e.activation(out=tA[tg], in_=tA[tg], func=AF.Silu),
                              rd=[tA_b[tg]], wr=[tA_b[tg]])
                        sc.op(TUNE["gmult"], lambda e: e.tensor_tensor(out=gbuf[:, p, :], in0=tA[tg], in1=tA[tv], op=ALU.mult),
                              rd=[tA_b[tg], tA_b[tv]], wr=[g_b[p]],
                              deps=BAR[0] if n == 0 else ())
                    if prev_fin[0] is not None:
                        prev_fin[0]()
                    prev_fin[0] = fin
                prev_fin[0]()

        drain(prep_gen(0))
        for G in range(NG):
            drain(pool_gen(G))
            if G + 1 < NG:
                drain(attn_gen(G))
                drain(prep_gen(G + 1))
            else:
                sc.new_segment()
                BAR[0] = []
                dma("sp", "c3", [(g_post, gains_d[1]), (g_fpre, gains_d[2]), (g_fpost, gains_d[3]), (cwb, cwb_d)],
                    wr=[buf("p2c")])
                fns0 = mk_sec1(0, False, early=True)
                drain(attn_gen(G))
                for name in ("Amm0", "Apost0", "Amm1", "Ba0", "Apost1", "Bb0", "Amm2", "Ba1", "Apost2", "Bb1", "Amm3",
                             "Ba2", "Apost3", "Bb2", "Ba3", "Bb3"):
                    fns0[name[:-1]](int(name[-1]))

        sc.new_segment()
        BAR[0] = []
        for n in range(NWU):
            load_wu(n)
        sc.op("pool", lambda e: e.memset(corr, 0.0), wr=[buf("corr")], deps=BAR[0], n=512)
        WDB = {"n": 0}
        if True:
            for G in range(NG):
                up_section(G)
                run_boundary(G, G + 1 if G + 1 < NG else None)

        sc.op("sp", lambda e: e.nop(), deps=outs)
        sc.emit(block)
    return nc


def host_prep(inputs):
    f = np.float32
    w_up = np.asarray(inputs["w_up"][0], f)
    w_up_r = np.empty((NPAIR, 128, 8, 256), f)
    wu4 = w_up.reshape(8, 128, 2 * DFF)
    for pr in range(NPAIR):
        w_up_r[pr, :, :, 0:128] = wu4[:, :, pr * 128:(pr + 1) * 128].transpose(1, 0, 2)
        w_up_r[pr, :, :, 128:256] = wu4[:, :, DFF + pr * 128:DFF + (pr + 1) * 128].transpose(1, 0, 2)
    gains = np.stack([np.broadcast_to(np.asarray(inputs[k][0], f), (128, D)) for k in
                      ("norm_mix_pre", "norm_mix_post", "norm_ffn_pre", "norm_ffn_post")]).astype(f)
    colp = np.concatenate([np.asarray(inputs["b_pool"][0], f).reshape(4, 128).T,
                           np.asarray(inputs["pool_scale"][0], f).reshape(4, 128).T], axis=1)
    cw = np.asarray(inputs["conv_w"][0], f)[:, 0, :]
    cb = np.asarray(inputs["conv_b"][0], f)
    cwb = np.concatenate([cw, cb[None]], axis=0).reshape(4, 44, 128).transpose(2, 1, 0)
    inv_freq = (10000.0 ** (-np.arange(0, DH, 2, dtype=f) / DH)).astype(f)
    ang = (np.arange(S, dtype=f)[:, None] * inv_freq[None, :]).astype(f)
    rope = np.concatenate([np.cos(ang), np.sin(ang)], axis=1).astype(f)
    rope = rope.reshape(NT, 128, 64).transpose(1, 0, 2)
    mk = np.zeros((3, 8, 8), f)
    for qb in range(8):
        for j in range(8):
            mk[0, qb, j] = 0.0 if j < qb else -1e30
            mk[1, qb, j] = -NEG if j < qb else 0.0
            mk[2, qb, j] = 0.0 if j == qb else NEG
    mk = np.broadcast_to(mk, (128, 3, 8, 8))
    blk = np.zeros((8, S), f)
    for j in range(8):
        blk[j, j * 256:(j + 1) * 256] = 1.0
    poolA = np.zeros((3, 4, 128, 128), f)
    tp = np.arange(128)[:, None]
    t = np.arange(128)[None, :]
    for g, w in enumerate((2, 4, 8, 16)):
        band = ((t - tp) >= 0) & ((t - tp) < w)
        poolA[0, g] = band / w - (t == tp)
        poolA[1, g] = ((t + 128 - tp) < w) / w
        cnt = np.minimum(t + 1, w)
        poolA[2, g] = band / cnt - (t == tp)
    poolA = poolA.transpose(2, 0, 1, 3)
    idtri = np.zeros((128, 2, 128), f)
    idtri[:, 0, :] = np.eye(128)
    idtri[:, 1, :] = np.where(tp > t, NEG, 0.0)
    c = lambda a: np.ascontiguousarray(a, dtype=f)
    shared = dict(
        w_in=c(inputs["w_in"][0]), w_out=c(inputs["w_out"][0]), w_up_r=c(w_up_r), w_down=c(inputs["w_down"][0]),
        w_pool=c(inputs["w_pool"][0]), gains=c(gains), colp=c(colp), cwb=c(cwb), rope=c(rope), mk=c(mk),
        blkind=c(blk), poolA=c(poolA), idtri=c(idtri))
    return shared


_NC_CACHE = {}


def kernel(**inputs):
    shared = host_prep(inputs)
    x = np.asarray(inputs["x"], np.float32)
    if "nc" not in _NC_CACHE:
        _NC_CACHE["nc"] = build()
    nc = _NC_CACHE["nc"]
    in_maps = [dict(shared, x=np.ascontiguousarray(x[b])) for b in range(8)]
    res = run_bass_kernel_spmd(nc, in_maps, core_ids=list(range(8)))
    return np.stack([r["out"] for r in res.results], axis=0).astype(np.float32)
```

```python
import numpy as np
from contextlib import ExitStack
import concourse.bass as bass
import concourse.mybir as mybir
from concourse.bass_utils import run_bass_kernel_spmd
from concourse.alu_op_type import AluOpType as ALU

F32 = mybir.dt.float32
BF16 = mybir.dt.bfloat16
I32 = mybir.dt.int32
AF = mybir.ActivationFunctionType
AX = mybir.AxisListType

S = 2048
D = 1024
NT = 16
NG = 4
H = 8
DH = 64
DFF = 2816
NPAIR = 22
NEG = -30000.0
EPS = 1e-6
SCALE = DH ** -0.5
COMPUTE = ("pe", "act", "dve", "pool")
TUNE = dict(r0="dve", r2="dve", r3="dve", ktevac="act", add1="dve", add3="dve", corr="dve", gmult="pool")


class Buf:
    def __init__(self):
        self.w = []
        self.r = []
        self.pw = []

    def rd(self):
        return list(self.w)

    def wr_new(self):
        deps = self.r + self.w
        self.pw = deps
        self.w = []
        self.r = []
        return list(deps)

    def wr_more(self):
        return list(self.pw)


class Sched:
    COST = dict(pe=lambda n: 0.008 + n / 2370.0, act=lambda n: 0.22 + n / 1400.0, dve=lambda n: 0.29 + n / 960.0,
                pool=lambda n: 0.36 + n / 560.0, sp=lambda n: 0.05)

    def __init__(self, nc, st):
        self.nc = nc
        self.st = st
        self.streams = {e: [] for e in COMPUTE + ("sp",)}
        self.sems = {e: st.enter_context(nc.semaphore("s_" + e)) for e in COMPUTE}
        self.dsem = {}
        self.seg = 0
        self.order = 0

    def new_segment(self):
        self.seg += 1

    def _collect(self, eng, rd, wr, wp, deps):
        ds = list(deps)
        for b in rd:
            ds += b.rd()
        for b in wr:
            ds += b.wr_new()
        for b in wp:
            ds += b.wr_more()
            if eng == "pe":
                ds += [h for h in b.w if h[0] == "pe"]
        return [d for d in ds if d is not None]

    def op(self, eng, fn, rd=(), wr=(), wp=(), deps=(), n=None):
        raw = self._collect(eng, rd, wr, wp, deps)
        idx = len(self.streams[eng])
        if n is None:
            n = 512
        self.streams[eng].append(dict(fn=fn, raw=raw, kind="op", sig=False, seq=0, eng=eng, idx=idx, seg=self.seg,
                                      cost=self.COST[eng](n), prog=self.order))
        self.order += 1
        h = (eng, idx)
        for b in rd:
            b.r.append(h)
        for b in list(wr) + list(wp):
            b.w.append(h)
        return h

    def dma(self, queue, fn, key, rd=(), wr=(), wp=(), deps=(), nbytes=524288):
        raw = self._collect(queue, rd, wr, wp, deps)
        if key not in self.dsem:
            self.dsem[key] = [self.st.enter_context(self.nc.semaphore("d_" + key)), 0]
        idx = len(self.streams[queue])
        ent = dict(fn=fn, raw=raw, kind="dma", key=key, sig=False, seq=0, eng=queue, idx=idx, seg=self.seg,
                   cost=(0.65 if queue == "pool" else 0.08), lat=2.0 + nbytes / 120000.0, prog=self.order)
        self.order += 1
        self.streams[queue].append(ent)
        h = ["dma", key, None, ent]
        for b in rd:
            b.r.append(h)
        for b in list(wr) + list(wp):
            b.w.append(h)
        return h

    @staticmethod
    def _ent_of(sched, d):
        return d[3] if d[0] == "dma" else sched.streams[d[0]][d[1]]

    def _schedule(self):
        import heapq
        nseg = self.seg + 1
        self.est = []
        new_streams = {e: [] for e in self.streams}
        t_base = 0.0
        for sg in range(nseg):
            ops = [o for e in self.streams for o in self.streams[e] if o["seg"] == sg]
            if not ops:
                continue
            ids = {id(o) for o in ops}
            users = {id(o): [] for o in ops}
            remaining = {}
            ready = {}
            for o in ops:
                cnt = 0
                for d in o["raw"]:
                    p = self._ent_of(self, d)
                    if id(p) in ids:
                        users[id(p)].append(o)
                        cnt += 1
                remaining[id(o)] = cnt
                ready[id(o)] = t_base
            heaps = {e: [] for e in self.streams}
            for o in ops:
                if remaining[id(o)] == 0:
                    heapq.heappush(heaps[o["eng"]], (ready[id(o)], o["prog"], id(o), o))
            free_at = {e: t_base for e in self.streams}
            done = 0
            tmax = t_base
            while done < len(ops):
                best = None
                for e, hp in heaps.items():
                    if hp:
                        st_ = max(free_at[e], hp[0][0])
                        if best is None or st_ < best[0] or (st_ == best[0] and hp[0][1] < best[2]):
                            best = (st_, e, hp[0][1])
                assert best is not None, "scheduler deadlock (dependency cycle)"
                st_, e, _ = best
                hp = heaps[e]
                cands = []
                while hp and hp[0][0] <= st_ + 1e-9:
                    cands.append(heapq.heappop(hp))
                cands.sort(key=lambda c: c[1])
                chosen = cands[0]
                for c in cands[1:]:
                    heapq.heappush(hp, c)
                o = chosen[3]
                fin_eng = st_ + o["cost"]
                free_at[e] = fin_eng
                fin = fin_eng + (o["lat"] if o["kind"] == "dma" else 0.0)
                o["t_fin"] = fin
                tmax = max(tmax, fin)
                new_streams[e].append(o)
                done += 1
                for u in users[id(o)]:
                    lat = 0.12 if (u["eng"] == e and o["kind"] != "dma") else 0.28
                    ready[id(u)] = max(ready[id(u)], fin + lat)
                    remaining[id(u)] -= 1
                    if remaining[id(u)] == 0:
                        heapq.heappush(heaps[u["eng"]], (ready[id(u)], u["prog"], id(u), u))
            self.est.append(round(tmax, 1))
            t_base = tmax
        for e in self.streams:
            assert len(new_streams[e]) == len(self.streams[e])
            self.streams[e] = new_streams[e]
            for pos, o in enumerate(new_streams[e]):
                o["pos"] = pos
        last_of_seg = {}
        for e in COMPUTE:
            for o in self.streams[e]:
                last_of_seg[(e, o["seg"])] = o["pos"]
        for e, stream in self.streams.items():
            prev_seg = None
            for o in stream:
                best = {}
                deps = []
                for d in o["raw"]:
                    if d[0] == "dma":
                        deps.append(d)
                        continue
                    p = self.streams_lookup(d)
                    if d[0] == "pe" and e == "pe":
                        continue
                    if d[0] not in best or best[d[0]] < p["pos"]:
                        best[d[0]] = p["pos"]
                if o["seg"] != prev_seg and o["seg"] > 0:
                    for e2 in COMPUTE:
                        sgs = [s_ for (ee, s_) in last_of_seg if ee == e2 and s_ < o["seg"]]
                        if sgs:
                            pos2 = last_of_seg[(e2, max(sgs))]
                            if not (e2 == "pe" and e == "pe"):
                                if e2 not in best or best[e2] < pos2:
                                    best[e2] = pos2
                prev_seg = o["seg"]
                for k, v in best.items():
                    deps.append((k, v))
                o["deps"] = deps

    def streams_lookup(self, d):
        return self._byid[(d[0], d[1])]

    def emit(self, block):
        self._byid = {}
        for e, stream in self.streams.items():
            for o in stream:
                self._byid[(e, o["idx"])] = o
        self._schedule()
        for eng, stream in self.streams.items():
            for o in stream:
                for d in o["deps"]:
                    if d[0] != "dma":
                        self.streams[d[0]][d[1]]["sig"] = True
        for eng, stream in self.streams.items():
            n = 0
            for o in stream:
                if o["kind"] == "op" and o["sig"]:
                    n += 1
                    o["seq"] = n
        keyq = {}
        for eng, stream in self.streams.items():
            for o in stream:
                if o["kind"] == "dma":
                    assert keyq.setdefault(o["key"], eng) == eng, "dma key used from two queues"
                    self.dsem[o["key"]][1] += 16 * o["ndma"]
                    o["target"] = self.dsem[o["key"]][1]

        def make(eng):
            def body(e):
                waited = {}
                for o in self.streams[eng]:
                    need = {}
                    for d in o["deps"]:
                        if d[0] == "dma":
                            ent = d[3]
                            sem, val = self.dsem[d[1]][0], ent["target"]
                            k = "d_" + d[1]
                        else:
                            src = self.streams[d[0]][d[1]]
                            sem, val = self.sems[d[0]], src["seq"]
                            k = d[0]
                        if k not in need or need[k][1] < val:
                            need[k] = (sem, val)
                    for k, (sem, val) in need.items():
                        if waited.get(k, 0) < val:
                            e.wait_ge(sem, val)
                            waited[k] = val
                    ins = o["fn"](e)
                    if o["kind"] == "dma":
                        sem = self.dsem[o["key"]][0]
                        for i in ins:
                            i.then_inc(sem, 16)
                    elif o["sig"]:
                        ins.then_inc(self.sems[eng], 1)
            return body

        block.sync(make("sp"))
        block.gpsimd(make("pool"))
        block.scalar(make("act"))
        block.vector(make("dve"))
        block.tensor(make("pe"))


def build(dbg=False, p1_only=False):
    nc = bass.Bass("TRN2", target_bir_lowering=False)

    def din(name, shape):
        return nc.dram_tensor(name, list(shape), F32, kind="ExternalInput").ap()

    x_d = din("x", [S, D])
    w_in_d = din("w_in", [D, 2 * D])
    w_out_d = din("w_out", [D, D])
    w_up_d = din("w_up_r", [NPAIR, 128, 8, 256])
    w_down_d = din("w_down", [DFF, D])
    w_pool_d = din("w_pool", [4, 128, 128])
    gains_d = din("gains", [4, 128, D])
    colp_d = din("colp", [128, 8])
    cwb_d = din("cwb", [128, 44, 4])
    rope_d = din("rope", [128, NT, 64])
    mk_d = din("mk", [128, 3, 8, 8])
    blk_d = din("blkind", [8, S])
    poolA_d = din("poolA", [128, 3, 4, 128])
    idtri_d = din("idtri", [128, 2, 128])
    out_d = nc.dram_tensor("out", [S, D], F32, kind="ExternalOutput").ap()

    with ExitStack() as st:
        def sb(name, shape, dt):
            return st.enter_context(nc.sbuf_tensor(name, list(shape), dt))

        yT = sb("yT", [128, 8, S], BF16)
        RA = sb("RA", [128, 8192], F32)
        RB = sb("RB", [128, 8192], F32)
        RC = sb("RC", [128, 8192], F32)
        RD = sb("RD", [128, 2, D], F32)
        RE = sb("RE", [128, 6144], F32)
        RS = sb("RS", [128, 11264], F32)
        CST = sb("CST", [128, 2, 128], BF16)
        SM = sb("SM", [128, 256], F32)

        w_in_sb = RA[:, :].bitcast(BF16).rearrange("p (c n) -> p c n", c=8)
        KT = RB[:, :].bitcast(BF16).rearrange("p (h n) -> p h n", h=8)
        VA = RC[:, :].bitcast(BF16).rearrange("p (t h n) -> p t h n", t=NT, h=8)
        hT = RE[:, 0:2048].bitcast(BF16).rearrange("p (c n) -> p c n", c=8)
        QTs = [RE[:, 2048 * (1 + i):2048 * (2 + i)].bitcast(BF16).rearrange("p (h n) -> p h n", h=8)
               for i in range(2)]
        w_out_sb = RA[:, 0:4096].bitcast(BF16).rearrange("p (c n) -> p c n", c=8)
        wdA = RA[:, 4096:8192].bitcast(BF16).rearrange("p (c n) -> p c n", c=8)
        wdB = RB[:, 0:7168].bitcast(BF16).rearrange("p (c n) -> p c n", c=14)
        gbuf = RC[:, 0:5632].bitcast(BF16).rearrange("p (c n) -> p c n", c=NPAIR)
        NWU = 3
        wu = [RC[:, 5632 + i * 1024: 5632 + (i + 1) * 1024].bitcast(BF16).rearrange("p (c n) -> p c n", c=8)
              for i in range(2)] + \
             [RE[:, 4096:5120].bitcast(BF16).rearrange("p (c n) -> p c n", c=8)]
        hT2 = RE[:, 0:2048].bitcast(BF16).rearrange("p (c n) -> p c n", c=8)
        xn2 = [RE[:, 2048 + i * 512: 2048 + (i + 1) * 512].bitcast(BF16) for i in range(2)]
        n1t = RE[:, 3072:4096]

        o = [0]

        def carve(nwords):
            a = o[0]
            o[0] += nwords
            assert o[0] <= 11264, o[0]
            return RS[:, a:a + nwords]

        g_pre = carve(1024)
        ropet = carve(NT * 64).rearrange("p (t n) -> p t n", t=NT)
        mk = carve(192).rearrange("p (a q j) -> p a q j", a=3, q=8)
        poolA = carve(768).bitcast(BF16).rearrange("p (a g n) -> p a g n", a=3, g=4)
        wpool = carve(256).bitcast(BF16).rearrange("p (g n) -> p g n", g=4)
        colp = carve(8)
        xn = [carve(512).bitcast(BF16) for _ in range(2)]
        rt1s = [carve(512).rearrange("p (h n) -> p h n", h=8) for _ in range(2)]
        rt2s = [carve(512).rearrange("p (h n) -> p h n", h=8) for _ in range(2)]
        qr = carve(256).bitcast(BF16).rearrange("p (h n) -> p h n", h=8)
        kr = carve(256).bitcast(BF16).rearrange("p (h n) -> p h n", h=8)
        uring = carve(5 * 256).bitcast(BF16).rearrange("p (t n) -> p t n", t=5)
        diffs = carve(256).bitcast(BF16)
        kmT = carve(32).bitcast(BF16).rearrange("p (h j) -> p h j", h=8)
        gm = carve(64).rearrange("p (h j) -> p h j", h=8)
        kmTf = carve(64).rearrange("p (h j) -> p h j", h=8)
        top8 = carve(64).rearrange("p (h j) -> p h j", h=8)
        selA = carve(64).rearrange("p (h j) -> p h j", h=8)
        MB = carve(32).bitcast(BF16).rearrange("p (h j) -> p h j", h=8)
        junk = carve(512).bitcast(BF16)
        off_live = o[0]
        PT = [carve(256).bitcast(BF16) for _ in range(3)]
        rinv = [carve(512) for _ in range(2)]
        o[0] = 0
        x1 = carve(5 * 1024).rearrange("p (t n) -> p t n", t=5)
        g_post = carve(1024)
        g_fpre = carve(1024)
        g_fpost = carve(1024)
        cwb = carve(176).rearrange("p (c k) -> p c k", c=44)
        halo = carve(176).rearrange("p (a c k) -> p a c k", a=2, c=44)
        corr = carve(88).rearrange("p (c k) -> p c k", c=44)
        ctmp = carve(44)
        assert o[0] <= off_live, (o[0], off_live)
        tA = [carve(512) for _ in range(4)]
        junk2 = RE[:, 5632:6144].bitcast(BF16)

        ident = CST[:, 0, :]
        tri = CST[:, 1, :]
        ssq = SM[:, 0:16]
        rstd = SM[:, 16:32]
        nt_a = SM[:, 32:48]
        nt_b = SM[:, 48:64]
        nt_c = SM[:, 64:80]
        sq1 = SM[:, 80:96]
        sq2 = SM[:, 96:112]
        sq3 = SM[:, 112:128]
        rs1 = SM[:, 128:144]
        rs2 = SM[:, 144:160]
        rs3 = SM[:, 160:176]

        pbig_t = st.enter_context(nc.psum_tensor("pbig", [128, 1024], F32))
        pT_t = st.enter_context(nc.psum_tensor("pT", [128, 512], F32))
        up_t = st.enter_context(nc.psum_tensor("pup", [128, 2048], F32))
        p7_t = st.enter_context(nc.psum_tensor("p7", [128, 512], F32))
        banks = [pbig_t[:, 0:512], pbig_t[:, 512:1024], pT_t[:, :]] + \
                [up_t[:, k * 512:(k + 1) * 512] for k in range(4)] + [p7_t[:, :]]
        acc = [pbig_t[:, :], up_t[:, 0:1024]]
        block = st.enter_context(nc.Block())
        sc = Sched(nc, st)

        def dma(queue, key, pairs, **kw):
            def fn(e, pairs=pairs):
                return [e.dma_start(out=o_, in_=i_) for (o_, i_) in pairs]
            h = sc.dma(queue, fn, key, **kw)
            h[3]["ndma"] = len(pairs)
            return h

        B = {}

        def buf(name):
            if name not in B:
                B[name] = Buf()
            return B[name]

        bank = [buf("bank%d" % i) for i in range(8)]
        accb = [[bank[0], bank[1]], [bank[3], bank[4]]]

        dma("sp", "c1", [(g_pre, gains_d[0]), (ropet, rope_d), (mk, mk_d), (colp, colp_d)],
            wr=[buf("g_pre"), buf("ropet"), buf("mk"), buf("colp")])
        dma("pool", "c2", [(CST[:, :, :], idtri_d), (poolA, poolA_d), (wpool, w_pool_d.rearrange("g i o -> i g o"))]
            + [(KT[64:72, h, :], blk_d) for h in range(H)],
            wr=[buf("cst"), buf("poolA"), buf("wpool"), buf("KTind")])
        w_in_v = w_in_d.rearrange("(c p) n -> p c n", p=128)
        dma("pool", "win", [(w_in_sb[:, c, :], w_in_v[:, c, :]) for c in range(8)], wr=[buf("w_in")])

        xslot = [buf("xs%d" % i) for i in range(2)]

        def load_x(tt, deps=()):
            s = tt % 2
            return dma("sp", "x%d" % s, [(RD[:, s, :], x_d[tt * 128:(tt + 1) * 128, :])], wr=[xslot[s]], deps=deps)

        sc.op("pool", lambda e: e.memset(VA[:, :, :, 64:128], 1.0), wr=[buf("VAones")], n=512)
        sc.op("pool", lambda e: e.memset(kmT, 0.0), wr=[buf("kmT")], n=64)

        def rsqrt_col(ssq_col, ssq_buf, y, name, newton_eng="dve", extra_rd=()):
            v, hv, t = nt_a[:, 0:1], nt_b[:, 0:1], nt_c[:, 0:1]
            col = int(name.split("_")[-1])
            v, hv, t = nt_a[:, col:col + 1], nt_b[:, col:col + 1], nt_c[:, col:col + 1]
            bv, bh, bt, by = buf(name + "_v"), buf(name + "_h"), buf(name + "_t"), buf(name)
            sc.op("dve", lambda e: e.tensor_scalar(out=v, in0=ssq_col, scalar1=1.0 / D, scalar2=EPS,
                                                   op0=ALU.mult, op1=ALU.add),
                  rd=[ssq_buf] + list(extra_rd), wr=[bv], n=1)
            sc.op("dve", lambda e: e.tensor_scalar(out=y.bitcast(I32), in0=v.bitcast(I32), scalar1=-0.5,
                                                   scalar2=1597463007.0, op0=ALU.mult, op1=ALU.add),
                  rd=[bv], wr=[by], n=1)
            ne = newton_eng
            sc.op(ne, lambda e: e.tensor_scalar(out=hv, in0=v, scalar1=-0.5, scalar2=None, op0=ALU.mult),
                  rd=[bv], wr=[bh], n=1)
            for _ in range(3):
                if ne == "dve":
                    sc.op("dve", lambda e: e.scalar_tensor_tensor(out=t, in0=hv, scalar=y, in1=y,
                                                                 op0=ALU.mult, op1=ALU.mult),
                          rd=[bh, by], wr=[bt], n=1)
                else:
                    sc.op(ne, lambda e: e.tensor_tensor(out=t, in0=hv, in1=y, op=ALU.mult), rd=[bh, by], wr=[bt], n=1)
                    sc.op(ne, lambda e: e.tensor_tensor(out=t, in0=t, in1=y, op=ALU.mult), rd=[bt, by], wr=[bt], n=1)
                sc.op(ne, lambda e: e.tensor_scalar(out=y, in0=t, scalar1=1.5, scalar2=y, op0=ALU.add, op1=ALU.mult),
                      rd=[bt, by], wr=[by], n=1)
            return by

        load_x(0)
        load_x(1)
        bT = bank[2]
        bT2 = bank[3]
        psT = banks[2].bitcast(BF16)
        psT2 = banks[3].bitcast(BF16)
        hT_b = [buf("hT%d" % i) for i in range(4)]
        QT_b = [buf("QT0"), buf("QT1")]
        KT_b = [buf("KT%d" % g) for g in range(NG)]
        VA_b = [buf("VA%d" % t) for t in range(NT)]
        ur_b = [buf("ur%d" % i) for i in range(5)]
        yT_b = [buf("yT%d" % g) for g in range(NG)]

        def prep_gen(G):
            QT = QTs[G % 2]
            QTb = QT_b[G % 2]

            def S1(i):
                tt = 4 * G + i
                s_ = tt % 2
                sc.op("act", lambda e: e.activation(out=junk, in_=RD[:, s_, :], func=AF.Square,
                                                    accum_out=ssq[:, tt:tt + 1]),
                      rd=[xslot[s_]], wr=[buf("junk"), buf("ssq%d" % tt)], n=1024)

            def S2(i):
                tt = 4 * G + i
                s_ = tt % 2
                rb = rsqrt_col(ssq[:, tt:tt + 1], buf("ssq%d" % tt), rstd[:, tt:tt + 1], "r0_%d" % tt, TUNE["r0"])
                sc.op("dve", lambda e: e.scalar_tensor_tensor(out=xn[tt % 2], in0=RD[:, s_, :],
                                                             scalar=rstd[:, tt:tt + 1], in1=g_pre,
                                                             op0=ALU.mult, op1=ALU.mult),
                      rd=[xslot[s_], rb, buf("g_pre")], wr=[buf("xn%d" % (tt % 2))], n=1024)
                if tt + 2 < NT:
                    load_x(tt + 2)

            def S3(i):
                tt = 4 * G + i
                for c in range(8):
                    sc.op("pe", lambda e, c=c: e.transpose(out=psT[:, c * 128:(c + 1) * 128],
                                                          in_=xn[tt % 2][:, c * 128:(c + 1) * 128], identity=ident),
                          rd=[buf("xn%d" % (tt % 2)), buf("cst")], wr=[bT] if c == 0 else (), wp=[bT] if c else (), n=128)

            def S4(i):
                sc.op("act", lambda e: e.copy(out=hT[:, :, i * 128:(i + 1) * 128],
                                             in_=psT.rearrange("p (c n) -> p c n", c=8)),
                      rd=[bT], wr=[hT_b[i]], n=1024)

            for step in ([(S1, 0), (S1, 1)], [(S2, 0)], [(S3, 0), (S2, 1)], [(S4, 0), (S1, 2)],
                         [(S3, 1), (S2, 2)], [(S4, 1), (S1, 3)], [(S3, 2), (S2, 3)], [(S4, 2)], [(S3, 3)], [(S4, 3)]):
                for fn, i in step:
                    fn(i)
                yield

            def proj(i, j, pb):
                for c in range(8):
                    sc.op("pe", lambda e, c=c: e.matmul(
                        banks[pb], lhsT=hT[:, c, i * 128:(i + 1) * 128],
                        rhs=w_in_sb[:, c, j * 512:(j + 1) * 512], start=(c == 0), stop=(c == 7)),
                        rd=[hT_b[i], buf("w_in")], wr=[bank[pb]] if c == 0 else (), wp=[bank[pb]] if c else ())

            def rope_dve(i, j):
                tt = 4 * G + i
                pb = j
                rt1, rt2 = rt1s[j], rt2s[j]
                psv = banks[pb].rearrange("p (h n) -> p h n", h=8)
                cosb = ropet[:, tt, 0:32].unsqueeze(1).to_broadcast([128, 8, 32])
                sinb = ropet[:, tt, 32:64].unsqueeze(1).to_broadcast([128, 8, 32])
                b1_, b2_ = buf("rt1_%d" % j), buf("rt2_%d" % j)
                sc.op("dve", lambda e: e.tensor_tensor(out=rt1[:, :, 0:32], in0=psv[:, :, 0:32], in1=cosb, op=ALU.mult),
                      rd=[bank[pb], buf("ropet")], wr=[b1_])
                sc.op("dve", lambda e: e.tensor_tensor(out=rt1[:, :, 32:64], in0=psv[:, :, 32:64], in1=cosb, op=ALU.mult),
                      rd=[bank[pb], buf("ropet")], wp=[b1_])
                sc.op("dve", lambda e: e.tensor_tensor(out=rt2[:, :, 0:32], in0=psv[:, :, 32:64], in1=sinb, op=ALU.mult),
                      rd=[bank[pb], buf("ropet")], wr=[b2_])
                sc.op("dve", lambda e: e.tensor_tensor(out=rt2[:, :, 32:64], in0=psv[:, :, 0:32], in1=sinb, op=ALU.mult),
                      rd=[bank[pb], buf("ropet")], wp=[b2_])

            def rope_pool(i, j):
                rt1, rt2 = rt1s[j], rt2s[j]
                b1_, b2_ = buf("rt1_%d" % j), buf("rt2_%d" % j)
                dst = qr if j == 0 else kr
                dstb = buf("qr") if j == 0 else buf("kr")
                sc.op("pool", lambda e: e.tensor_tensor(out=dst[:, :, 0:32], in0=rt1[:, :, 0:32], in1=rt2[:, :, 0:32],
                                                        op=ALU.subtract), rd=[b1_, b2_], wr=[dstb])
                sc.op("pool", lambda e: e.tensor_tensor(out=dst[:, :, 32:64], in0=rt1[:, :, 32:64], in1=rt2[:, :, 32:64],
                                                        op=ALU.add), rd=[b1_, b2_], wp=[dstb])

            def b1(i):
                proj(i, 0, 0)
                proj(i, 1, 1)

            def b2(i):
                rope_dve(i, 0)
                rope_dve(i, 1)

            def b3(i):
                rope_pool(i, 0)
                rope_pool(i, 1)
                proj(i, 2, 0)
                proj(i, 3, 1)

            def b4(i):
                tt = 4 * G + i
                psv = banks[0].rearrange("p (h n) -> p h n", h=8)
                sc.op("act", lambda e: e.copy(out=VA[:, tt, :, 0:64], in_=psv),
                      rd=[bank[0]], wr=[VA_b[tt]], deps=buf("VAones").rd())
                sc.op("act", lambda e: e.copy(out=uring[:, tt % 5, :], in_=banks[1]),
                      rd=[bank[1]], wr=[ur_b[tt % 5]])
                for h in range(H):
                    sc.op("pe", lambda e, h=h: e.transpose(out=psT[0:64, h * 128:(h + 1) * 128], in_=qr[:, h, :],
                                                          identity=ident),
                          rd=[buf("qr"), buf("cst")], wr=[bT] if h == 0 else (), wp=[bT] if h else (), n=128)
                for h in range(H):
                    sc.op("pe", lambda e, h=h: e.transpose(out=psT2[0:64, h * 128:(h + 1) * 128], in_=kr[:, h, :],
                                                          identity=ident),
                          rd=[buf("kr"), buf("cst")], wr=[bT2] if h == 0 else (), wp=[bT2] if h else (), n=128)

            def b5(i):
                tt = 4 * G + i
                sc.op("act", lambda e: e.copy(
                    out=QT[0:64, :, i * 128:(i + 1) * 128], in_=psT[0:64, :].rearrange("p (h n) -> p h n", h=8)),
                    rd=[bT], wr=[QTb] if i == 0 else (), wp=[QTb] if i else (), n=1024)
                sc.op(TUNE["ktevac"], lambda e: (e.tensor_copy if TUNE["ktevac"] == "dve" else e.copy)(
                    out=KT[0:64, :, tt * 128:(tt + 1) * 128], in_=psT2[0:64, :].rearrange("p (h n) -> p h n", h=8)),
                    rd=[bT2], wr=[KT_b[G]] if i == 0 else (), wp=[KT_b[G]] if i else (), n=1024)
                if tt % 2 == 1:
                    jb = tt // 2
                    sc.op("dve", lambda e: e.tensor_reduce(
                        out=kmTf[0:64, :, jb:jb + 1], in_=KT[0:64, :, jb * 256:(jb + 1) * 256],
                        axis=AX.X, op=ALU.add), rd=[KT_b[G]], wr=[buf("kmTf")], n=64)
                    sc.op("dve", lambda e: e.tensor_copy(out=kmT[0:64, :, jb:jb + 1], in_=kmTf[0:64, :, jb:jb + 1]),
                          rd=[buf("kmTf")], wp=[buf("kmT")], n=64)

            def b6(i):
                for h in range(H):
                    sc.op("pe", lambda e, h=h: e.matmul(
                        banks[3][:, h * 8:(h + 1) * 8], lhsT=QT[0:64, h, i * 128:(i + 1) * 128], rhs=kmT[0:64, h, :],
                        start=True, stop=True),
                        rd=[QTb, buf("kmT")], wr=[bT2] if h == 0 else (), wp=[bT2] if h else (), n=8)

            def b7(i):
                qb = (4 * G + i) // 2
                pastneg = mk[:, 0, qb, :].unsqueeze(1).to_broadcast([128, 8, 8])
                pastP = mk[:, 1, qb, :].unsqueeze(1).to_broadcast([128, 8, 8])
                Cc = mk[:, 2, qb, :].unsqueeze(1).to_broadcast([128, 8, 8])
                sc.op("dve", lambda e: e.tensor_tensor(
                    out=gm, in0=banks[3][:, 0:64].rearrange("p (h j) -> p h j", h=8), in1=pastneg, op=ALU.add),
                    rd=[bT2, buf("mk")], wr=[buf("gm")], n=64)
                for h in range(H):
                    sc.op("dve", lambda e, h=h: e.max(out=top8[:, h, :], in_=gm[:, h, :]),
                          rd=[buf("gm")], wr=[buf("top8")] if h == 0 else (), wp=[buf("top8")] if h else (), n=8)
                sc.op("dve", lambda e: e.tensor_tensor(out=selA, in0=gm, in1=top8[:, :, 2:3].to_broadcast([128, 8, 8]),
                                                       op=ALU.is_ge),
                      rd=[buf("gm"), buf("top8")], wr=[buf("selA")], n=64)
                sc.op("dve", lambda e: e.tensor_tensor(out=gm, in0=selA, in1=pastP, op=ALU.mult),
                      rd=[buf("selA"), buf("mk")], wr=[buf("gm")], n=64)
                sc.op("dve", lambda e: e.tensor_tensor(out=MB, in0=gm, in1=Cc, op=ALU.add),
                      rd=[buf("gm"), buf("mk")], wr=[buf("MB")], n=64)

            def b8(i):
                for h in range(H):
                    sc.op("pe", lambda e, h=h: e.transpose(out=psT[64:72, h * 128:(h + 1) * 128], in_=MB[:, h, :],
                                                          identity=ident),
                          rd=[buf("MB"), buf("cst")], wr=[bT] if h == 0 else (), wp=[bT] if h else (), n=128)

            def b9(i):
                sc.op("act", lambda e: e.copy(out=QT[64:72, :, i * 128:(i + 1) * 128],
                                             in_=psT[64:72, :].rearrange("p (h n) -> p h n", h=8)),
                      rd=[bT], wp=[QTb], n=1024)

            stages = [b1, b2, b3, b4, b5, b6, b7, b8, b9]
            SKEW = 5
            nsteps = len(stages) + 3 * SKEW
            for t in range(nsteps):
                for i in range(4):
                    k = t - SKEW * i
                    if 0 <= k < len(stages):
                        stages[k](i)
                yield
            if G == NG - 1:
                w_out_v = w_out_d.rearrange("(c p) n -> p c n", p=128)
                w_down_v = w_down_d.rearrange("(c p) n -> p c n", p=128)
                dma("pool", "wout", [(w_out_sb[:, c, :], w_out_v[:, c, :]) for c in range(8)], wr=[buf("w_in")])
                dma("pool", "wdA", [(wdA[:, c, :], w_down_v[:, c, :]) for c in range(8)], wp=[buf("w_in")])

        def attn_gen(G):
            QT = QTs[G % 2]
            QTb = QT_b[G % 2]
            nkt = 4 * G + 4
            its = [(h, kt) for h in range(H) for kt in range(nkt)]
            N = len(its)

            def emit_qk(n):
                h, kt = its[n]
                nd = kt - 4 * G
                c0 = max(0, nd) * 128
                sslot = 5 + (n % 2)
                ps = banks[sslot]
                sc.op("pe", lambda e: e.matmul(
                    ps[:, c0:512], lhsT=KT[0:72, h, kt * 128:(kt + 1) * 128], rhs=QT[0:72, h, c0:512],
                    start=True, stop=(nd < 0)),
                    rd=[KT_b[kt // 4], QTb, buf("KTind")], wr=[bank[sslot]])
                if nd >= 0:
                    sc.op("pe", lambda e: e.matmul(ps[:, c0:c0 + 128], lhsT=ident, rhs=tri, start=False, stop=True),
                          rd=[buf("cst")], wp=[bank[sslot]], n=128)

            pending = []
            if G == NG - 1:
                for r_ in range(12):
                    sc.op("pe", lambda e: e.matmul(banks[4], lhsT=ident, rhs=VA[:, 0, 0:4, :].rearrange("p a b -> p (a b)"), start=True, stop=True),
                          rd=[QTb, buf("cst"), VA_b[0], buf("VAones")], wr=[bank[4]] if r_ == 0 else (), wp=[bank[4]] if r_ else ())
            emit_qk(0)
            if N > 1:
                emit_qk(1)
            for n in range(N):
                h, kt = its[n]
                nd = kt - 4 * G
                c0 = max(0, nd) * 128
                sslot = 5 + (n % 2)
                ps = banks[sslot]
                pslot = n % 3
                pob = 7 if h % 2 == 0 else 4
                po = banks[pob]
                sc.op("act", lambda e, c0=c0, ps=ps, pslot=pslot: e.activation(
                    out=PT[pslot][:, c0:512], in_=ps[:, c0:512], func=AF.Exp, scale=SCALE),
                    rd=[bank[sslot]], wr=[buf("PT%d" % pslot)])
                sc.op("pe", lambda e, h=h, kt=kt, c0=c0, pslot=pslot, po=po: e.matmul(
                    po[:, c0:512], lhsT=VA[:, kt, h, :], rhs=PT[pslot][:, c0:512],
                    start=(kt == 0), stop=(kt == nkt - 1)),
                    rd=[buf("PT%d" % pslot), VA_b[kt], buf("VAones")],
                    wr=[bank[pob]] if kt == 0 else (), wp=[bank[pob]] if kt else ())
                if n + 2 < N:
                    emit_qk(n + 2)
                if kt == nkt - 1:
                    def norm(h=h, po=po, pob=pob):
                        ri = rinv[h % 2]
                        rib = buf("rinv%d" % (h % 2))
                        sc.op("dve", lambda e: e.reciprocal(out=ri[0:64, :], in_=po[64:128, :]),
                              rd=[bank[pob]], wr=[rib])
                        p0 = (h % 2) * 64
                        sc.op("dve", lambda e: e.tensor_tensor(
                            out=yT[p0:p0 + 64, h // 2, G * 512:(G + 1) * 512], in0=po[0:64, :], in1=ri[0:64, :],
                            op=ALU.mult),
                            rd=[bank[pob], rib], wr=[yT_b[G]] if h == 0 else (), wp=[yT_b[G]] if h else ())
                    pending.append((n + 2, norm))
                while pending and (pending[0][0] <= n or n == N - 1):
                    pending.pop(0)[1]()
                yield
        def pool_gen(G):
            for g in range(4):
                for i in range(4):
                    tt = 4 * G + i
                    a_idx = 2 if tt == 0 else 0
                    sc.op("pe", lambda e, g=g, i=i, tt=tt, a_idx=a_idx: e.matmul(
                        banks[0][:, i * 128:(i + 1) * 128], lhsT=uring[:, tt % 5, g * 128:(g + 1) * 128],
                        rhs=poolA[:, a_idx, g, :], start=True, stop=(tt == 0)),
                        rd=[ur_b[tt % 5], buf("poolA")], wr=[bank[0]] if i == 0 else (), wp=[bank[0]] if i else (), n=128)
                    if tt > 0:
                        sc.op("pe", lambda e, g=g, i=i, tt=tt: e.matmul(
                            banks[0][:, i * 128:(i + 1) * 128], lhsT=uring[:, (tt - 1) % 5, g * 128:(g + 1) * 128],
                            rhs=poolA[:, 1, g, :], start=False, stop=True),
                            rd=[ur_b[(tt - 1) % 5], buf("poolA")], wp=[bank[0]], n=128)
                sc.op("act", lambda e: e.copy(out=diffs, in_=banks[0]), rd=[bank[0]], wr=[buf("diffs")])
                sc.op("pe", lambda e, g=g: e.matmul(banks[1], lhsT=wpool[:, g, :], rhs=diffs, start=True, stop=True),
                      rd=[buf("diffs"), buf("wpool")], wr=[bank[1]])
                sc.op("dve", lambda e, g=g: e.tensor_scalar(
                    out=yT[:, 4 + g, G * 512:(G + 1) * 512], in0=banks[1], scalar1=colp[:, g:g + 1],
                    scalar2=colp[:, 4 + g:5 + g], op0=ALU.add, op1=ALU.mult),
                    rd=[bank[1], buf("colp")], wp=[yT_b[G]])
                yield

        def drain(gen):
            n = 0
            for _ in gen:
                n += 1
            return n

        def count(genf, G):
            return None

        outs = []
        BAR = [None]
        if dbg:
            ydbg = nc.dram_tensor("dbg_yT", [1024, S], F32, kind="ExternalOutput").ap()
            for c in range(8):
                for half in range(4):
                    sc.op("dve", lambda e, c=c, half=half: e.tensor_copy(out=RD[:, 0, 0:512],
                                                                        in_=yT[:, c, half * 512:(half + 1) * 512]),
                          rd=yT_b, wr=[buf("dbgt")])
                    outs.append(dma("sp", "dbg", [(ydbg[c * 128:(c + 1) * 128, half * 512:(half + 1) * 512],
                                                   RD[:, 0, 0:512])], rd=[buf("dbgt")]))
        if True:
            w_down_v = w_down_d.rearrange("(c p) n -> p c n", p=128)
            wu_b = [buf("wu%d" % i) for i in range(NWU)]

            def load_wu(n):
                pr = n % NPAIR
                return dma("pool", "wu%d" % (n % NWU), [(wu[n % NWU], w_up_d[pr])], wr=[wu_b[n % NWU]],
                           deps=BAR[0] if n < NWU else ())
            x1_b = [buf("x1_%d" % i) for i in range(5)]
            hT2_b = buf("hT2")
            g_b = [buf("gbuf%d" % p_) for p_ in range(NPAIR)]
            tA_b = [buf("tA%d" % i) for i in range(4)]
            halo_b = [buf("halo0"), buf("halo1")]
            n1a = n1t
            n1b = RE[:, 5120:6144]
            acc3 = [pbig_t[:, :], up_t[:, 0:1024], up_t[:, 1024:2048]]
            acc3b = [[bank[0], bank[1]], [bank[3], bank[4]], [bank[5], bank[6]]]

            def mk_sec1(G, with_d, early=False):
                def aidx(i):
                    if early:
                        return 0
                    return 1 if with_d else (1, 2)[i % 2]

                def Amm(i):
                    tt = 4 * G + i
                    pacc, pb_ = acc3[aidx(i)], acc3b[aidx(i)]
                    load_x(tt, deps=BAR[0] if (tt < 2 or early) else ())
                    for dh in range(2):
                        for c in range(8):
                            sc.op("pe", lambda e, c=c, dh=dh: e.matmul(
                                pacc[:, dh * 512:(dh + 1) * 512], lhsT=yT[:, c, tt * 128:(tt + 1) * 128],
                                rhs=w_out_sb[:, c, dh * 512:(dh + 1) * 512], start=(c == 0), stop=(c == 7)),
                                rd=[yT_b[G], buf("w_in")], wr=[pb_[dh]] if c == 0 else (), wp=[pb_[dh]] if c else (),
                                deps=BAR[0] if ((tt == 0 or early) and c == 0) else ())

                def Apost(i, part=None):
                    tt = 4 * G + i
                    s_ = tt % 2
                    pacc, pb_ = acc3[aidx(i)], acc3b[aidx(i)]
                    xb = buf("xn2_%d" % (tt % 2))
                    if part in (None, 0):
                        sc.op("act", lambda e: e.activation(out=xn2[tt % 2], in_=pacc, func=AF.Square,
                                                            accum_out=sq1[:, tt:tt + 1]),
                              rd=pb_, wr=[xb, buf("sq1_%d" % tt)], deps=BAR[0] if (tt == 0 or early) else (), n=1024)
                    if part in (None, 1):
                        rsqrt_col(sq1[:, tt:tt + 1], buf("sq1_%d" % tt), rs1[:, tt:tt + 1], "r1_%d" % tt, "pool")
                    if part in (None, 2):
                        rb1 = buf("r1_%d" % tt)
                        sc.op("dve", lambda e: e.scalar_tensor_tensor(
                            out=n1a, in0=pacc, scalar=rs1[:, tt:tt + 1], in1=g_post, op0=ALU.mult, op1=ALU.mult),
                            rd=pb_ + [rb1, buf("p2c")], wr=[buf("n1a")], n=1024)
                        sc.op(TUNE["add1"], lambda e: e.tensor_tensor(out=x1[:, tt % 5, :], in0=n1a, in1=RD[:, s_, :], op=ALU.add),
                              rd=[buf("n1a"), xslot[s_]], wr=[x1_b[tt % 5]], deps=BAR[0] if (tt == 0 or early) else (), n=1024)

                def Ba(i, part=None):
                    tt = 4 * G + i
                    xb = buf("xn2_%d" % (tt % 2))
                    if part in (None, 0):
                        sc.op("act", lambda e: e.activation(out=xn2[tt % 2], in_=x1[:, tt % 5, :], func=AF.Square,
                                                            accum_out=sq2[:, tt:tt + 1]),
                              rd=[x1_b[tt % 5]], wr=[xb, buf("sq2_%d" % tt)], n=1024)
                    if part in (None, 1):
                        rsqrt_col(sq2[:, tt:tt + 1], buf("sq2_%d" % tt), rs2[:, tt:tt + 1], "r2_%d" % tt, TUNE["r2"],
                                  [buf("r1_%d" % tt)])
                    if part in (None, 2):
                        rb2 = buf("r2_%d" % tt)
                        sc.op("dve", lambda e: e.scalar_tensor_tensor(out=xn2[tt % 2], in0=x1[:, tt % 5, :],
                                                                     scalar=rs2[:, tt:tt + 1], in1=g_fpre,
                                                                     op0=ALU.mult, op1=ALU.mult),
                              rd=[x1_b[tt % 5], rb2, buf("p2c")], wr=[xb], n=1024)

                def Bb(i, part=None):
                    tt = 4 * G + i
                    xb = buf("xn2_%d" % (tt % 2))
                    if part in (None, 0):
                        for c in range(8):
                            sc.op("pe", lambda e, c=c: e.transpose(out=psT[:, c * 128:(c + 1) * 128],
                                                                  in_=xn2[tt % 2][:, c * 128:(c + 1) * 128], identity=ident),
                                  rd=[xb, buf("cst")], wr=[bT] if c == 0 else (), wp=[bT] if c else (), n=128)
                    if part in (None, 1):
                        sc.op("act", lambda e: e.copy(out=hT2[:, :, i * 128:(i + 1) * 128],
                                                     in_=psT.rearrange("p (c n) -> p c n", c=8)),
                              rd=[bT], wr=[hT2_b] if i == 0 else (), wp=[hT2_b] if i else (), n=1024)
                return dict(Amm=Amm, Apost=Apost, Ba=Ba, Bb=Bb)

            def mk_down(G):
                def didx(i):
                    return (0, 2)[i % 2]

                def Dmm(i):
                    pacc, pb_ = acc3[didx(i)], acc3b[didx(i)]
                    for dh in range(2):
                        for c in range(NPAIR):
                            wsrc = wdA[:, c, dh * 512:(dh + 1) * 512] if c < 8 else wdB[:, c - 8, dh * 512:(dh + 1) * 512]
                            sc.op("pe", lambda e, c=c, dh=dh, wsrc=wsrc: e.matmul(
                                pacc[:, dh * 512:(dh + 1) * 512], lhsT=gbuf[:, c, i * 128:(i + 1) * 128], rhs=wsrc,
                                start=(c == 0), stop=(c == NPAIR - 1)),
                                rd=[g_b[c], buf("w_in"), buf("wdB")], wr=[pb_[dh]] if c == 0 else (), wp=[pb_[dh]] if c else ())

                def Dpost(i):
                    tt = 4 * G + i
                    pacc, pb_ = acc3[didx(i)], acc3b[didx(i)]
                    sc.op("act", lambda e: e.activation(out=n1b, in_=pacc, func=AF.Square,
                                                        accum_out=sq3[:, tt:tt + 1]),
                          rd=pb_, wr=[buf("n1b"), buf("sq3_%d" % tt)], n=1024)
                    rb3 = rsqrt_col(sq3[:, tt:tt + 1], buf("sq3_%d" % tt), rs3[:, tt:tt + 1], "r3_%d" % tt, TUNE["r3"],
                                    [buf("r2_%d" % tt)])
                    sc.op("dve", lambda e: e.scalar_tensor_tensor(
                        out=n1b, in0=pacc, scalar=rs3[:, tt:tt + 1], in1=g_fpost, op0=ALU.mult, op1=ALU.mult),
                        rd=pb_ + [rb3, buf("p2c")], wr=[buf("n1b")], n=1024)
                    sc.op(TUNE["add3"], lambda e: e.tensor_tensor(out=x1[:, tt % 5, :], in0=n1b, in1=x1[:, tt % 5, :], op=ALU.add),
                          rd=[buf("n1b")], wr=[x1_b[tt % 5]], n=1024)
                    outs.append(dma("sp", "out%d" % (tt % 5), [(out_d[tt * 128:(tt + 1) * 128, :], x1[:, tt % 5, :])],
                                    rd=[x1_b[tt % 5]]))
                return dict(Dmm=Dmm, Dpost=Dpost)

            ORDER = ["Amm0", "Dmm0", "Apost0", "Amm1", "Dmm1", "Dpost0", "Ba0", "Apost1", "Amm2", "Bb0", "Dmm2", "Dpost1",
                     "Ba1", "Apost2", "Amm3", "Bb1", "Dmm3", "Dpost2", "Ba2", "Apost3", "Bb2", "Ba3", "Dpost3", "Bb3"]

            def run_boundary(Gd, Ga):
                fns = {}
                if Gd is not None:
                    fns.update(mk_down(Gd))
                if Ga is not None:
                    fns.update(mk_sec1(Ga, Gd is not None))
                for name in ORDER:
                    key, i = name[:-1], int(name[-1])
                    if key in fns:
                        fns[key](i)

            def up_section(G):
                if G > 0:
                    hp = halo[:, (G - 1) % 2, :, :]
                    hb = halo_b[(G - 1) % 2]
                    sc.op("pool", lambda e: e.tensor_tensor(out=corr[:, :, 0], in0=hp[:, :, 1], in1=cwb[:, :, 1],
                                                            op=ALU.mult), rd=[hb, buf("p2c")], wr=[buf("corr")], n=44)
                    sc.op("pool", lambda e: e.tensor_tensor(out=ctmp, in0=hp[:, :, 0], in1=cwb[:, :, 0],
                                                            op=ALU.mult), rd=[hb, buf("p2c")], wr=[buf("ctmp")], n=44)
                    sc.op("pool", lambda e: e.tensor_tensor(out=corr[:, :, 0], in0=corr[:, :, 0], in1=ctmp, op=ALU.add),
                          rd=[buf("ctmp"), buf("corr")], wr=[buf("corr")], n=44)
                    sc.op("pool", lambda e: e.tensor_tensor(out=corr[:, :, 1], in0=hp[:, :, 1], in1=cwb[:, :, 0],
                                                            op=ALU.mult), rd=[hb, buf("p2c")], wr=[buf("corr")], n=44)
                    for k_ in range(2):
                        sc.op("pool", lambda e, k_=k_: e.tensor_tensor(out=corr[:, :, k_], in0=corr[:, :, k_],
                                                                      in1=cwb[:, :, 3], op=ALU.add),
                              rd=[buf("p2c")], wr=[buf("corr")], n=44)
                hcur = halo[:, G % 2, :, :]
                hcb = halo_b[G % 2]
                first_halo = [True]
                prev_fin = [None]
                for p in range(NPAIR):
                    n = G * NPAIR + p
                    ws = n % NWU
                    for half in range(2):
                        q_ = 2 * n + half
                        bk = 3 + (q_ % 4)
                        ts = q_ % 4
                        ch = p + NPAIR * half
                        ps = banks[bk]
                        for c in range(8):
                            sc.op("pe", lambda e, c=c, half=half, ps=ps, ws=ws: e.matmul(
                                ps, lhsT=wu[ws][:, c, half * 128:(half + 1) * 128], rhs=hT2[:, c, :],
                                start=(c == 0), stop=(c == 7)),
                                rd=[wu_b[ws], hT2_b], wr=[bank[bk]] if c == 0 else (), wp=[bank[bk]] if c else (),
                                deps=BAR[0] if (n == 0 and c == 0) else ())
                        if G == 0:
                            sc.op("act", lambda e, ps=ps, ch=ch, ts=ts: e.activation(
                                out=tA[ts], in_=ps, func=AF.Identity, scale=cwb[:, ch, 2:3], bias=cwb[:, ch, 3:4]),
                                rd=[bank[bk], buf("p2c")], wr=[tA_b[ts]], deps=BAR[0] if q_ < 4 else ())
                        else:
                            sc.op("act", lambda e, ps=ps, ch=ch, ts=ts: e.activation(
                                out=tA[ts][:, 2:512], in_=ps[:, 2:512], func=AF.Identity, scale=cwb[:, ch, 2:3],
                                bias=cwb[:, ch, 3:4]),
                                rd=[bank[bk], buf("p2c")], wr=[tA_b[ts]])
                            for k_ in range(2):
                                sc.op("act", lambda e, ps=ps, ch=ch, ts=ts, k_=k_: e.activation(
                                    out=tA[ts][:, k_:k_ + 1], in_=ps[:, k_:k_ + 1], func=AF.Identity,
                                    scale=cwb[:, ch, 2:3], bias=corr[:, ch, k_:k_ + 1]),
                                    rd=[bank[bk], buf("p2c"), buf("corr")], wp=[tA_b[ts]], n=1)
                        if G < NG - 1:
                            sc.op("act", lambda e, ps=ps, ch=ch: e.copy(out=hcur[:, ch, :], in_=ps[:, 510:512]),
                                  rd=[bank[bk]], wr=[hcb] if first_halo[0] else (), wp=[hcb] if not first_halo[0] else (), n=2)
                            first_halo[0] = False
                        sc.op("dve", lambda e, ps=ps, ch=ch, ts=ts: e.scalar_tensor_tensor(
                            out=tA[ts][:, 1:512], in0=ps[:, 0:511], scalar=cwb[:, ch, 1:2], in1=tA[ts][:, 1:512],
                            op0=ALU.mult, op1=ALU.add),
                            rd=[bank[bk], buf("p2c")], wr=[tA_b[ts]])
                        sc.op("dve", lambda e, ps=ps, ch=ch, ts=ts: e.scalar_tensor_tensor(
                            out=tA[ts][:, 2:512], in0=ps[:, 0:510], scalar=cwb[:, ch, 0:1], in1=tA[ts][:, 2:512],
                            op0=ALU.mult, op1=ALU.add),
                            rd=[bank[bk], buf("p2c")], wr=[tA_b[ts]])
                    if n + NWU < NG * NPAIR:
                        load_wu(n + NWU)
                    if WDB["n"] < 14 and n >= 2:
                        c_ = WDB["n"]
                        dma("pool", "wdB", [(wdB[:, c_, :], w_down_v[:, 8 + c_, :])],
                            wr=[buf("wdB")] if c_ == 0 else (), wp=[buf("wdB")] if c_ else (), deps=BAR[0])
                        WDB["n"] += 1

                    def fin(n=n, p=p):
                        tg, tv = (2 * n) % 4, (2 * n + 1) % 4
                        sc.op("act", lambda e: e.activation(out=tA[tg], in_=tA[tg], func=AF.Silu),
                              rd=[tA_b[tg]], wr=[tA_b[tg]])
                        sc.op(TUNE["gmult"], lambda e: e.tensor_tensor(out=gbuf[:, p, :], in0=tA[tg], in1=tA[tv], op=ALU.mult),
                              rd=[tA_b[tg], tA_b[tv]], wr=[g_b[p]],
                              deps=BAR[0] if n == 0 else ())
                    if prev_fin[0] is not None:
                        prev_fin[0]()
                    prev_fin[0] = fin
                prev_fin[0]()

        drain(prep_gen(0))
        for G in range(NG):
            drain(pool_gen(G))
            if G + 1 < NG:
                drain(attn_gen(G))
                drain(prep_gen(G + 1))
            else:
                sc.new_segment()
                BAR[0] = []
                dma("sp", "c3", [(g_post, gains_d[1]), (g_fpre, gains_d[2]), (g_fpost, gains_d[3]), (cwb, cwb_d)],
                    wr=[buf("p2c")])
                fns0 = mk_sec1(0, False, early=True)
                drain(attn_gen(G))
                for name in ("Amm0", "Apost0", "Amm1", "Ba0", "Apost1", "Bb0", "Amm2", "Ba1", "Apost2", "Bb1", "Amm3",
                             "Ba2", "Apost3", "Bb2", "Ba3", "Bb3"):
                    fns0[name[:-1]](int(name[-1]))

        sc.new_segment()
        BAR[0] = []
        for n in range(NWU):
            load_wu(n)
        sc.op("pool", lambda e: e.memset(corr, 0.0), wr=[buf("corr")], deps=BAR[0], n=512)
        WDB = {"n": 0}
        if True:
            for G in range(NG):
                up_section(G)
                run_boundary(G, G + 1 if G + 1 < NG else None)

        sc.op("sp", lambda e: e.nop(), deps=outs)
        sc.emit(block)
    return nc


def host_prep(inputs):
    f = np.float32
    w_up = np.asarray(inputs["w_up"][0], f)
    w_up_r = np.empty((NPAIR, 128, 8, 256), f)
    wu4 = w_up.reshape(8, 128, 2 * DFF)
    for pr in range(NPAIR):
        w_up_r[pr, :, :, 0:128] = wu4[:, :, pr * 128:(pr + 1) * 128].transpose(1, 0, 2)
        w_up_r[pr, :, :, 128:256] = wu4[:, :, DFF + pr * 128:DFF + (pr + 1) * 128].transpose(1, 0, 2)
    gains = np.stack([np.broadcast_to(np.asarray(inputs[k][0], f), (128, D)) for k in
                      ("norm_mix_pre", "norm_mix_post", "norm_ffn_pre", "norm_ffn_post")]).astype(f)
    colp = np.concatenate([np.asarray(inputs["b_pool"][0], f).reshape(4, 128).T,
                           np.asarray(inputs["pool_scale"][0], f).reshape(4, 128).T], axis=1)
    cw = np.asarray(inputs["conv_w"][0], f)[:, 0, :]
    cb = np.asarray(inputs["conv_b"][0], f)
    cwb = np.concatenate([cw, cb[None]], axis=0).reshape(4, 44, 128).transpose(2, 1, 0)
    inv_freq = (10000.0 ** (-np.arange(0, DH, 2, dtype=f) / DH)).astype(f)
    ang = (np.arange(S, dtype=f)[:, None] * inv_freq[None, :]).astype(f)
    rope = np.concatenate([np.cos(ang), np.sin(ang)], axis=1).astype(f)
    rope = rope.reshape(NT, 128, 64).transpose(1, 0, 2)
    mk = np.zeros((3, 8, 8), f)
    for qb in range(8):
        for j in range(8):
            mk[0, qb, j] = 0.0 if j < qb else -1e30
            mk[1, qb, j] = -NEG if j < qb else 0.0
            mk[2, qb, j] = 0.0 if j == qb else NEG
    mk = np.broadcast_to(mk, (128, 3, 8, 8))
    blk = np.zeros((8, S), f)
    for j in range(8):
        blk[j, j * 256:(j + 1) * 256] = 1.0
    poolA = np.zeros((3, 4, 128, 128), f)
    tp = np.arange(128)[:, None]
    t = np.arange(128)[None, :]
    for g, w in enumerate((2, 4, 8, 16)):
        band = ((t - tp) >= 0) & ((t - tp) < w)
        poolA[0, g] = band / w - (t == tp)
        poolA[1, g] = ((t + 128 - tp) < w) / w
        cnt = np.minimum(t + 1, w)
        poolA[2, g] = band / cnt - (t == tp)
    poolA = poolA.transpose(2, 0, 1, 3)
    idtri = np.zeros((128, 2, 128), f)
    idtri[:, 0, :] = np.eye(128)
    idtri[:, 1, :] = np.where(tp > t, NEG, 0.0)
    c = lambda a: np.ascontiguousarray(a, dtype=f)
    shared = dict(
        w_in=c(inputs["w_in"][0]), w_out=c(inputs["w_out"][0]), w_up_r=c(w_up_r), w_down=c(inputs["w_down"][0]),
        w_pool=c(inputs["w_pool"][0]), gains=c(gains), colp=c(colp), cwb=c(cwb), rope=c(rope), mk=c(mk),
        blkind=c(blk), poolA=c(poolA), idtri=c(idtri))
    return shared


_NC_CACHE = {}


def kernel(**inputs):
    shared = host_prep(inputs)
    x = np.asarray(inputs["x"], np.float32)
    if "nc" not in _NC_CACHE:
        _NC_CACHE["nc"] = build()
    nc = _NC_CACHE["nc"]
    in_maps = [dict(shared, x=np.ascontiguousarray(x[b])) for b in range(8)]
    res = run_bass_kernel_spmd(nc, in_maps, core_ids=list(range(8)))
    return np.stack([r["out"] for r in res.results], axis=0).astype(np.float32)
```

```python
import numpy as np
from contextlib import ExitStack
import concourse.bass as bass
import concourse.mybir as mybir
from concourse.bass_utils import run_bass_kernel_spmd
from concourse.alu_op_type import AluOpType as ALU

F32 = mybir.dt.float32
BF16 = mybir.dt.bfloat16
I32 = mybir.dt.int32
AF = mybir.ActivationFunctionType
AX = mybir.AxisListType

S = 2048
D = 1024
NT = 16
NG = 4
H = 8
DH = 64
DFF = 2816
NPAIR = 22
NEG = -30000.0
EPS = 1e-6
SCALE = DH ** -0.5
COMPUTE = ("pe", "act", "dve", "pool")
TUNE = dict(r0="dve", r2="dve", r3="dve", ktevac="act", add1="dve", add3="dve", corr="dve", gmult="pool")


class Buf:
    def __init__(self):
        self.w = []
        self.r = []
        self.pw = []

    def rd(self):
        return list(self.w)

    def wr_new(self):
        deps = self.r + self.w
        self.pw = deps
        self.w = []
        self.r = []
        return list(deps)

    def wr_more(self):
        return list(self.pw)


class Sched:
    COST = dict(pe=lambda n: 0.008 + n / 2370.0, act=lambda n: 0.22 + n / 1400.0, dve=lambda n: 0.29 + n / 960.0,
                pool=lambda n: 0.36 + n / 560.0, sp=lambda n: 0.05)

    def __init__(self, nc, st):
        self.nc = nc
        self.st = st
        self.streams = {e: [] for e in COMPUTE + ("sp",)}
        self.sems = {e: st.enter_context(nc.semaphore("s_" + e)) for e in COMPUTE}
        self.dsem = {}
        self.seg = 0
        self.order = 0

    def new_segment(self):
        self.seg += 1

    def _collect(self, eng, rd, wr, wp, deps):
        ds = list(deps)
        for b in rd:
            ds += b.rd()
        for b in wr:
            ds += b.wr_new()
        for b in wp:
            ds += b.wr_more()
            if eng == "pe":
                ds += [h for h in b.w if h[0] == "pe"]
        return [d for d in ds if d is not None]

    def op(self, eng, fn, rd=(), wr=(), wp=(), deps=(), n=None):
        raw = self._collect(eng, rd, wr, wp, deps)
        idx = len(self.streams[eng])
        if n is None:
            n = 512
        self.streams[eng].append(dict(fn=fn, raw=raw, kind="op", sig=False, seq=0, eng=eng, idx=idx, seg=self.seg,
                                      cost=self.COST[eng](n), prog=self.order))
        self.order += 1
        h = (eng, idx)
        for b in rd:
            b.r.append(h)
        for b in list(wr) + list(wp):
            b.w.append(h)
        return h

    def dma(self, queue, fn, key, rd=(), wr=(), wp=(), deps=(), nbytes=524288):
        raw = self._collect(queue, rd, wr, wp, deps)
        if key not in self.dsem:
            self.dsem[key] = [self.st.enter_context(self.nc.semaphore("d_" + key)), 0]
        idx = len(self.streams[queue])
        ent = dict(fn=fn, raw=raw, kind="dma", key=key, sig=False, seq=0, eng=queue, idx=idx, seg=self.seg,
                   cost=(0.65 if queue == "pool" else 0.08), lat=2.0 + nbytes / 120000.0, prog=self.order)
        self.order += 1
        self.streams[queue].append(ent)
        h = ["dma", key, None, ent]
        for b in rd:
            b.r.append(h)
        for b in list(wr) + list(wp):
            b.w.append(h)
        return h

    @staticmethod
    def _ent_of(sched, d):
        return d[3] if d[0] == "dma" else sched.streams[d[0]][d[1]]

    def _schedule(self):
        import heapq
        nseg = self.seg + 1
        self.est = []
        new_streams = {e: [] for e in self.streams}
        t_base = 0.0
        for sg in range(nseg):
            ops = [o for e in self.streams for o in self.streams[e] if o["seg"] == sg]
            if not ops:
                continue
            ids = {id(o) for o in ops}
            users = {id(o): [] for o in ops}
            remaining = {}
            ready = {}
            for o in ops:
                cnt = 0
                for d in o["raw"]:
                    p = self._ent_of(self, d)
                    if id(p) in ids:
                        users[id(p)].append(o)
                        cnt += 1
                remaining[id(o)] = cnt
                ready[id(o)] = t_base
            heaps = {e: [] for e in self.streams}
            for o in ops:
                if remaining[id(o)] == 0:
                    heapq.heappush(heaps[o["eng"]], (ready[id(o)], o["prog"], id(o), o))
            free_at = {e: t_base for e in self.streams}
            done = 0
            tmax = t_base
            while done < len(ops):
                best = None
                for e, hp in heaps.items():
                    if hp:
                        st_ = max(free_at[e], hp[0][0])
                        if best is None or st_ < best[0] or (st_ == best[0] and hp[0][1] < best[2]):
                            best = (st_, e, hp[0][1])
                assert best is not None, "scheduler deadlock (dependency cycle)"
                st_, e, _ = best
                hp = heaps[e]
                cands = []
                while hp and hp[0][0] <= st_ + 1e-9:
                    cands.append(heapq.heappop(hp))
                cands.sort(key=lambda c: c[1])
                chosen = cands[0]
                for c in cands[1:]:
                    heapq.heappush(hp, c)
                o = chosen[3]
                fin_eng = st_ + o["cost"]
                free_at[e] = fin_eng
                fin = fin_eng + (o["lat"] if o["kind"] == "dma" else 0.0)
                o["t_fin"] = fin
                tmax = max(tmax, fin)
                new_streams[e].append(o)
                done += 1
                for u in users[id(o)]:
                    lat = 0.12 if (u["eng"] == e and o["kind"] != "dma") else 0.28
                    ready[id(u)] = max(ready[id(u)], fin + lat)
                    remaining[id(u)] -= 1
                    if remaining[id(u)] == 0:
                        heapq.heappush(heaps[u["eng"]], (ready[id(u)], u["prog"], id(u), u))
            self.est.append(round(tmax, 1))
            t_base = tmax
        for e in self.streams:
            assert len(new_streams[e]) == len(self.streams[e])
            self.streams[e] = new_streams[e]
            for pos, o in enumerate(new_streams[e]):
                o["pos"] = pos
        last_of_seg = {}
        for e in COMPUTE:
            for o in self.streams[e]:
                last_of_seg[(e, o["seg"])] = o["pos"]
        for e, stream in self.streams.items():
            prev_seg = None
            for o in stream:
                best = {}
                deps = []
                for d in o["raw"]:
                    if d[0] == "dma":
                        deps.append(d)
                        continue
                    p = self.streams_lookup(d)
                    if d[0] == "pe" and e == "pe":
                        continue
                    if d[0] not in best or best[d[0]] < p["pos"]:
                        best[d[0]] = p["pos"]
                if o["seg"] != prev_seg and o["seg"] > 0:
                    for e2 in COMPUTE:
                        sgs = [s_ for (ee, s_) in last_of_seg if ee == e2 and s_ < o["seg"]]
                        if sgs:
                            pos2 = last_of_seg[(e2, max(sgs))]
                            if not (e2 == "pe" and e == "pe"):
                                if e2 not in best or best[e2] < pos2:
                                    best[e2] = pos2
                prev_seg = o["seg"]
                for k, v in best.items():
                    deps.append((k, v))
                o["deps"] = deps

    def streams_lookup(self, d):
        return self._byid[(d[0], d[1])]

    def emit(self, block):
        self._byid = {}
        for e, stream in self.streams.items():
            for o in stream:
                self._byid[(e, o["idx"])] = o
        self._schedule()
        for eng, stream in self.streams.items():
            for o in stream:
                for d in o["deps"]:
                    if d[0] != "dma":
                        self.streams[d[0]][d[1]]["sig"] = True
        for eng, stream in self.streams.items():
            n = 0
            for o in stream:
                if o["kind"] == "op" and o["sig"]:
                    n += 1
                    o["seq"] = n
        keyq = {}
        for eng, stream in self.streams.items():
            for o in stream:
                if o["kind"] == "dma":
                    assert keyq.setdefault(o["key"], eng) == eng, "dma key used from two queues"
                    self.dsem[o["key"]][1] += 16 * o["ndma"]
                    o["target"] = self.dsem[o["key"]][1]

        def make(eng):
            def body(e):
                waited = {}
                for o in self.streams[eng]:
                    need = {}
                    for d in o["deps"]:
                        if d[0] == "dma":
                            ent = d[3]
                            sem, val = self.dsem[d[1]][0], ent["target"]
                            k = "d_" + d[1]
                        else:
                            src = self.streams[d[0]][d[1]]
                            sem, val = self.sems[d[0]], src["seq"]
                            k = d[0]
                        if k not in need or need[k][1] < val:
                            need[k] = (sem, val)
                    for k, (sem, val) in need.items():
                        if waited.get(k, 0) < val:
                            e.wait_ge(sem, val)
                            waited[k] = val
                    ins = o["fn"](e)
                    if o["kind"] == "dma":
                        sem = self.dsem[o["key"]][0]
                        for i in ins:
                            i.then_inc(sem, 16)
                    elif o["sig"]:
                        ins.then_inc(self.sems[eng], 1)
            return body

        block.sync(make("sp"))
        block.gpsimd(make("pool"))
        block.scalar(make("act"))
        block.vector(make("dve"))
        block.tensor(make("pe"))


def build(dbg=False, p1_only=False):
    nc = bass.Bass("TRN2", target_bir_lowering=False)

    def din(name, shape):
        return nc.dram_tensor(name, list(shape), F32, kind="ExternalInput").ap()

    x_d = din("x", [S, D])
    w_in_d = din("w_in", [D, 2 * D])
    w_out_d = din("w_out", [D, D])
    w_up_d = din("w_up_r", [NPAIR, 128, 8, 256])
    w_down_d = din("w_down", [DFF, D])
    w_pool_d = din("w_pool", [4, 128, 128])
    gains_d = din("gains", [4, 128, D])
    colp_d = din("colp", [128, 8])
    cwb_d = din("cwb", [128, 44, 4])
    rope_d = din("rope", [128, NT, 64])
    mk_d = din("mk", [128, 3, 8, 8])
    blk_d = din("blkind", [8, S])
    poolA_d = din("poolA", [128, 3, 4, 128])
    idtri_d = din("idtri", [128, 2, 128])
    out_d = nc.dram_tensor("out", [S, D], F32, kind="ExternalOutput").ap()

    with ExitStack() as st:
        def sb(name, shape, dt):
            return st.enter_context(nc.sbuf_tensor(name, list(shape), dt))

        yT = sb("yT", [128, 8, S], BF16)
        RA = sb("RA", [128, 8192], F32)
        RB = sb("RB", [128, 8192], F32)
        RC = sb("RC", [128, 8192], F32)
        RD = sb("RD", [128, 2, D], F32)
        RE = sb("RE", [128, 6144], F32)
        RS = sb("RS", [128, 11264], F32)
        CST = sb("CST", [128, 2, 128], BF16)
        SM = sb("SM", [128, 256], F32)

        w_in_sb = RA[:, :].bitcast(BF16).rearrange("p (c n) -> p c n", c=8)
        KT = RB[:, :].bitcast(BF16).rearrange("p (h n) -> p h n", h=8)
        VA = RC[:, :].bitcast(BF16).rearrange("p (t h n) -> p t h n", t=NT, h=8)
        hT = RE[:, 0:2048].bitcast(BF16).rearrange("p (c n) -> p c n", c=8)
        QTs = [RE[:, 2048 * (1 + i):2048 * (2 + i)].bitcast(BF16).rearrange("p (h n) -> p h n", h=8)
               for i in range(2)]
        w_out_sb = RA[:, 0:4096].bitcast(BF16).rearrange("p (c n) -> p c n", c=8)
        wdA = RA[:, 4096:8192].bitcast(BF16).rearrange("p (c n) -> p c n", c=8)
        wdB = RB[:, 0:7168].bitcast(BF16).rearrange("p (c n) -> p c n", c=14)
        gbuf = RC[:, 0:5632].bitcast(BF16).rearrange("p (c n) -> p c n", c=NPAIR)
        NWU = 3
        wu = [RC[:, 5632 + i * 1024: 5632 + (i + 1) * 1024].bitcast(BF16).rearrange("p (c n) -> p c n", c=8)
              for i in range(2)] + \
             [RE[:, 4096:5120].bitcast(BF16).rearrange("p (c n) -> p c n", c=8)]
        hT2 = RE[:, 0:2048].bitcast(BF16).rearrange("p (c n) -> p c n", c=8)
        xn2 = [RE[:, 2048 + i * 512: 2048 + (i + 1) * 512].bitcast(BF16) for i in range(2)]
        n1t = RE[:, 3072:4096]

        o = [0]

        def carve(nwords):
            a = o[0]
            o[0] += nwords
            assert o[0] <= 11264, o[0]
            return RS[:, a:a + nwords]

        g_pre = carve(1024)
        ropet = carve(NT * 64).rearrange("p (t n) -> p t n", t=NT)
        mk = carve(192).rearrange("p (a q j) -> p a q j", a=3, q=8)
        poolA = carve(768).bitcast(BF16).rearrange("p (a g n) -> p a g n", a=3, g=4)
        wpool = carve(256).bitcast(BF16).rearrange("p (g n) -> p g n", g=4)
        colp = carve(8)
        xn = [carve(512).bitcast(BF16) for _ in range(2)]
        rt1s = [carve(512).rearrange("p (h n) -> p h n", h=8) for _ in range(2)]
        rt2s = [carve(512).rearrange("p (h n) -> p h n", h=8) for _ in range(2)]
        qr = carve(256).bitcast(BF16).rearrange("p (h n) -> p h n", h=8)
        kr = carve(256).bitcast(BF16).rearrange("p (h n) -> p h n", h=8)
        uring = carve(5 * 256).bitcast(BF16).rearrange("p (t n) -> p t n", t=5)
        diffs = carve(256).bitcast(BF16)
        kmT = carve(32).bitcast(BF16).rearrange("p (h j) -> p h j", h=8)
        gm = carve(64).rearrange("p (h j) -> p h j", h=8)
        kmTf = carve(64).rearrange("p (h j) -> p h j", h=8)
        top8 = carve(64).rearrange("p (h j) -> p h j", h=8)
        selA = carve(64).rearrange("p (h j) -> p h j", h=8)
        MB = carve(32).bitcast(BF16).rearrange("p (h j) -> p h j", h=8)
        junk = carve(512).bitcast(BF16)
        off_live = o[0]
        PT = [carve(256).bitcast(BF16) for _ in range(3)]
        rinv = [carve(512) for _ in range(2)]
        o[0] = 0
        x1 = carve(5 * 1024).rearrange("p (t n) -> p t n", t=5)
        g_post = carve(1024)
        g_fpre = carve(1024)
        g_fpost = carve(1024)
        cwb = carve(176).rearrange("p (c k) -> p c k", c=44)
        halo = carve(176).rearrange("p (a c k) -> p a c k", a=2, c=44)
        corr = carve(88).rearrange("p (c k) -> p c k", c=44)
        ctmp = carve(44)
        assert o[0] <= off_live, (o[0], off_live)
        tA = [carve(512) for _ in range(4)]
        junk2 = RE[:, 5632:6144].bitcast(BF16)

        ident = CST[:, 0, :]
        tri = CST[:, 1, :]
        ssq = SM[:, 0:16]
        rstd = SM[:, 16:32]
        nt_a = SM[:, 32:48]
        nt_b = SM[:, 48:64]
        nt_c = SM[:, 64:80]
        sq1 = SM[:, 80:96]
        sq2 = SM[:, 96:112]
        sq3 = SM[:, 112:128]
        rs1 = SM[:, 128:144]
        rs2 = SM[:, 144:160]
        rs3 = SM[:, 160:176]

        pbig_t = st.enter_context(nc.psum_tensor("pbig", [128, 1024], F32))
        pT_t = st.enter_context(nc.psum_tensor("pT", [128, 512], F32))
        up_t = st.enter_context(nc.psum_tensor("pup", [128, 2048], F32))
        p7_t = st.enter_context(nc.psum_tensor("p7", [128, 512], F32))
        banks = [pbig_t[:, 0:512], pbig_t[:, 512:1024], pT_t[:, :]] + \
                [up_t[:, k * 512:(k + 1) * 512] for k in range(4)] + [p7_t[:, :]]
        acc = [pbig_t[:, :], up_t[:, 0:1024]]
        block = st.enter_context(nc.Block())
        sc = Sched(nc, st)

        def dma(queue, key, pairs, **kw):
            def fn(e, pairs=pairs):
                return [e.dma_start(out=o_, in_=i_) for (o_, i_) in pairs]
            h = sc.dma(queue, fn, key, **kw)
            h[3]["ndma"] = len(pairs)
            return h

        B = {}

        def buf(name):
            if name not in B:
                B[name] = Buf()
            return B[name]

        bank = [buf("bank%d" % i) for i in range(8)]
        accb = [[bank[0], bank[1]], [bank[3], bank[4]]]

        dma("sp", "c1", [(g_pre, gains_d[0]), (ropet, rope_d), (mk, mk_d), (colp, colp_d)],
            wr=[buf("g_pre"), buf("ropet"), buf("mk"), buf("colp")])
        dma("pool", "c2", [(CST[:, :, :], idtri_d), (poolA, poolA_d), (wpool, w_pool_d.rearrange("g i o -> i g o"))]
            + [(KT[64:72, h, :], blk_d) for h in range(H)],
            wr=[buf("cst"), buf("poolA"), buf("wpool"), buf("KTind")])
        w_in_v = w_in_d.rearrange("(c p) n -> p c n", p=128)
        dma("pool", "win", [(w_in_sb[:, c, :], w_in_v[:, c, :]) for c in range(8)], wr=[buf("w_in")])

        xslot = [buf("xs%d" % i) for i in range(2)]

        def load_x(tt, deps=()):
            s = tt % 2
            return dma("sp", "x%d" % s, [(RD[:, s, :], x_d[tt * 128:(tt + 1) * 128, :])], wr=[xslot[s]], deps=deps)

        sc.op("pool", lambda e: e.memset(VA[:, :, :, 64:128], 1.0), wr=[buf("VAones")], n=512)
        sc.op("pool", lambda e: e.memset(kmT, 0.0), wr=[buf("kmT")], n=64)

        def rsqrt_col(ssq_col, ssq_buf, y, name, newton_eng="dve", extra_rd=()):
            v, hv, t = nt_a[:, 0:1], nt_b[:, 0:1], nt_c[:, 0:1]
            col = int(name.split("_")[-1])
            v, hv, t = nt_a[:, col:col + 1], nt_b[:, col:col + 1], nt_c[:, col:col + 1]
            bv, bh, bt, by = buf(name + "_v"), buf(name + "_h"), buf(name + "_t"), buf(name)
            sc.op("dve", lambda e: e.tensor_scalar(out=v, in0=ssq_col, scalar1=1.0 / D, scalar2=EPS,
                                                   op0=ALU.mult, op1=ALU.add),
                  rd=[ssq_buf] + list(extra_rd), wr=[bv], n=1)
            sc.op("dve", lambda e: e.tensor_scalar(out=y.bitcast(I32), in0=v.bitcast(I32), scalar1=-0.5,
                                                   scalar2=1597463007.0, op0=ALU.mult, op1=ALU.add),
                  rd=[bv], wr=[by], n=1)
            ne = newton_eng
            sc.op(ne, lambda e: e.tensor_scalar(out=hv, in0=v, scalar1=-0.5, scalar2=None, op0=ALU.mult),
                  rd=[bv], wr=[bh], n=1)
            for _ in range(3):
                if ne == "dve":
                    sc.op("dve", lambda e: e.scalar_tensor_tensor(out=t, in0=hv, scalar=y, in1=y,
                                                                 op0=ALU.mult, op1=ALU.mult),
                          rd=[bh, by], wr=[bt], n=1)
                else:
                    sc.op(ne, lambda e: e.tensor_tensor(out=t, in0=hv, in1=y, op=ALU.mult), rd=[bh, by], wr=[bt], n=1)
                    sc.op(ne, lambda e: e.tensor_tensor(out=t, in0=t, in1=y, op=ALU.mult), rd=[bt, by], wr=[bt], n=1)
                sc.op(ne, lambda e: e.tensor_scalar(out=y, in0=t, scalar1=1.5, scalar2=y, op0=ALU.add, op1=ALU.mult),
                      rd=[bt, by], wr=[by], n=1)
            return by

        load_x(0)
        load_x(1)
        bT = bank[2]
        bT2 = bank[3]
        psT = banks[2].bitcast(BF16)
        psT2 = banks[3].bitcast(BF16)
        hT_b = [buf("hT%d" % i) for i in range(4)]
        QT_b = [buf("QT0"), buf("QT1")]
        KT_b = [buf("KT%d" % g) for g in range(NG)]
        VA_b = [buf("VA%d" % t) for t in range(NT)]
        ur_b = [buf("ur%d" % i) for i in range(5)]
        yT_b = [buf("yT%d" % g) for g in range(NG)]

        def prep_gen(G):
            QT = QTs[G % 2]
            QTb = QT_b[G % 2]

            def S1(i):
                tt = 4 * G + i
                s_ = tt % 2
                sc.op("act", lambda e: e.activation(out=junk, in_=RD[:, s_, :], func=AF.Square,
                                                    accum_out=ssq[:, tt:tt + 1]),
                      rd=[xslot[s_]], wr=[buf("junk"), buf("ssq%d" % tt)], n=1024)

            def S2(i):
                tt = 4 * G + i
                s_ = tt % 2
                rb = rsqrt_col(ssq[:, tt:tt + 1], buf("ssq%d" % tt), rstd[:, tt:tt + 1], "r0_%d" % tt, TUNE["r0"])
                sc.op("dve", lambda e: e.scalar_tensor_tensor(out=xn[tt % 2], in0=RD[:, s_, :],
                                                             scalar=rstd[:, tt:tt + 1], in1=g_pre,
                                                             op0=ALU.mult, op1=ALU.mult),
                      rd=[xslot[s_], rb, buf("g_pre")], wr=[buf("xn%d" % (tt % 2))], n=1024)
                if tt + 2 < NT:
                    load_x(tt + 2)

            def S3(i):
                tt = 4 * G + i
                for c in range(8):
                    sc.op("pe", lambda e, c=c: e.transpose(out=psT[:, c * 128:(c + 1) * 128],
                                                          in_=xn[tt % 2][:, c * 128:(c + 1) * 128], identity=ident),
                          rd=[buf("xn%d" % (tt % 2)), buf("cst")], wr=[bT] if c == 0 else (), wp=[bT] if c else (), n=128)

            def S4(i):
                sc.op("act", lambda e: e.copy(out=hT[:, :, i * 128:(i + 1) * 128],
                                             in_=psT.rearrange("p (c n) -> p c n", c=8)),
                      rd=[bT], wr=[hT_b[i]], n=1024)

            for step in ([(S1, 0), (S1, 1)], [(S2, 0)], [(S3, 0), (S2, 1)], [(S4, 0), (S1, 2)],
                         [(S3, 1), (S2, 2)], [(S4, 1), (S1, 3)], [(S3, 2), (S2, 3)], [(S4, 2)], [(S3, 3)], [(S4, 3)]):
                for fn, i in step:
                    fn(i)
                yield

            def proj(i, j, pb):
                for c in range(8):
                    sc.op("pe", lambda e, c=c: e.matmul(
                        banks[pb], lhsT=hT[:, c, i * 128:(i + 1) * 128],
                        rhs=w_in_sb[:, c, j * 512:(j + 1) * 512], start=(c == 0), stop=(c == 7)),
                        rd=[hT_b[i], buf("w_in")], wr=[bank[pb]] if c == 0 else (), wp=[bank[pb]] if c else ())

            def rope_dve(i, j):
                tt = 4 * G + i
                pb = j
                rt1, rt2 = rt1s[j], rt2s[j]
                psv = banks[pb].rearrange("p (h n) -> p h n", h=8)
                cosb = ropet[:, tt, 0:32].unsqueeze(1).to_broadcast([128, 8, 32])
                sinb = ropet[:, tt, 32:64].unsqueeze(1).to_broadcast([128, 8, 32])
                b1_, b2_ = buf("rt1_%d" % j), buf("rt2_%d" % j)
                sc.op("dve", lambda e: e.tensor_tensor(out=rt1[:, :, 0:32], in0=psv[:, :, 0:32], in1=cosb, op=ALU.mult),
                      rd=[bank[pb], buf("ropet")], wr=[b1_])
                sc.op("dve", lambda e: e.tensor_tensor(out=rt1[:, :, 32:64], in0=psv[:, :, 32:64], in1=cosb, op=ALU.mult),
                      rd=[bank[pb], buf("ropet")], wp=[b1_])
                sc.op("dve", lambda e: e.tensor_tensor(out=rt2[:, :, 0:32], in0=psv[:, :, 32:64], in1=sinb, op=ALU.mult),
                      rd=[bank[pb], buf("ropet")], wr=[b2_])
                sc.op("dve", lambda e: e.tensor_tensor(out=rt2[:, :, 32:64], in0=psv[:, :, 0:32], in1=sinb, op=ALU.mult),
                      rd=[bank[pb], buf("ropet")], wp=[b2_])

            def rope_pool(i, j):
                rt1, rt2 = rt1s[j], rt2s[j]
                b1_, b2_ = buf("rt1_%d" % j), buf("rt2_%d" % j)
                dst = qr if j == 0 else kr
                dstb = buf("qr") if j == 0 else buf("kr")
                sc.op("pool", lambda e: e.tensor_tensor(out=dst[:, :, 0:32], in0=rt1[:, :, 0:32], in1=rt2[:, :, 0:32],
                                                        op=ALU.subtract), rd=[b1_, b2_], wr=[dstb])
                sc.op("pool", lambda e: e.tensor_tensor(out=dst[:, :, 32:64], in0=rt1[:, :, 32:64], in1=rt2[:, :, 32:64],
                                                        op=ALU.add), rd=[b1_, b2_], wp=[dstb])

            def b1(i):
                proj(i, 0, 0)
                proj(i, 1, 1)

            def b2(i):
                rope_dve(i, 0)
                rope_dve(i, 1)

            def b3(i):
                rope_pool(i, 0)
                rope_pool(i, 1)
                proj(i, 2, 0)
                proj(i, 3, 1)

            def b4(i):
                tt = 4 * G + i
                psv = banks[0].rearrange("p (h n) -> p h n", h=8)
                sc.op("act", lambda e: e.copy(out=VA[:, tt, :, 0:64], in_=psv),
                      rd=[bank[0]], wr=[VA_b[tt]], deps=buf("VAones").rd())
                sc.op("act", lambda e: e.copy(out=uring[:, tt % 5, :], in_=banks[1]),
                      rd=[bank[1]], wr=[ur_b[tt % 5]])
                for h in range(H):
                    sc.op("pe", lambda e, h=h: e.transpose(out=psT[0:64, h * 128:(h + 1) * 128], in_=qr[:, h, :],
                                                          identity=ident),
                          rd=[buf("qr"), buf("cst")], wr=[bT] if h == 0 else (), wp=[bT] if h else (), n=128)
                for h in range(H):
                    sc.op("pe", lambda e, h=h: e.transpose(out=psT2[0:64, h * 128:(h + 1) * 128], in_=kr[:, h, :],
                                                          identity=ident),
                          rd=[buf("kr"), buf("cst")], wr=[bT2] if h == 0 else (), wp=[bT2] if h else (), n=128)

            def b5(i):
                tt = 4 * G + i
                sc.op("act", lambda e: e.copy(
                    out=QT[0:64, :, i * 128:(i + 1) * 128], in_=psT[0:64, :].rearrange("p (h n) -> p h n", h=8)),
                    rd=[bT], wr=[QTb] if i == 0 else (), wp=[QTb] if i else (), n=1024)
                sc.op(TUNE["ktevac"], lambda e: (e.tensor_copy if TUNE["ktevac"] == "dve" else e.copy)(
                    out=KT[0:64, :, tt * 128:(tt + 1) * 128], in_=psT2[0:64, :].rearrange("p (h n) -> p h n", h=8)),
                    rd=[bT2], wr=[KT_b[G]] if i == 0 else (), wp=[KT_b[G]] if i else (), n=1024)
                if tt % 2 == 1:
                    jb = tt // 2
                    sc.op("dve", lambda e: e.tensor_reduce(
                        out=kmTf[0:64, :, jb:jb + 1], in_=KT[0:64, :, jb * 256:(jb + 1) * 256],
                        axis=AX.X, op=ALU.add), rd=[KT_b[G]], wr=[buf("kmTf")], n=64)
                    sc.op("dve", lambda e: e.tensor_copy(out=kmT[0:64, :, jb:jb + 1], in_=kmTf[0:64, :, jb:jb + 1]),
                          rd=[buf("kmTf")], wp=[buf("kmT")], n=64)

            def b6(i):
                if (4 * G + i) // 2 == 0:
                    return
                for h in range(H):
                    sc.op("pe", lambda e, h=h: e.matmul(
                        banks[3][:, h * 8:(h + 1) * 8], lhsT=QT[0:64, h, i * 128:(i + 1) * 128], rhs=kmT[0:64, h, :],
                        start=True, stop=True),
                        rd=[QTb, buf("kmT")], wr=[bT2] if h == 0 else (), wp=[bT2] if h else (), n=8)

            def b7(i):
                qb = (4 * G + i) // 2
                pastneg = mk[:, 0, qb, :].unsqueeze(1).to_broadcast([128, 8, 8])
                pastP = mk[:, 1, qb, :].unsqueeze(1).to_broadcast([128, 8, 8])
                Cc = mk[:, 2, qb, :].unsqueeze(1).to_broadcast([128, 8, 8])
                if qb == 0:
                    sc.op("dve", lambda e: e.tensor_copy(out=MB, in_=Cc), rd=[buf("mk")], wr=[buf("MB")], n=64)
                    return
                sc.op("dve", lambda e: e.tensor_tensor(
                    out=gm, in0=banks[3][:, 0:64].rearrange("p (h j) -> p h j", h=8), in1=pastneg, op=ALU.add),
                    rd=[bT2, buf("mk")], wr=[buf("gm")], n=64)
                for h in range(H):
                    sc.op("dve", lambda e, h=h: e.max(out=top8[:, h, :], in_=gm[:, h, :]),
                          rd=[buf("gm")], wr=[buf("top8")] if h == 0 else (), wp=[buf("top8")] if h else (), n=8)
                sc.op("dve", lambda e: e.tensor_tensor(out=selA, in0=gm, in1=top8[:, :, 2:3].to_broadcast([128, 8, 8]),
                                                       op=ALU.is_ge),
                      rd=[buf("gm"), buf("top8")], wr=[buf("selA")], n=64)
                sc.op("dve", lambda e: e.tensor_tensor(out=gm, in0=selA, in1=pastP, op=ALU.mult),
                      rd=[buf("selA"), buf("mk")], wr=[buf("gm")], n=64)
                sc.op("dve", lambda e: e.tensor_tensor(out=MB, in0=gm, in1=Cc, op=ALU.add),
                      rd=[buf("gm"), buf("mk")], wr=[buf("MB")], n=64)

            def b8(i):
                for h in range(H):
                    sc.op("pe", lambda e, h=h: e.transpose(out=psT[64:72, h * 128:(h + 1) * 128], in_=MB[:, h, :],
                                                          identity=ident),
                          rd=[buf("MB"), buf("cst")], wr=[bT] if h == 0 else (), wp=[bT] if h else (), n=128)

            def b9(i):
                sc.op("act", lambda e: e.copy(out=QT[64:72, :, i * 128:(i + 1) * 128],
                                             in_=psT[64:72, :].rearrange("p (h n) -> p h n", h=8)),
                      rd=[bT], wp=[QTb], n=1024)

            stages = [b1, b2, b3, b4, b5, b6, b7, b8, b9]
            SKEW = 5
            nsteps = len(stages) + 3 * SKEW
            for t in range(nsteps):
                for i in range(4):
                    k = t - SKEW * i
                    if 0 <= k < len(stages):
                        stages[k](i)
                yield
            if G == NG - 1:
                w_out_v = w_out_d.rearrange("(c p) n -> p c n", p=128)
                w_down_v = w_down_d.rearrange("(c p) n -> p c n", p=128)
                dma("pool", "wout", [(w_out_sb[:, c, :], w_out_v[:, c, :]) for c in range(8)], wr=[buf("w_in")])
                dma("pool", "wdA", [(wdA[:, c, :], w_down_v[:, c, :]) for c in range(8)], wp=[buf("w_in")])

        def attn_gen(G):
            QT = QTs[G % 2]
            QTb = QT_b[G % 2]
            nkt = 4 * G + 4
            its = [(h, kt) for h in range(H) for kt in range(nkt)]
            N = len(its)

            def emit_qk(n):
                h, kt = its[n]
                nd = kt - 4 * G
                c0 = max(0, nd) * 128
                sslot = 5 + (n % 2)
                ps = banks[sslot]
                sc.op("pe", lambda e: e.matmul(
                    ps[:, c0:512], lhsT=KT[0:72, h, kt * 128:(kt + 1) * 128], rhs=QT[0:72, h, c0:512],
                    start=True, stop=(nd < 0)),
                    rd=[KT_b[kt // 4], QTb, buf("KTind")], wr=[bank[sslot]])
                if nd >= 0:
                    sc.op("pe", lambda e: e.matmul(ps[:, c0:c0 + 128], lhsT=ident, rhs=tri, start=False, stop=True),
                          rd=[buf("cst")], wp=[bank[sslot]], n=128)

            pending = []
            if G == NG - 1:
                for r_ in range(12):
                    sc.op("pe", lambda e: e.matmul(banks[4], lhsT=ident, rhs=VA[:, 0, 0:4, :].rearrange("p a b -> p (a b)"), start=True, stop=True),
                          rd=[QTb, buf("cst"), VA_b[0], buf("VAones")], wr=[bank[4]] if r_ == 0 else (), wp=[bank[4]] if r_ else ())
            emit_qk(0)
            if N > 1:
                emit_qk(1)
            for n in range(N):
                h, kt = its[n]
                nd = kt - 4 * G
                c0 = max(0, nd) * 128
                sslot = 5 + (n % 2)
                ps = banks[sslot]
                pslot = n % 3
                pob = 7 if h % 2 == 0 else 4
                po = banks[pob]
                sc.op("act", lambda e, c0=c0, ps=ps, pslot=pslot: e.activation(
                    out=PT[pslot][:, c0:512], in_=ps[:, c0:512], func=AF.Exp, scale=SCALE),
                    rd=[bank[sslot]], wr=[buf("PT%d" % pslot)])
                sc.op("pe", lambda e, h=h, kt=kt, c0=c0, pslot=pslot, po=po: e.matmul(
                    po[:, c0:512], lhsT=VA[:, kt, h, :], rhs=PT[pslot][:, c0:512],
                    start=(kt == 0), stop=(kt == nkt - 1)),
                    rd=[buf("PT%d" % pslot), VA_b[kt], buf("VAones")],
                    wr=[bank[pob]] if kt == 0 else (), wp=[bank[pob]] if kt else ())
                if n + 2 < N:
                    emit_qk(n + 2)
                if kt == nkt - 1:
                    def norm(h=h, po=po, pob=pob):
                        ri = rinv[h % 2]
                        rib = buf("rinv%d" % (h % 2))
                        sc.op("dve", lambda e: e.reciprocal(out=ri[0:64, :], in_=po[64:128, :]),
                              rd=[bank[pob]], wr=[rib])
                        p0 = (h % 2) * 64
                        sc.op("dve", lambda e: e.tensor_tensor(
                            out=yT[p0:p0 + 64, h // 2, G * 512:(G + 1) * 512], in0=po[0:64, :], in1=ri[0:64, :],
                            op=ALU.mult),
                            rd=[bank[pob], rib], wr=[yT_b[G]] if h == 0 else (), wp=[yT_b[G]] if h else ())
                    pending.append((n + 2, norm))
                while pending and (pending[0][0] <= n or n == N - 1):
                    pending.pop(0)[1]()
                yield
        def pool_gen(G):
            for g in range(4):
                for i in range(4):
                    tt = 4 * G + i
                    a_idx = 2 if tt == 0 else 0
                    sc.op("pe", lambda e, g=g, i=i, tt=tt, a_idx=a_idx: e.matmul(
                        banks[0][:, i * 128:(i + 1) * 128], lhsT=uring[:, tt % 5, g * 128:(g + 1) * 128],
                        rhs=poolA[:, a_idx, g, :], start=True, stop=(tt == 0)),
                        rd=[ur_b[tt % 5], buf("poolA")], wr=[bank[0]] if i == 0 else (), wp=[bank[0]] if i else (), n=128)
                    if tt > 0:
                        sc.op("pe", lambda e, g=g, i=i, tt=tt: e.matmul(
                            banks[0][:, i * 128:(i + 1) * 128], lhsT=uring[:, (tt - 1) % 5, g * 128:(g + 1) * 128],
                            rhs=poolA[:, 1, g, :], start=False, stop=True),
                            rd=[ur_b[(tt - 1) % 5], buf("poolA")], wp=[bank[0]], n=128)
                sc.op("act", lambda e: e.copy(out=diffs, in_=banks[0]), rd=[bank[0]], wr=[buf("diffs")])
                sc.op("pe", lambda e, g=g: e.matmul(banks[1], lhsT=wpool[:, g, :], rhs=diffs, start=True, stop=True),
                      rd=[buf("diffs"), buf("wpool")], wr=[bank[1]])
                sc.op("dve", lambda e, g=g: e.tensor_scalar(
                    out=yT[:, 4 + g, G * 512:(G + 1) * 512], in0=banks[1], scalar1=colp[:, g:g + 1],
                    scalar2=colp[:, 4 + g:5 + g], op0=ALU.add, op1=ALU.mult),
                    rd=[bank[1], buf("colp")], wp=[yT_b[G]])
                yield

        def drain(gen):
            n = 0
            for _ in gen:
                n += 1
            return n

        def count(genf, G):
            return None

        outs = []
        BAR = [None]
        if dbg:
            ydbg = nc.dram_tensor("dbg_yT", [1024, S], F32, kind="ExternalOutput").ap()
            for c in range(8):
                for half in range(4):
                    sc.op("dve", lambda e, c=c, half=half: e.tensor_copy(out=RD[:, 0, 0:512],
                                                                        in_=yT[:, c, half * 512:(half + 1) * 512]),
                          rd=yT_b, wr=[buf("dbgt")])
                    outs.append(dma("sp", "dbg", [(ydbg[c * 128:(c + 1) * 128, half * 512:(half + 1) * 512],
                                                   RD[:, 0, 0:512])], rd=[buf("dbgt")]))
        if True:
            w_down_v = w_down_d.rearrange("(c p) n -> p c n", p=128)
            wu_b = [buf("wu%d" % i) for i in range(NWU)]

            def load_wu(n):
                pr = n % NPAIR
                return dma("pool", "wu%d" % (n % NWU), [(wu[n % NWU], w_up_d[pr])], wr=[wu_b[n % NWU]],
                           deps=BAR[0] if n < NWU else ())
            x1_b = [buf("x1_%d" % i) for i in range(5)]
            hT2_b = buf("hT2")
            g_b = [buf("gbuf%d" % p_) for p_ in range(NPAIR)]
            tA_b = [buf("tA%d" % i) for i in range(4)]
            halo_b = [buf("halo0"), buf("halo1")]
            n1a = n1t
            n1b = RE[:, 5120:6144]
            acc3 = [pbig_t[:, :], up_t[:, 0:1024], up_t[:, 1024:2048]]
            acc3b = [[bank[0], bank[1]], [bank[3], bank[4]], [bank[5], bank[6]]]

            def mk_sec1(G, with_d, early=False):
                def aidx(i):
                    if early:
                        return 0
                    return 1 if with_d else (1, 2)[i % 2]

                def Amm(i):
                    tt = 4 * G + i
                    pacc, pb_ = acc3[aidx(i)], acc3b[aidx(i)]
                    load_x(tt, deps=BAR[0] if (tt < 2 or early) else ())
                    for dh in range(2):
                        for c in range(8):
                            sc.op("pe", lambda e, c=c, dh=dh: e.matmul(
                                pacc[:, dh * 512:(dh + 1) * 512], lhsT=yT[:, c, tt * 128:(tt + 1) * 128],
                                rhs=w_out_sb[:, c, dh * 512:(dh + 1) * 512], start=(c == 0), stop=(c == 7)),
                                rd=[yT_b[G], buf("w_in")], wr=[pb_[dh]] if c == 0 else (), wp=[pb_[dh]] if c else (),
                                deps=BAR[0] if ((tt == 0 or early) and c == 0) else ())

                def Apost(i, part=None):
                    tt = 4 * G + i
                    s_ = tt % 2
                    pacc, pb_ = acc3[aidx(i)], acc3b[aidx(i)]
                    xb = buf("xn2_%d" % (tt % 2))
                    if part in (None, 0):
                        sc.op("act", lambda e: e.activation(out=xn2[tt % 2], in_=pacc, func=AF.Square,
                                                            accum_out=sq1[:, tt:tt + 1]),
                              rd=pb_, wr=[xb, buf("sq1_%d" % tt)], deps=BAR[0] if (tt == 0 or early) else (), n=1024)
                    if part in (None, 1):
                        rsqrt_col(sq1[:, tt:tt + 1], buf("sq1_%d" % tt), rs1[:, tt:tt + 1], "r1_%d" % tt, "pool")
                    if part in (None, 2):
                        rb1 = buf("r1_%d" % tt)
                        sc.op("dve", lambda e: e.scalar_tensor_tensor(
                            out=n1a, in0=pacc, scalar=rs1[:, tt:tt + 1], in1=g_post, op0=ALU.mult, op1=ALU.mult),
                            rd=pb_ + [rb1, buf("p2c")], wr=[buf("n1a")], n=1024)
                        sc.op(TUNE["add1"], lambda e: e.tensor_tensor(out=x1[:, tt % 5, :], in0=n1a, in1=RD[:, s_, :], op=ALU.add),
                              rd=[buf("n1a"), xslot[s_]], wr=[x1_b[tt % 5]], deps=BAR[0] if (tt == 0 or early) else (), n=1024)

                def Ba(i, part=None):
                    tt = 4 * G + i
                    xb = buf("xn2_%d" % (tt % 2))
                    if part in (None, 0):
                        sc.op("act", lambda e: e.activation(out=xn2[tt % 2], in_=x1[:, tt % 5, :], func=AF.Square,
                                                            accum_out=sq2[:, tt:tt + 1]),
                              rd=[x1_b[tt % 5]], wr=[xb, buf("sq2_%d" % tt)], n=1024)
                    if part in (None, 1):
                        rsqrt_col(sq2[:, tt:tt + 1], buf("sq2_%d" % tt), rs2[:, tt:tt + 1], "r2_%d" % tt, TUNE["r2"],
                                  [buf("r1_%d" % tt)])
                    if part in (None, 2):
                        rb2 = buf("r2_%d" % tt)
                        sc.op("dve", lambda e: e.scalar_tensor_tensor(out=xn2[tt % 2], in0=x1[:, tt % 5, :],
                                                                     scalar=rs2[:, tt:tt + 1], in1=g_fpre,
                                                                     op0=ALU.mult, op1=ALU.mult),
                              rd=[x1_b[tt % 5], rb2, buf("p2c")], wr=[xb], n=1024)

                def Bb(i, part=None):
                    tt = 4 * G + i
                    xb = buf("xn2_%d" % (tt % 2))
                    if part in (None, 0):
                        for c in range(8):
                            sc.op("pe", lambda e, c=c: e.transpose(out=psT[:, c * 128:(c + 1) * 128],
                                                                  in_=xn2[tt % 2][:, c * 128:(c + 1) * 128], identity=ident),
                                  rd=[xb, buf("cst")], wr=[bT] if c == 0 else (), wp=[bT] if c else (), n=128)
                    if part in (None, 1):
                        sc.op("act", lambda e: e.copy(out=hT2[:, :, i * 128:(i + 1) * 128],
                                                     in_=psT.rearrange("p (c n) -> p c n", c=8)),
                              rd=[bT], wr=[hT2_b] if i == 0 else (), wp=[hT2_b] if i else (), n=1024)
                return dict(Amm=Amm, Apost=Apost, Ba=Ba, Bb=Bb)

            def mk_down(G):
                def didx(i):
                    return (0, 2)[i % 2]

                def Dmm(i):
                    pacc, pb_ = acc3[didx(i)], acc3b[didx(i)]
                    for dh in range(2):
                        for c in range(NPAIR):
                            wsrc = wdA[:, c, dh * 512:(dh + 1) * 512] if c < 8 else wdB[:, c - 8, dh * 512:(dh + 1) * 512]
                            sc.op("pe", lambda e, c=c, dh=dh, wsrc=wsrc: e.matmul(
                                pacc[:, dh * 512:(dh + 1) * 512], lhsT=gbuf[:, c, i * 128:(i + 1) * 128], rhs=wsrc,
                                start=(c == 0), stop=(c == NPAIR - 1)),
                                rd=[g_b[c], buf("w_in"), buf("wdB")], wr=[pb_[dh]] if c == 0 else (), wp=[pb_[dh]] if c else ())

                def Dpost(i):
                    tt = 4 * G + i
                    pacc, pb_ = acc3[didx(i)], acc3b[didx(i)]
                    sc.op("act", lambda e: e.activation(out=n1b, in_=pacc, func=AF.Square,
                                                        accum_out=sq3[:, tt:tt + 1]),
                          rd=pb_, wr=[buf("n1b"), buf("sq3_%d" % tt)], n=1024)
                    rb3 = rsqrt_col(sq3[:, tt:tt + 1], buf("sq3_%d" % tt), rs3[:, tt:tt + 1], "r3_%d" % tt, TUNE["r3"],
                                    [buf("r2_%d" % tt)])
                    sc.op("dve", lambda e: e.scalar_tensor_tensor(
                        out=n1b, in0=pacc, scalar=rs3[:, tt:tt + 1], in1=g_fpost, op0=ALU.mult, op1=ALU.mult),
                        rd=pb_ + [rb3, buf("p2c")], wr=[buf("n1b")], n=1024)
                    sc.op(TUNE["add3"], lambda e: e.tensor_tensor(out=x1[:, tt % 5, :], in0=n1b, in1=x1[:, tt % 5, :], op=ALU.add),
                          rd=[buf("n1b")], wr=[x1_b[tt % 5]], n=1024)
                    outs.append(dma("sp", "out%d" % (tt % 5), [(out_d[tt * 128:(tt + 1) * 128, :], x1[:, tt % 5, :])],
                                    rd=[x1_b[tt % 5]]))
                return dict(Dmm=Dmm, Dpost=Dpost)

            ORDER = ["Amm0", "Dmm0", "Apost0", "Amm1", "Dmm1", "Dpost0", "Ba0", "Apost1", "Amm2", "Bb0", "Dmm2", "Dpost1",
                     "Ba1", "Apost2", "Amm3", "Bb1", "Dmm3", "Dpost2", "Ba2", "Apost3", "Bb2", "Ba3", "Dpost3", "Bb3"]

            def run_boundary(Gd, Ga):
                fns = {}
                if Gd is not None:
                    fns.update(mk_down(Gd))
                if Ga is not None:
                    fns.update(mk_sec1(Ga, Gd is not None))
                for name in ORDER:
                    key, i = name[:-1], int(name[-1])
                    if key in fns:
                        fns[key](i)

            def up_section(G):
                if G > 0:
                    hp = halo[:, (G - 1) % 2, :, :]
                    hb = halo_b[(G - 1) % 2]
                    sc.op("pool", lambda e: e.tensor_tensor(out=corr[:, :, 0], in0=hp[:, :, 1], in1=cwb[:, :, 1],
                                                            op=ALU.mult), rd=[hb, buf("p2c")], wr=[buf("corr")], n=44)
                    sc.op("pool", lambda e: e.tensor_tensor(out=ctmp, in0=hp[:, :, 0], in1=cwb[:, :, 0],
                                                            op=ALU.mult), rd=[hb, buf("p2c")], wr=[buf("ctmp")], n=44)
                    sc.op("pool", lambda e: e.tensor_tensor(out=corr[:, :, 0], in0=corr[:, :, 0], in1=ctmp, op=ALU.add),
                          rd=[buf("ctmp"), buf("corr")], wr=[buf("corr")], n=44)
                    sc.op("pool", lambda e: e.tensor_tensor(out=corr[:, :, 1], in0=hp[:, :, 1], in1=cwb[:, :, 0],
                                                            op=ALU.mult), rd=[hb, buf("p2c")], wr=[buf("corr")], n=44)
                    for k_ in range(2):
                        sc.op("pool", lambda e, k_=k_: e.tensor_tensor(out=corr[:, :, k_], in0=corr[:, :, k_],
                                                                      in1=cwb[:, :, 3], op=ALU.add),
                              rd=[buf("p2c")], wr=[buf("corr")], n=44)
                hcur = halo[:, G % 2, :, :]
                hcb = halo_b[G % 2]
                first_halo = [True]
                prev_fin = [None]
                for p in range(NPAIR):
                    n = G * NPAIR + p
                    ws = n % NWU
                    for half in range(2):
                        q_ = 2 * n + half
                        bk = 3 + (q_ % 4)
                        ts = q_ % 4
                        ch = p + NPAIR * half
                        ps = banks[bk]
                        for c in range(8):
                            sc.op("pe", lambda e, c=c, half=half, ps=ps, ws=ws: e.matmul(
                                ps, lhsT=wu[ws][:, c, half * 128:(half + 1) * 128], rhs=hT2[:, c, :],
                                start=(c == 0), stop=(c == 7)),
                                rd=[wu_b[ws], hT2_b], wr=[bank[bk]] if c == 0 else (), wp=[bank[bk]] if c else (),
                                deps=BAR[0] if (n == 0 and c == 0) else ())
                        if G == 0:
                            sc.op("act", lambda e, ps=ps, ch=ch, ts=ts: e.activation(
                                out=tA[ts], in_=ps, func=AF.Identity, scale=cwb[:, ch, 2:3], bias=cwb[:, ch, 3:4]),
                                rd=[bank[bk], buf("p2c")], wr=[tA_b[ts]], deps=BAR[0] if q_ < 4 else ())
                        else:
                            sc.op("act", lambda e, ps=ps, ch=ch, ts=ts: e.activation(
                                out=tA[ts][:, 2:512], in_=ps[:, 2:512], func=AF.Identity, scale=cwb[:, ch, 2:3],
                                bias=cwb[:, ch, 3:4]),
                                rd=[bank[bk], buf("p2c")], wr=[tA_b[ts]])
                            for k_ in range(2):
                                sc.op("act", lambda e, ps=ps, ch=ch, ts=ts, k_=k_: e.activation(
                                    out=tA[ts][:, k_:k_ + 1], in_=ps[:, k_:k_ + 1], func=AF.Identity,
                                    scale=cwb[:, ch, 2:3], bias=corr[:, ch, k_:k_ + 1]),
                                    rd=[bank[bk], buf("p2c"), buf("corr")], wp=[tA_b[ts]], n=1)
                        if G < NG - 1:
                            sc.op("act", lambda e, ps=ps, ch=ch: e.copy(out=hcur[:, ch, :], in_=ps[:, 510:512]),
                                  rd=[bank[bk]], wr=[hcb] if first_halo[0] else (), wp=[hcb] if not first_halo[0] else (), n=2)
                            first_halo[0] = False
                        sc.op("dve", lambda e, ps=ps, ch=ch, ts=ts: e.scalar_tensor_tensor(
                            out=tA[ts][:, 1:512], in0=ps[:, 0:511], scalar=cwb[:, ch, 1:2], in1=tA[ts][:, 1:512],
                            op0=ALU.mult, op1=ALU.add),
                            rd=[bank[bk], buf("p2c")], wr=[tA_b[ts]])
                        sc.op("dve", lambda e, ps=ps, ch=ch, ts=ts: e.scalar_tensor_tensor(
                            out=tA[ts][:, 2:512], in0=ps[:, 0:510], scalar=cwb[:, ch, 0:1], in1=tA[ts][:, 2:512],
                            op0=ALU.mult, op1=ALU.add),
                            rd=[bank[bk], buf("p2c")], wr=[tA_b[ts]])
                    if n + NWU < NG * NPAIR:
                        load_wu(n + NWU)
                    if WDB["n"] < 14 and n >= 2:
                        c_ = WDB["n"]
                        dma("pool", "wdB", [(wdB[:, c_, :], w_down_v[:, 8 + c_, :])],
                            wr=[buf("wdB")] if c_ == 0 else (), wp=[buf("wdB")] if c_ else (), deps=BAR[0])
                        WDB["n"] += 1

                    def fin(n=n, p=p):
                        tg, tv = (2 * n) % 4, (2 * n + 1) % 4
                        sc.op("act", lambda e: e.activation(out=tA[tg], in_=tA[tg], func=AF.Silu),
                              rd=[tA_b[tg]], wr=[tA_b[tg]])
                        sc.op(TUNE["gmult"], lambda e: e.tensor_tensor(out=gbuf[:, p, :], in0=tA[tg], in1=tA[tv], op=ALU.mult),
                              rd=[tA_b[tg], tA_b[tv]], wr=[g_b[p]],
                              deps=BAR[0] if n == 0 else ())
                    if prev_fin[0] is not None:
                        prev_fin[0]()
                    prev_fin[0] = fin
                prev_fin[0]()

        drain(prep_gen(0))
        for G in range(NG):
            drain(pool_gen(G))
            if G + 1 < NG:
                drain(attn_gen(G))
                drain(prep_gen(G + 1))
            else:
                sc.new_segment()
                BAR[0] = []
                dma("sp", "c3", [(g_post, gains_d[1]), (g_fpre, gains_d[2]), (g_fpost, gains_d[3]), (cwb, cwb_d)],
                    wr=[buf("p2c")])
                fns0 = mk_sec1(0, False, early=True)
                drain(attn_gen(G))
                for name in ("Amm0", "Apost0", "Amm1", "Ba0", "Apost1", "Bb0", "Amm2", "Ba1", "Apost2", "Bb1", "Amm3",
                             "Ba2", "Apost3", "Bb2", "Ba3", "Bb3"):
                    fns0[name[:-1]](int(name[-1]))

        sc.new_segment()
        BAR[0] = []
        for n in range(NWU):
            load_wu(n)
        sc.op("pool", lambda e: e.memset(corr, 0.0), wr=[buf("corr")], deps=BAR[0], n=512)
        WDB = {"n": 0}
        if True:
            for G in range(NG):
                up_section(G)
                run_boundary(G, G + 1 if G + 1 < NG else None)

        sc.op("sp", lambda e: e.nop(), deps=outs)
        sc.emit(block)
    return nc


def host_prep(inputs):
    f = np.float32
    w_up = np.asarray(inputs["w_up"][0], f)
    w_up_r = np.empty((NPAIR, 128, 8, 256), f)
    wu4 = w_up.reshape(8, 128, 2 * DFF)
    for pr in range(NPAIR):
        w_up_r[pr, :, :, 0:128] = wu4[:, :, pr * 128:(pr + 1) * 128].transpose(1, 0, 2)
        w_up_r[pr, :, :, 128:256] = wu4[:, :, DFF + pr * 128:DFF + (pr + 1) * 128].transpose(1, 0, 2)
    gains = np.stack([np.broadcast_to(np.asarray(inputs[k][0], f), (128, D)) for k in
                      ("norm_mix_pre", "norm_mix_post", "norm_ffn_pre", "norm_ffn_post")]).astype(f)
    colp = np.concatenate([np.asarray(inputs["b_pool"][0], f).reshape(4, 128).T,
                           np.asarray(inputs["pool_scale"][0], f).reshape(4, 128).T], axis=1)
    cw = np.asarray(inputs["conv_w"][0], f)[:, 0, :]
    cb = np.asarray(inputs["conv_b"][0], f)
    cwb = np.concatenate([cw, cb[None]], axis=0).reshape(4, 44, 128).transpose(2, 1, 0)
    inv_freq = (10000.0 ** (-np.arange(0, DH, 2, dtype=f) / DH)).astype(f)
    ang = (np.arange(S, dtype=f)[:, None] * inv_freq[None, :]).astype(f)
    rope = np.concatenate([np.cos(ang), np.sin(ang)], axis=1).astype(f)
    rope = rope.reshape(NT, 128, 64).transpose(1, 0, 2)
    mk = np.zeros((3, 8, 8), f)
    for qb in range(8):
        for j in range(8):
            mk[0, qb, j] = 0.0 if j < qb else -1e30
            mk[1, qb, j] = -NEG if j < qb else 0.0
            mk[2, qb, j] = 0.0 if j == qb else NEG
    mk = np.broadcast_to(mk, (128, 3, 8, 8))
    blk = np.zeros((8, S), f)
    for j in range(8):
        blk[j, j * 256:(j + 1) * 256] = 1.0
    poolA = np.zeros((3, 4, 128, 128), f)
    tp = np.arange(128)[:, None]
    t = np.arange(128)[None, :]
    for g, w in enumerate((2, 4, 8, 16)):
        band = ((t - tp) >= 0) & ((t - tp) < w)
        poolA[0, g] = band / w - (t == tp)
        poolA[1, g] = ((t + 128 - tp) < w) / w
        cnt = np.minimum(t + 1, w)
        poolA[2, g] = band / cnt - (t == tp)
    poolA = poolA.transpose(2, 0, 1, 3)
    idtri = np.zeros((128, 2, 128), f)
    idtri[:, 0, :] = np.eye(128)
    idtri[:, 1, :] = np.where(tp > t, NEG, 0.0)
    c = lambda a: np.ascontiguousarray(a, dtype=f)
    shared = dict(
        w_in=c(inputs["w_in"][0]), w_out=c(inputs["w_out"][0]), w_up_r=c(w_up_r), w_down=c(inputs["w_down"][0]),
        w_pool=c(inputs["w_pool"][0]), gains=c(gains), colp=c(colp), cwb=c(cwb), rope=c(rope), mk=c(mk),
        blkind=c(blk), poolA=c(poolA), idtri=c(idtri))
    return shared


_NC_CACHE = {}


def kernel(**inputs):
    shared = host_prep(inputs)
    x = np.asarray(inputs["x"], np.float32)
    if "nc" not in _NC_CACHE:
        _NC_CACHE["nc"] = build()
    nc = _NC_CACHE["nc"]
    in_maps = [dict(shared, x=np.ascontiguousarray(x[b])) for b in range(8)]
    res = run_bass_kernel_spmd(nc, in_maps, core_ids=list(range(8)))
    return np.stack([r["out"] for r in res.results], axis=0).astype(np.float32)
```
